# Optimizing a Trainium2 kernel written in Bass

```python
import jax, jax.numpy as jnp
from jax import lax
import numpy as np

D_MODEL = 1024
BATCH = 2
SEQ = 8192
DEPTH = 4

GRID_W = 64
CTX_LEN = 256
N_MIXERS = 3
EPS = 1e-6
POOL_WINDOWS = (2, 4, 8, 16)
POOL_GROUPS = 4
POOL_GW = D_MODEL // POOL_GROUPS
HEAD_DIM = 128
N_HEADS = D_MODEL // HEAD_DIM
N_KV_HEADS = N_HEADS // 2
QKV_WIDTH = (N_HEADS + 2 * N_KV_HEADS) * HEAD_DIM
ROPE_BASE = 10000.0
Q_BLOCK = 128
CHUNK = 128
GMLP_HALF = 2 * D_MODEL
GMLP_GROUPS = 8
GMLP_GW = GMLP_HALF // GMLP_GROUPS
D_FF = 4 * D_MODEL

kernel_name = "hybrid_pool_gqa_gmlp_dit_block"


def n_layers_of(kind):
    return len(range(kind, DEPTH, N_MIXERS))


def rms_norm(x, g):
    xf = x.astype(jnp.float32)
    y = xf * lax.rsqrt(jnp.mean(xf * xf, axis=-1, keepdims=True) + EPS)
    return (y * g.astype(jnp.float32)).astype(x.dtype)


def layer_norm(x, g, b):
    xf = x.astype(jnp.float32)
    mu = jnp.mean(xf, axis=-1, keepdims=True)
    xc = xf - mu
    y = xc * lax.rsqrt(jnp.mean(xc * xc, axis=-1, keepdims=True) + EPS)
    return (y * g.astype(jnp.float32) + b.astype(jnp.float32)).astype(x.dtype)


def modulate(h, shift, scale):
    return h * (1 + scale[:, None, :]) + shift[:, None, :]


def pool_mix(h, w, scale):
    B, L, D = h.shape
    hf = h.astype(jnp.float32)
    cs = jnp.concatenate([jnp.zeros((B, 1, D), jnp.float32), jnp.cumsum(hf, axis=1)], axis=1)
    csg = cs.reshape(B, L + 1, POOL_GROUPS, POOL_GW)
    hg = hf.reshape(B, L, POOL_GROUPS, POOL_GW)
    pos = jnp.arange(L)
    outs = []
    for g, win in enumerate(POOL_WINDOWS):
        lo = jnp.clip(pos - win // 2, 0, L)
        hi = jnp.clip(pos + win - win // 2, 0, L)
        s = jnp.take(csg[:, :, g], hi, axis=1) - jnp.take(csg[:, :, g], lo, axis=1)
        cnt = (hi - lo).astype(jnp.float32)[None, :, None]
        outs.append(s / cnt - hg[:, :, g])
    p = jnp.stack(outs, axis=2).astype(h.dtype)
    y = jnp.einsum("blgc,gcd->blgd", p, w).reshape(B, L, D)
    return y * scale


def axial_rope_tables(L):
    rows_n = L // GRID_W
    row = jnp.repeat(jnp.arange(rows_n), GRID_W).astype(jnp.float32)
    col = jnp.tile(jnp.arange(GRID_W), rows_n).astype(jnp.float32)
    half = HEAD_DIM // 2
    inv = ROPE_BASE ** (-jnp.arange(0, half, 2, dtype=jnp.float32) / half)
    ang_r = row[:, None] * inv[None, :]
    ang_c = col[:, None] * inv[None, :]
    return jnp.cos(ang_r), jnp.sin(ang_r), jnp.cos(ang_c), jnp.sin(ang_c)


def rotate(x, cos, sin):
    x1, x2 = jnp.split(x, 2, axis=-1)
    cos = cos[None, :, None, :]
    sin = sin[None, :, None, :]
    return jnp.concatenate([x1 * cos - x2 * sin, x1 * sin + x2 * cos], axis=-1)


def apply_axial_rope(x, tables):
    cr, sr, cc, scol = tables
    xf = x.astype(jnp.float32)
    xr, xc = jnp.split(xf, 2, axis=-1)
    return jnp.concatenate([rotate(xr, cr, sr), rotate(xc, cc, scol)], axis=-1).astype(x.dtype)


def gqa_core(q, k, v):
    B, Lq = q.shape[0], q.shape[1]
    G = N_HEADS // N_KV_HEADS
    qg = q.reshape(B, Lq, N_KV_HEADS, G, HEAD_DIM)
    s = jnp.einsum("bqkgd,bskd->bkgqs", qg, k).astype(jnp.float32) * (HEAD_DIM ** -0.5)
    p = jax.nn.softmax(s, axis=-1).astype(v.dtype)
    o = jnp.einsum("bkgqs,bskd->bqkgd", p, v)
    return o.reshape(B, Lq, N_HEADS * HEAD_DIM)


def qkv_proj(h, w_qkv, q_g, k_g):
    B, L, _ = h.shape
    qkv = h @ w_qkv
    q, k, v = jnp.split(qkv, [N_HEADS * HEAD_DIM, (N_HEADS + N_KV_HEADS) * HEAD_DIM], axis=-1)
    q = rms_norm(q.reshape(B, L, N_HEADS, HEAD_DIM), q_g)
    k = rms_norm(k.reshape(B, L, N_KV_HEADS, HEAD_DIM), k_g)
    v = v.reshape(B, L, N_KV_HEADS, HEAD_DIM)
    return q, k, v


def attn_mix(h_ctx, h_lat, w_qkv, w_o, q_g, k_g, ctx_out):
    B, S, _ = h_lat.shape
    qc, kc, vc = qkv_proj(h_ctx, w_qkv, q_g, k_g)
    ql, kl, vl = qkv_proj(h_lat, w_qkv, q_g, k_g)
    tables = axial_rope_tables(S)
    ql = apply_axial_rope(ql, tables)
    kl = apply_axial_rope(kl, tables)
    k_all = jnp.concatenate([kc, kl], axis=1)
    v_all = jnp.concatenate([vc, vl], axis=1)
    nb = S // Q_BLOCK
    qb = ql.reshape(B, nb, Q_BLOCK, N_HEADS, HEAD_DIM).transpose(1, 0, 2, 3, 4)
    ob = lax.map(lambda qq: gqa_core(qq, k_all, v_all), qb)
    y_lat = ob.transpose(1, 0, 2, 3).reshape(B, S, N_HEADS * HEAD_DIM) @ w_o
    y_ctx = gqa_core(qc, kc, vc) @ w_o if ctx_out else None
    return y_ctx, y_lat


def gmlp_mix(h, w_in, ln_g, ln_b, ws, bs, w_out):
    B, L, _ = h.shape
    z = jax.nn.gelu(h @ w_in)
    u, v = jnp.split(z, 2, axis=-1)
    v = layer_norm(v, ln_g, ln_b)
    vg = v.reshape(B, L // CHUNK, CHUNK, GMLP_GROUPS, GMLP_GW)
    sv = jnp.einsum("gqp,bnpgc->bnqgc", ws, vg) + bs.T[None, None, :, :, None]
    return (u * sv.reshape(B, L, GMLP_HALF)) @ w_out


def sq_relu_mlp(h, w1, w2):
    return jnp.square(jax.nn.relu(h @ w1)) @ w2


def setup_inputs(seed: int = 0) -> dict:
    key = jax.random.key(seed)
    ks = jax.random.split(key, 24)
    f32 = jnp.float32
    D = D_MODEL
    nP, nA, nG = n_layers_of(0), n_layers_of(1), n_layers_of(2)

    def nrm(k, shape, s):
        return jax.random.normal(k, shape, f32) * s

    return {
        "x": nrm(ks[0], (BATCH, SEQ, D), 1.0),
        "c": nrm(ks[1], (BATCH, D), 1.0),
        "ctx": nrm(ks[2], (BATCH, CTX_LEN, D), 1.0),
        "c_ctx": nrm(ks[3], (D,), 1.0),
        "ada_w": nrm(ks[4], (DEPTH, D, 6 * D), 0.5 * D ** -0.5),
        "ada_b": nrm(ks[5], (DEPTH, 6 * D), 0.02),
        "norm_g": 1.0 + nrm(ks[6], (DEPTH, 2, D), 0.02),
        "mlp_w1": nrm(ks[7], (DEPTH, D, D_FF), D ** -0.5),
        "mlp_w2": nrm(ks[8], (DEPTH, D_FF, D), D_FF ** -0.5),
        "pool_w": nrm(ks[9], (nP, POOL_GROUPS, POOL_GW, POOL_GW), POOL_GW ** -0.5),
        "pool_scale": 1.0 + nrm(ks[10], (nP, D), 0.02),
        "attn_w_qkv": nrm(ks[11], (nA, D, QKV_WIDTH), D ** -0.5),
        "attn_w_o": nrm(ks[12], (nA, N_HEADS * HEAD_DIM, D), (N_HEADS * HEAD_DIM) ** -0.5),
        "attn_q_g": 1.0 + nrm(ks[13], (nA, HEAD_DIM), 0.02),
        "attn_k_g": 1.0 + nrm(ks[14], (nA, HEAD_DIM), 0.02),
        "gm_w_in": nrm(ks[15], (nG, D, 2 * GMLP_HALF), D ** -0.5),
        "gm_ln_g": 1.0 + nrm(ks[16], (nG, GMLP_HALF), 0.02),
        "gm_ln_b": nrm(ks[17], (nG, GMLP_HALF), 0.02),
        "gm_ws": nrm(ks[18], (nG, GMLP_GROUPS, CHUNK, CHUNK), CHUNK ** -0.5),
        "gm_bs": nrm(ks[19], (nG, GMLP_GROUPS, CHUNK), 0.02),
        "gm_w_out": nrm(ks[20], (nG, GMLP_HALF, D), GMLP_HALF ** -0.5),
        "final_g": 1.0 + nrm(ks[21], (D,), 0.02),
    }


def reference(x, c, ctx, c_ctx, ada_w, ada_b, norm_g, mlp_w1, mlp_w2, pool_w, pool_scale,
              attn_w_qkv, attn_w_o, attn_q_g, attn_k_g, gm_w_in, gm_ln_g, gm_ln_b, gm_ws, gm_bs,
              gm_w_out, final_g):
    last_ctx_read = max([i for i in range(DEPTH) if i % N_MIXERS == 1], default=-1)
    s_lat = jax.nn.silu(c)
    s_ctx = jax.nn.silu(c_ctx)[None, :]
    h_lat, h_ctx = x, ctx
    for i in range(DEPTH):
        kind, j = i % N_MIXERS, i // N_MIXERS
        ctx_in = i <= last_ctx_read
        ctx_out = i < last_ctx_read
        sh1, sc1, g1, sh2, sc2, g2 = jnp.split(s_lat @ ada_w[i] + ada_b[i], 6, axis=-1)
        a_l = modulate(rms_norm(h_lat, norm_g[i, 0]), sh1, sc1)
        if ctx_in:
            csh1, csc1, cg1, csh2, csc2, cg2 = jnp.split(s_ctx @ ada_w[i] + ada_b[i], 6, axis=-1)
            a_c = modulate(rms_norm(h_ctx, norm_g[i, 0]), csh1, csc1)
        y_c = None
        if kind == 0:
            y_l = pool_mix(a_l, pool_w[j], pool_scale[j])
            if ctx_out:
                y_c = pool_mix(a_c, pool_w[j], pool_scale[j])
        elif kind == 1:
            y_c, y_l = attn_mix(a_c, a_l, attn_w_qkv[j], attn_w_o[j], attn_q_g[j], attn_k_g[j], ctx_out)
        else:
            y_l = gmlp_mix(a_l, gm_w_in[j], gm_ln_g[j], gm_ln_b[j], gm_ws[j], gm_bs[j], gm_w_out[j])
            if ctx_out:
                y_c = gmlp_mix(a_c, gm_w_in[j], gm_ln_g[j], gm_ln_b[j], gm_ws[j], gm_bs[j], gm_w_out[j])
        h_lat = h_lat + g1[:, None, :] * y_l
        m_l = modulate(rms_norm(h_lat, norm_g[i, 1]), sh2, sc2)
        h_lat = h_lat + g2[:, None, :] * sq_relu_mlp(m_l, mlp_w1[i], mlp_w2[i])
        if ctx_out:
            h_ctx = h_ctx + cg1[:, None, :] * y_c
            m_c = modulate(rms_norm(h_ctx, norm_g[i, 1]), csh2, csc2)
            h_ctx = h_ctx + cg2[:, None, :] * sq_relu_mlp(m_c, mlp_w1[i], mlp_w2[i])
    return rms_norm(h_lat, final_g)
```

```python
import numpy as np
import concourse.bass as bass
import concourse.mybir as mybir
from concourse.bass_utils import run_bass_kernel_spmd

F32 = mybir.dt.float32
BF16 = mybir.dt.bfloat16
AF = mybir.ActivationFunctionType
ALU = mybir.AluOpType
AX = mybir.AxisListType


class Buf:
    __slots__ = ("name", "last_write", "readers")

    def __init__(self, name=""):
        self.name = name
        self.last_write = None
        self.readers = []


class _Ins:
    __slots__ = ("eng", "fn", "deps", "idx", "signal", "dma_tok", "dma_prev")

    def __init__(self, eng, fn, deps, idx):
        self.eng = eng
        self.fn = fn
        self.deps = deps
        self.idx = idx
        self.signal = False
        self.dma_tok = None
        self.dma_prev = None


ENGS = ("pe", "act", "dve", "pool", "sp")
SEM_LIMIT = 12000
N_DMA_SEMS = 12


class Prog:
    def __init__(self, nc):
        self.nc = nc
        self.streams = {e: [] for e in ENGS}
        self.dma_count = {e: 0 for e in ENGS}

    def _collect(self, reads, writes):
        deps = []
        for b in reads:
            if b.last_write is not None:
                deps.append(b.last_write)
        for b in writes:
            if b.last_write is not None:
                deps.append(b.last_write)
            deps.extend(b.readers)
        return deps

    def _commit(self, tok, reads, writes):
        for b in reads:
            b.readers.append(tok)
        for b in writes:
            b.last_write = tok
            b.readers = []

    def op(self, eng, fn, reads=(), writes=()):
        deps = self._collect(reads, writes)
        st = self.streams[eng]
        ins = _Ins(eng, fn, deps, len(st))
        st.append(ins)
        self._commit(("c", eng, ins.idx), reads, writes)
        return ins

    def coll(self, kind, groups, src, dst, reads=(), writes=()):
        return self.dma("pool", None, None, reads, writes,
                        fn=lambda e: e.collective_compute(kind, ALU.bypass, replica_groups=groups,
                                                          ins=[src], outs=[dst]))

    def dma(self, eng, out, in_, reads=(), writes=(), fn=None):
        deps = self._collect(reads, writes)
        st = self.streams[eng]
        k = self.dma_count[eng]
        self.dma_count[eng] = k + 1
        slot, rnd = k % N_DMA_SEMS, k // N_DMA_SEMS
        if fn is None:
            fn = lambda e: e.dma_start(out=out, in_=in_)
        ins = _Ins(eng, fn, deps, len(st))
        ins.dma_tok = ("d", eng, slot, 16 * (rnd + 1))
        if rnd > 0:
            ins.dma_prev = ("d", eng, slot, 16 * rnd)
        st.append(ins)
        self._commit(ins.dma_tok, reads, writes)
        return ins

    def emit(self, final_wait_bufs=()):
        nc = self.nc
        final_deps = []
        for b in final_wait_bufs:
            if b.last_write is not None:
                final_deps.append(b.last_write)
        for e in ENGS:
            for ins in self.streams[e]:
                for d in ins.deps:
                    if d[0] == "c":
                        self.streams[d[1]][d[2]].signal = True
        for d in final_deps:
            if d[0] == "c":
                self.streams[d[1]][d[2]].signal = True
        count_of = {}
        n_epochs = {}
        for e in ENGS:
            c = 0
            for ins in self.streams[e]:
                if ins.signal:
                    c += 1
                    count_of[(e, ins.idx)] = c
            n_epochs[e] = max(1, (c + SEM_LIMIT - 1) // SEM_LIMIT)
        import contextlib
        with contextlib.ExitStack() as es:
            csem = {e: [es.enter_context(nc.semaphore(f"c_{e}_{i}")) for i in range(n_epochs[e])]
                    for e in ENGS}
            dsem = {e: [es.enter_context(nc.semaphore(f"d_{e}_{i}")) for i in range(N_DMA_SEMS)]
                    for e in ENGS if self.dma_count[e] > 0}
            block = es.enter_context(nc.Block())

            def resolve(tok):
                if tok[0] == "c":
                    c = count_of[(tok[1], tok[2])]
                    ep = (c - 1) // SEM_LIMIT
                    return (csem[tok[1]][ep], ("c", tok[1], ep), c - ep * SEM_LIMIT)
                return (dsem[tok[1]][tok[2]], ("d", tok[1], tok[2]), tok[3])

            def run(e, eng):
                known = {}
                for ins in self.streams[e]:
                    toks = list(ins.deps)
                    if ins.dma_prev is not None:
                        toks.append(ins.dma_prev)
                    need = {}
                    for t in toks:
                        sem, key, val = resolve(t)
                        if known.get(key, 0) >= val:
                            continue
                        if key not in need or need[key][1] < val:
                            need[key] = (sem, val)
                    for key, (sem, val) in need.items():
                        eng.wait_ge(sem, val)
                        known[key] = val
                    bi = ins.fn(eng)
                    if ins.dma_tok is not None:
                        bi.then_inc(dsem[e][ins.dma_tok[2]], 16)
                    elif ins.signal:
                        c = count_of[(e, ins.idx)]
                        ep = (c - 1) // SEM_LIMIT
                        bi.then_inc(csem[e][ep], 1)
                if e == "sp":
                    for t in final_deps:
                        sem, key, val = resolve(t)
                        if known.get(key, 0) >= val:
                            continue
                        eng.wait_ge(sem, val)
                        known[key] = max(known.get(key, 0), val)

            @block.tensor
            def _(eng):
                run("pe", eng)

            @block.scalar
            def _(eng):
                run("act", eng)

            @block.vector
            def _(eng):
                run("dve", eng)

            @block.gpsimd
            def _(eng):
                run("pool", eng)

            @block.sync
            def _(eng):
                run("sp", eng)


T = 2048
HAL = 8
TC = T + 2 * HAL
NCTX = 256
SLOT = 4288
NSLOT = 8
RING_SLOTS = 6
EPS = 1e-6
SM_SCALE = 128 ** -0.5
SM_SHIFT = -(128 ** 0.5)


def _mm(out, lhsT, rhs, start, stop):
    return lambda e: e.matmul(out, lhsT=lhsT, rhs=rhs, start=start, stop=stop)


def _act(out, in_, func, **kw):
    return lambda e: e.activation(out=out, in_=in_, func=func, **kw)


def _tt(out, in0, in1, op):
    return lambda e: e.tensor_tensor(out=out, in0=in0, in1=in1, op=op)


def _stt(out, in0, scalar, in1, op0, op1):
    return lambda e: e.scalar_tensor_tensor(out=out, in0=in0, scalar=scalar, in1=in1, op0=op0, op1=op1)


def _ts(out, in0, s1, s2, op0, op1=None):
    if op1 is None:
        return lambda e: e.tensor_scalar(out=out, in0=in0, scalar1=s1, scalar2=None, op0=op0)
    return lambda e: e.tensor_scalar(out=out, in0=in0, scalar1=s1, scalar2=s2, op0=op0, op1=op1)


class Blk:
    pass


class MK:
    def __init__(self, seg):
        import contextlib
        self.seg = seg
        self.nc = nc = bass.Bass("TRN2", target_bir_lowering=False)
        self.es = es = contextlib.ExitStack()
        self.p = Prog(nc)
        self.din = {}
        self.dout = {}

        def sb(name, shape, dt):
            return es.enter_context(nc.sbuf_tensor(name, shape, dt))

        self.HT = sb("HT", [128, 8 * TC], F32)
        self.HT3 = self.HT[:, :].rearrange("p (c t) -> p c t", c=8)
        self.ACTA = sb("ACTA", [128, 8 * T], BF16)
        self.A3 = self.ACTA[:, :].rearrange("p (c t) -> p c t", c=8)
        self.ACTB = sb("ACTB", [128, 8192], F32)
        self.BIG = sb("BIG", [128, NSLOT * SLOT], BF16)
        self.MOD = sb("MOD", [128, 4 * 96], F32)
        self.ADAB = sb("ADAB", [128, 96], F32)
        self.NG2 = sb("NG2", [128, 128], F32)
        self.PS2 = sb("PS2", [128, 32], F32)
        self.FG = sb("FG", [128, 8], F32)
        self.CV = sb("CV", [128, 16], F32)
        self.SBF = sb("SBF", [128, 16], BF16)
        self.ONES = sb("ONES", [128, 128], BF16)
        self.IDENT = sb("IDENT", [128, 128], BF16)
        self.ROT = sb("ROT", [128, 128], F32)
        self.EPSC = sb("EPSC", [128, 2], F32)
        self.DER = sb("DER", [128, 4 * 48], F32)
        self.HM = sb("HM", [128, 32], F32)
        self.EDGE = sb("EDGE", [128, 128], F32)
        self.QKG = sb("QKG", [128, 2], F32)
        self.SCR = sb("SCR", [128, 1024], F32)
        self.SMALL = sb("SMALL", [128, 64], F32)
        self.PS = es.enter_context(nc.psum_tensor("PS", [128, 8 * 512], F32))

        self.HTB = [[Buf(f"h{c}_{t}") for t in range(4)] for c in range(8)]
        self.HALB = [Buf(f"hal{c}") for c in range(8)]
        self.AB = [[Buf(f"a{c}_{t}") for t in range(4)] for c in range(8)]
        self.ACTBB = [Buf(f"actb{k}") for k in range(16)]
        self.SLOTB = [Buf(f"slot{k}") for k in range(NSLOT)]
        self.PSB = [Buf(f"ps{k}") for k in range(8)]
        self.SCRB = [Buf(f"scr{k}") for k in range(4)]
        self.MODB = [Buf(f"mod{i}") for i in range(4)]
        self.DERB = [Buf(f"der{i}") for i in range(4)]
        self.CONSTB = Buf("const")
        self.SMALLB = Buf("small")
        self.ADABB = Buf("adab")
        self.ring_pos = 0
        self.rot = {}
        self.outbufs = []

    def inp(self, name, shape, dt=F32):
        t = self.nc.dram_tensor(name, list(shape), dt, kind="ExternalInput").ap()
        self.din[name] = t
        return t

    def outp(self, name, shape, dt=F32):
        t = self.nc.dram_tensor(name, list(shape), dt, kind="ExternalOutput").ap()
        self.dout[name] = t
        return t

    def bank(self, role, banks):
        k = self.rot.get(role, 0)
        self.rot[role] = k + 1
        b = banks[k % len(banks)]
        return b, self.PS[:, b * 512:(b + 1) * 512], self.PSB[b]

    def scr_bf(self, r, n=512):
        return self.SCR[:, r * 256:(r + 1) * 256].bitcast(BF16)[:, 0:n]

    def scr_f32(self, r2, n=512):
        return self.SCR[:, r2 * 512:r2 * 512 + n]

    def actb_f32(self, off, n):
        return self.ACTB[:, off:off + n]

    def actb_bufs(self, off, n):
        return self.ACTBB[off // 512:(off + n + 511) // 512]

    def ring_load(self, dram_ap, a, b, eng="pool"):
        elems = a * b
        n = (elems + SLOT - 1) // SLOT
        if self.ring_pos + n > RING_SLOTS:
            self.ring_pos = 0
        pos = self.ring_pos
        self.ring_pos = (pos + n) % RING_SLOTS
        view = self.BIG[:, pos * SLOT:pos * SLOT + elems].rearrange("p (a b) -> p a b", a=a)
        bufs = self.SLOTB[pos:pos + n]
        self.p.dma(eng, view, dram_ap, writes=bufs)
        return view, bufs

    def out_dma(self, dst, src, reads):
        b = Buf("out")
        self.outbufs.append(b)
        self.p.dma("sp", dst, src, reads=reads, writes=[b])

    def col(self, t, idx):
        return t[:, idx:idx + 1]

    def modcol(self, i, part, ch, s):
        return self.col(self.MOD, i * 96 + (part * 8 + ch) * 2 + s)

    def dercol(self, i, kind, ch, s):
        return self.col(self.DER, i * 48 + (kind * 8 + ch) * 2 + s)

    def setup(self):
        p = self.p
        c = self.CONSTB
        p.op("dve", lambda e: e.memset(self.ONES[:], 1.0), writes=[c])
        p.op("dve", lambda e: e.memset(self.EPSC[:, 0:1], EPS), writes=[c])
        p.op("dve", lambda e: e.memset(self.EPSC[:, 1:2], SM_SHIFT), writes=[c])
        small = [("ng2", self.NG2, 128), ("ps2", self.PS2, 32), ("fg", self.FG, 8), ("cvec", self.CV, 16),
                 ("rot", self.ROT, 128), ("hm", self.HM, 32), ("edge", self.EDGE, 128), ("qkg", self.QKG, 2)]
        for name, t, n in small:
            d = self.inp(name, [128, n])
            p.dma("sp", t[:], d, writes=[c])
        d = self.inp("ident", [128, 128])
        p.dma("pool", self.IDENT[:], d, writes=[c])
        p.op("act", _act(self.SBF[:], self.CV[:], AF.Silu), reads=[c], writes=[c])
        self.ada_w = self.inp("ada_w", [4, 1024, 6144])
        self.adab_d = self.inp("adab", [128, 4 * 96])
        self.w1 = self.inp("mlp_w1", [4, 1024, 4096])
        self.w2 = self.inp("mlp_w2", [4, 4096, 1024])

    def load_h(self):
        hin = self.inp("hin", [1024, TC])
        v = hin.rearrange("(c p) t -> p c t", p=128)
        for ch in range(8):
            self.p.dma("sp", self.HT3[:, ch, :], v[:, ch, :], writes=self.HTB[ch] + [self.HALB[ch]])

    def store_h(self):
        hout = self.outp("hout", [1024, T])
        v = hout.rearrange("(c p) t -> p c t", p=128)
        for ch in range(8):
            self.out_dma(v[:, ch, :], self.HT3[:, ch, HAL:HAL + T], self.HTB[ch])

    def lat_blocks(self):
        blks = []
        for tb in range(4):
            b = Blk()
            b.n, b.s, b.tb = 512, 0, tb
            b.h = (lambda ch, tb=tb: self.HT3[:, ch, HAL + tb * 512:HAL + (tb + 1) * 512])
            b.hB = [self.HTB[ch][tb] for ch in range(8)]
            b.a = (lambda ch, tb=tb: self.A3[:, ch, tb * 512:(tb + 1) * 512])
            b.aB = [self.AB[ch][tb] for ch in range(8)]
            hid3 = self.ACTB[:, 0:4096].bitcast(BF16).rearrange("p (c t) -> p c t", c=4)
            b.hid = (lambda fc, tb=tb, hid3=hid3: hid3[:, fc, tb * 512:(tb + 1) * 512])
            b.hidB = [self.ACTBB[fc * 2 + tb // 2] for fc in range(4)]
            blks.append(b)
        return blks

    def ctx_block(self):
        b = Blk()
        b.n, b.s, b.tb = NCTX, 1, 0
        base = 6 * SLOT
        ch3 = self.BIG[:, base:base + 4096].bitcast(F32).rearrange("p (c t) -> p c t", c=8)
        ca3 = self.BIG[:, base + 4096:base + 6144].rearrange("p (c t) -> p c t", c=8)
        chid = self.BIG[:, base + 6144:base + 7168].rearrange("p (c t) -> p c t", c=4)
        self.CH3 = ch3
        b.h = lambda ch: ch3[:, ch, :]
        self.CTXHB = [Buf(f"ctxh{c}") for c in range(8)]
        b.hB = self.CTXHB
        b.a = lambda ch: ca3[:, ch, :]
        b.aB = [Buf(f"ctxa{c}") for c in range(8)]
        b.hid = lambda fc: chid[:, fc, :]
        b.hidB = [Buf(f"ctxhid{c}") for c in range(4)]
        return b

    def ada_piece(self, i, k):
        p = self.p
        v = self.ada_w[i, :, :].rearrange("(kc q) f -> q kc f", q=128)[:, :, k * 512:(k + 1) * 512]
        w, wb = self.ring_load(v, 8, 512)
        s3 = self.SBF[:, :].rearrange("p (kc s) -> p kc s", s=2)
        for j in range(4):
            idx = k * 4 + j
            for kc in range(8):
                p.op("pe", _mm(self.PS[:, 7 * 512 + 2 * idx:7 * 512 + 2 * idx + 2], w[:, kc, j * 128:(j + 1) * 128],
                               s3[:, kc, :], kc == 0, kc == 7), reads=wb + [self.CONSTB], writes=[self.PSB[7]])

    def ada_finish(self, i):
        p = self.p
        p.dma("sp", self.ADAB[:], self.adab_d[:, i * 96:(i + 1) * 96], writes=[self.ADABB])
        p.op("dve", _tt(self.MOD[:, i * 96:(i + 1) * 96], self.PS[:, 7 * 512:7 * 512 + 96], self.ADAB[:], ALU.add),
             reads=[self.PSB[7], self.ADABB], writes=[self.MODB[i]])
        for kind, part, which in ((0, 1, 0), (1, 4, 1)):
            src = self.MOD[:, i * 96 + part * 16:i * 96 + part * 16 + 16]
            dst = self.DER[:, i * 48 + kind * 16:i * 48 + kind * 16 + 16]
            ng = self.NG2[:, (i * 2 + which) * 16:(i * 2 + which) * 16 + 16]
            p.op("dve", _stt(dst, src, 1.0, ng, ALU.add, ALU.mult), reads=[self.MODB[i], self.CONSTB],
                 writes=[self.DERB[i]])
        if i % 3 == 0:
            j = i // 3
            src = self.MOD[:, i * 96 + 2 * 16:i * 96 + 3 * 16]
            dst = self.DER[:, i * 48 + 32:i * 48 + 48]
            p.op("dve", _tt(dst, src, self.PS2[:, j * 16:(j + 1) * 16], ALU.mult), reads=[self.MODB[i], self.CONSTB],
                 writes=[self.DERB[i]])

    def ada_all(self, i):
        for k in range(12):
            self.ada_piece(i, k)
        self.ada_finish(i)

    def rstd_block(self, hfn, hbufs, n, dst=None, dstB=None):
        p = self.p
        _, ss, ssB = self.bank("ss", [0, 1])
        for ch in range(8):
            r = ch % 2
            sq = self.scr_bf(r, n)
            hb = hbufs[ch] if isinstance(hbufs[ch], list) else [hbufs[ch]]
            p.op("act", _act(sq, hfn(ch), AF.Square), reads=hb, writes=[self.SCRB[r]])
            p.op("pe", _mm(ss[:, :n], self.ONES[:], sq, ch == 0, ch == 7), reads=[self.SCRB[r], self.CONSTB],
                 writes=[ssB])
        if dst is None:
            _, rs, rsB = self.bank("rs", [2, 3])
            dst, dstB = rs[:, :n], [rsB]
        p.op("act", _act(dst, ss[:, :n], AF.Sqrt, scale=1.0 / 1024, bias=self.EPSC[:, 0:1]),
             reads=[ssB, self.CONSTB], writes=dstB)
        p.op("dve", lambda e, dst=dst: e.reciprocal(out=dst, in_=dst), reads=dstB, writes=dstB)
        return dst, dstB

    def norm_mod(self, blks, i, which):
        p = self.p
        part_sh = 0 if which == 0 else 3
        for b in blks:
            rs, rsB = self.rstd_block(b.h, b.hB, b.n)
            for ch in range(8):
                _, tmp, tmpB = self.bank("tmp", [4, 5])
                p.op("dve", _tt(tmp[:, :b.n], b.h(ch), rs, ALU.mult), reads=[b.hB[ch]] + rsB, writes=[tmpB])
                p.op("act", _act(b.a(ch), tmp[:, :b.n], AF.Identity, scale=self.dercol(i, which, ch, b.s),
                                 bias=self.modcol(i, part_sh, ch, b.s)),
                     reads=[tmpB, self.DERB[i], self.MODB[i]], writes=[b.aB[ch]])

    def mlp(self, blks, i, ada_next=None):
        p = self.p
        self.norm_mod(blks, i, 1)
        ada_k = 0
        for fb in range(8):
            v1 = self.w1[i, :, :].rearrange("(kc q) f -> q kc f", q=128)[:, :, fb * 512:(fb + 1) * 512]
            w1p, w1b = self.ring_load(v1, 8, 512)
            v2 = self.w2[i, fb * 512:(fb + 1) * 512, :].rearrange("(fc q) d -> q fc d", q=128)
            w2p, w2b = self.ring_load(v2, 4, 1024)
            for b in blks:
                for fc in range(4):
                    _, hp, hpB = self.bank("hp", [0, 1, 2])
                    for kc in range(8):
                        p.op("pe", _mm(hp[:, :b.n], w1p[:, kc, fc * 128:(fc + 1) * 128], b.a(kc), kc == 0, kc == 7),
                             reads=w1b + [b.aB[kc]], writes=[hpB])
                    r2 = self.rot.get("relu", 0) % 2
                    self.rot["relu"] = r2 + 1
                    rr = self.scr_f32(r2, b.n)
                    rrB = self.SCRB[2 * r2:2 * r2 + 2]
                    p.op("act", _act(rr, hp[:, :b.n], AF.Relu), reads=[hpB], writes=rrB)
                    p.op("dve", _tt(b.hid(fc), hp[:, :b.n], rr, ALU.mult), reads=[hpB] + rrB, writes=[b.hidB[fc]])
            for b in blks:
                for dc in range(8):
                    _, yp, ypB = self.bank("yp", [3, 4, 5, 6])
                    for fc in range(4):
                        p.op("pe", _mm(yp[:, :b.n], w2p[:, fc, dc * 128:(dc + 1) * 128], b.hid(fc), fc == 0, fc == 3),
                             reads=w2b + [b.hidB[fc]], writes=[ypB])
                    p.op("dve", _stt(b.h(dc), yp[:, :b.n], self.modcol(i, 5, dc, b.s), b.h(dc), ALU.mult, ALU.add),
                         reads=[ypB, self.MODB[i], b.hB[dc]], writes=[b.hB[dc]])
            if ada_next is not None:
                for _ in range(2):
                    if ada_k < 12:
                        self.ada_piece(ada_next, ada_k)
                        ada_k += 1
        if ada_next is not None:
            self.ada_finish(ada_next)

    def pool_mix(self, i, j, segs):
        p = self.p
        pw = self.inp_once("pool_w", [2, 4, 256, 256])
        for sg in segs:
            n, s = sg["n"], sg["s"]
            ncol = n + 2 * HAL
            A = self.actb_f32(0, ncol)
            AB_ = self.actb_bufs(0, ncol)
            SA = self.actb_f32(2560, ncol)
            SAB = self.actb_bufs(2560, ncol)
            RS = self.actb_f32(5120, ncol)
            RSB = self.actb_bufs(5120, ncol)
            c0 = 0
            for (o, m) in sg["stat_cols"]:
                self.rstd_block(lambda ch, o=o, m=m: sg["hcols"](ch, o, m), sg["hbufs_all"], m,
                                dst=RS[:, sg["rs_off"] + o:sg["rs_off"] + o + m], dstB=RSB)
            for ch in range(8):
                g = ch // 2
                w = 2 << g
                hb = sg["hbufs_ch"](ch)
                if sg["halo"]:
                    p.op("dve", _tt(A[:, 0:ncol], sg["hcols"](ch, 0, ncol), RS[:, 0:ncol], ALU.mult),
                         reads=hb + RSB, writes=AB_)
                    p.op("act", _act(A[:, 0:ncol], A[:, 0:ncol], AF.Identity, scale=self.dercol(i, 0, ch, s),
                                     bias=self.modcol(i, 0, ch, s)), reads=AB_ + [self.DERB[i], self.MODB[i]],
                         writes=AB_)
                    p.op("dve", _tt(A[:, 0:HAL], A[:, 0:HAL], self.HM[:, 0:HAL], ALU.mult), reads=AB_ + [self.CONSTB],
                         writes=AB_)
                    p.op("dve", _tt(A[:, HAL + n:ncol], A[:, HAL + n:ncol], self.HM[:, HAL:2 * HAL], ALU.mult),
                         reads=AB_ + [self.CONSTB], writes=AB_)
                else:
                    p.op("dve", lambda e, A=A, ncol=ncol: e.memset(A[:, 0:ncol], 0.0), writes=AB_)
                    p.op("dve", _tt(A[:, HAL:HAL + n], sg["hcols"](ch, 0, n), RS[:, HAL:HAL + n], ALU.mult),
                         reads=hb + RSB, writes=AB_)
                    p.op("act", _act(A[:, HAL:HAL + n], A[:, HAL:HAL + n], AF.Identity,
                                     scale=self.dercol(i, 0, ch, s), bias=self.modcol(i, 0, ch, s)),
                         reads=AB_ + [self.DERB[i], self.MODB[i]], writes=AB_)
                m = ncol - 1
                p.op("dve", _tt(SA[:, 0:m], A[:, 0:m], A[:, 1:m + 1], ALU.add), reads=AB_, writes=SAB)
                sh = 2
                while sh < w:
                    m2 = m - sh
                    p.op("dve", _tt(SA[:, 0:m2], SA[:, 0:m2], SA[:, sh:sh + m2], ALU.add), reads=SAB, writes=SAB)
                    m = m2
                    sh *= 2
                o = HAL - w // 2
                for b in sg["blks"]:
                    t0 = b.tb * 512
                    p.op("dve", _stt(b.a(ch), SA[:, o + t0:o + t0 + b.n], 1.0 / w, A[:, HAL + t0:HAL + t0 + b.n],
                                     ALU.mult, ALU.subtract), reads=SAB + AB_, writes=[b.aB[ch]])
                eo = sg["edge_off"] + g * 16
                tmpe = self.SMALL[:, 0:16]
                for (dst_t, ecol) in ((0, 0), (n - HAL, HAL)):
                    bfix = sg["blks"][0] if dst_t == 0 else sg["blks"][-1]
                    lt = dst_t - bfix.tb * 512
                    p.op("dve", _tt(tmpe[:, ecol:ecol + HAL], SA[:, o + dst_t:o + dst_t + HAL],
                                    self.EDGE[:, eo + ecol:eo + ecol + HAL], ALU.mult), reads=SAB + [self.CONSTB],
                         writes=[self.SMALLB])
                    p.op("dve", _tt(bfix.a(ch)[:, lt:lt + HAL], tmpe[:, ecol:ecol + HAL],
                                    A[:, HAL + dst_t:HAL + dst_t + HAL], ALU.subtract),
                         reads=[self.SMALLB] + AB_, writes=[bfix.aB[ch]])
        v = pw[j, :, :, :].rearrange("g (kc q) d -> q (g kc) d", q=128)
        wp, wpb = self.ring_load(v, 8, 256)
        for sg in segs:
            for b in sg["blks"]:
                for dc in range(8):
                    g, dh = dc // 2, dc % 2
                    _, yp, ypB = self.bank("yp", [3, 4, 5, 6])
                    for kc in range(2):
                        p.op("pe", _mm(yp[:, :b.n], wp[:, g * 2 + kc, dh * 128:(dh + 1) * 128], b.a(g * 2 + kc),
                                       kc == 0, kc == 1), reads=wpb + [b.aB[g * 2 + kc]], writes=[ypB])
                    p.op("dve", _stt(b.h(dc), yp[:, :b.n], self.dercol(i, 2, dc, b.s), b.h(dc), ALU.mult, ALU.add),
                         reads=[ypB, self.DERB[i], b.hB[dc]], writes=[b.hB[dc]])

    def inp_once(self, name, shape, dt=F32):
        if name in self.din:
            return self.din[name]
        return self.inp(name, shape, dt)

    def lat_seg(self, blks):
        return dict(n=T, s=0, halo=True, blks=blks, edge_off=0, rs_off=0,
                    stat_cols=[(0, 512), (512, 512), (1024, 512), (1536, 512), (2048, 16)],
                    hcols=lambda ch, o, m: self.HT3[:, ch, o:o + m],
                    hbufs_all=[self.HTB[ch] + [self.HALB[ch]] for ch in range(8)],
                    hbufs_ch=lambda ch: self.HTB[ch] + [self.HALB[ch]])

    def ctx_seg(self, cb):
        return dict(n=NCTX, s=1, halo=False, blks=[cb], edge_off=64, rs_off=HAL,
                    stat_cols=[(0, 256)],
                    hcols=lambda ch, o, m: self.CH3[:, ch, o:o + m],
                    hbufs_all=[[self.CTXHB[ch]] for ch in range(8)],
                    hbufs_ch=lambda ch: [self.CTXHB[ch]])

    def ring_load_f32(self, dram_ap, ncols, eng="sp"):
        elems = 2 * ncols
        n = (elems + SLOT - 1) // SLOT
        if self.ring_pos + n > RING_SLOTS:
            self.ring_pos = 0
        pos = self.ring_pos
        self.ring_pos = (pos + n) % RING_SLOTS
        view = self.BIG[:, pos * SLOT:pos * SLOT + elems].bitcast(F32)
        bufs = self.SLOTB[pos:pos + n]
        self.p.dma(eng, view, dram_ap, writes=bufs)
        return view, bufs

    def qk_prep(self, ps, psB, n, gcol, rope, t0, out_ap, outB):
        p = self.p
        k = self.rot.get("qkset", 0)
        self.rot["qkset"] = k + 1
        base = (k % 2) * 2560
        r = k % 2
        sq = self.scr_bf(r, n)
        p.op("act", _act(sq, ps, AF.Square), reads=[psB], writes=[self.SCRB[r]])
        _, ss, ssB = self.bank("ss2", [2, 3])
        p.op("pe", _mm(ss[:, :n], self.ONES[:], sq, True, True), reads=[self.SCRB[r], self.CONSTB], writes=[ssB])
        rs = self.actb_f32(base, n)
        rsB = self.actb_bufs(base, n)
        p.op("act", _act(rs, ss[:, :n], AF.Sqrt, scale=1.0 / 128, bias=self.EPSC[:, 0:1]),
             reads=[ssB, self.CONSTB], writes=rsB)
        p.op("dve", lambda e: e.reciprocal(out=rs, in_=rs), reads=rsB, writes=rsB)
        xg = self.actb_f32(base + 512, n)
        xgB = self.actb_bufs(base + 512, n)
        p.op("dve", _stt(xg, ps, gcol, rs, ALU.mult, ALU.mult), reads=[psB, self.CONSTB] + rsB, writes=xgB)
        if rope:
            _, rp, rpB = self.bank("rot", [4, 5])
            p.op("pe", _mm(rp[:, :n], self.ROT[:], xg, True, True), reads=xgB + [self.CONSTB], writes=[rpB])
            t1 = self.actb_f32(base + 1024, n)
            t1B = self.actb_bufs(base + 1024, n)
            t2 = self.actb_f32(base + 1536, n)
            t2B = self.actb_bufs(base + 1536, n)
            p.op("dve", _tt(t1, xg, self.ropeC[:, t0:t0 + n], ALU.mult), reads=xgB + self.ropeCB, writes=t1B)
            p.op("dve", _tt(t2, rp[:, :n], self.ropeS[:, t0:t0 + n], ALU.mult), reads=[rpB] + self.ropeSB, writes=t2B)
            p.op("dve", _tt(out_ap, t1, t2, ALU.add), reads=t1B + t2B, writes=outB)
        else:
            p.op("act", _act(out_ap, xg, AF.Copy), reads=xgB, writes=outB)

    def stage(self, n):
        k = self.rot.get("stage", 0)
        self.rot["stage"] = k + 1
        base = (k % 2) * 2560 + 2048
        v = self.ACTB[:, base:base + 256].bitcast(BF16)[:, 0:n]
        return v, self.actb_bufs(base, 256)

    def attn_pre(self, blks, cb):
        p = self.p
        allb = blks + [cb]
        self.norm_mod(allb, 1, 0)
        wqkv = self.inp("attn_w_qkv", [1, 1024, 2048])
        wv_ = wqkv[0, :, :].rearrange("(kc q) f -> q kc f", q=128)
        self.ropeC, self.ropeCB = self.ring_load_f32(self.inp("ropec", [128, T]), T)
        self.ropeS, self.ropeSB = self.ring_load_f32(self.inp("ropes", [128, T]), T)
        kT_out = self.outp("kT_out", [4, 128, T + NCTX], BF16)
        v_out = self.outp("v_out", [18, 128, 516], BF16)
        qT_out = self.outp("qT_out", [8, 128, T], BF16)
        wk, wkb = self.ring_load(wv_[:, :, 1024:1536], 8, 512)
        for g in range(4):
            for b in allb:
                _, ps, psB = self.bank("qk", [0, 1])
                for kc in range(8):
                    p.op("pe", _mm(ps[:, :b.n], wk[:, kc, g * 128:(g + 1) * 128], b.a(kc), kc == 0, kc == 7),
                         reads=wkb + [b.aB[kc]], writes=[psB])
                st, stB = self.stage(b.n)
                self.qk_prep(ps[:, :b.n], psB, b.n, self.QKG[:, 1:2], b.s == 0, b.tb * 512, st, stB)
                c0 = b.tb * 512 if b.s == 0 else T
                self.out_dma(kT_out[g, :, c0:c0 + b.n], st, stB)
        wvp, wvb = self.ring_load(wv_[:, :, 1536:2048], 8, 512)
        vst = []
        for k in range(2):
            off = 5120 + k * 512
            v3 = self.ACTB[:, off:off + 258].bitcast(BF16).rearrange("p (g d) -> p g d", g=4)
            vb = self.actb_bufs(off, 258)
            p.op("dve", lambda e, v3=v3: e.memset(v3[:, :, 128:129], 1.0), writes=vb)
            vst.append((v3, vb, self.ACTB[:, off:off + 258].bitcast(BF16)))
        for tile in range(18):
            if tile < 16:
                b = blks[tile // 4]
                lo = (tile % 4) * 128
            else:
                b = cb
                lo = (tile - 16) * 128
            _, ps, psB = self.bank("v", [6])
            for kc in range(8):
                p.op("pe", _mm(ps[:, :512], b.a(kc)[:, lo:lo + 128], wvp[:, kc, :], kc == 0, kc == 7),
                     reads=wvb + [b.aB[kc]], writes=[psB])
            v3, vb, vflat = vst[tile % 2]
            p.op("act", _act(v3[:, :, 0:128], ps[:, :512].rearrange("p (g d) -> p g d", g=4), AF.Copy),
                 reads=[psB], writes=vb)
            self.out_dma(v_out[tile, :, :], vflat, vb)
        for half in range(2):
            wq, wqb = self.ring_load(wv_[:, :, half * 512:(half + 1) * 512], 8, 512)
            for hl in range(4):
                h = half * 4 + hl
                for b in blks:
                    _, ps, psB = self.bank("qk", [0, 1])
                    for kc in range(8):
                        p.op("pe", _mm(ps[:, :b.n], wq[:, kc, hl * 128:(hl + 1) * 128], b.a(kc), kc == 0, kc == 7),
                             reads=wqb + [b.aB[kc]], writes=[psB])
                    st, stB = self.stage(b.n)
                    self.qk_prep(ps[:, :b.n], psB, b.n, self.QKG[:, 0:1], True, b.tb * 512, st, stB)
                    self.out_dma(qT_out[h, :, b.tb * 512:b.tb * 512 + b.n], st, stB)

    def build_A(self):
        self.setup()
        self.load_h()
        blks = self.lat_blocks()
        cb = self.ctx_block()
        cin = self.inp("ctxin", [1024, NCTX])
        cv = cin.rearrange("(c p) t -> p c t", p=128)
        for ch in range(8):
            self.p.dma("sp", self.CH3[:, ch, :], cv[:, ch, :], writes=[self.CTXHB[ch]])
        self.ada_all(0)
        self.pool_mix(0, 0, [self.lat_seg(blks), self.ctx_seg(cb)])
        self.mlp(blks + [cb], 0, ada_next=1)
        self.attn_pre(blks, cb)
        self.store_h()
        self.p.emit(final_wait_bufs=self.outbufs)
        self.es.close()
        return self.nc

    def acta_buf(self, region):
        return self.AB[region // 4][region % 4]

    def attn_core(self, blks):
        p = self.p
        qT_in = self.inp("qT_in", [8, 128, T], BF16)
        kT_all = self.inp("kT_all", [4, 128, 8448], BF16)
        v_all = self.inp("v_all", [4, 128, 66 * 129], BF16)
        Q3 = self.ACTB[:, :].bitcast(BF16).rearrange("p (h t) -> p h t", h=8)
        for h in range(8):
            p.dma("sp", Q3[:, h, :], qT_in[h, :, :], writes=self.ACTBB[2 * h:2 * h + 2])
        OT3 = self.A3
        TPb = self.PS[:, 7 * 512:7 * 512 + 256].bitcast(BF16)
        ON = self.scr_bf(2, 512)
        pending = []

        def make_evac(head, qb):
            def ev():
                for qt in range(4):
                    rc = self.SMALL[:, 16 + qt:17 + qt]
                    p.op("dve", lambda e, rc=rc, qt=qt: e.reciprocal(out=rc, in_=self.PS[:, qt * 512 + 128:qt * 512 + 129]),
                         reads=[self.PSB[qt]], writes=[self.SMALLB])
                    p.op("dve", _ts(ON[:, qt * 128:(qt + 1) * 128], self.PS[:, qt * 512:qt * 512 + 128], rc, None, ALU.mult),
                         reads=[self.PSB[qt], self.SMALLB], writes=[self.SCRB[2]])
                for qt in range(4):
                    p.op("pe", lambda e, qt=qt: e.transpose(out=TPb[:, qt * 128:(qt + 1) * 128],
                                                            in_=ON[:, qt * 128:(qt + 1) * 128], identity=self.IDENT[:]),
                         reads=[self.SCRB[2], self.CONSTB], writes=[self.PSB[7]])
                p.op("dve", lambda e: e.tensor_copy(out=OT3[:, head, qb * 512:(qb + 1) * 512], in_=TPb[:, 0:512]),
                     reads=[self.PSB[7]], writes=[self.AB[head][qb]])
            return ev

        for g in range(4):
            sb0 = (g % 2) * 4
            Kv = self.BIG[:, sb0 * SLOT:sb0 * SLOT + 8448]
            KB = self.SLOTB[sb0:sb0 + 2]
            V3 = self.BIG[:, (sb0 + 2) * SLOT:(sb0 + 2) * SLOT + 66 * 129].rearrange("p (k d) -> p k d", k=66)
            VB = self.SLOTB[sb0 + 2:sb0 + 4]
            p.dma("sp", Kv, kT_all[g, :, :], writes=KB)
            p.dma("sp", self.BIG[:, (sb0 + 2) * SLOT:(sb0 + 2) * SLOT + 66 * 129], v_all[g, :, :], writes=VB)
            jobs = [(qb, h2, kt) for qb in range(4) for h2 in range(2) for kt in range(66)]

            def emit_S(job, jidx):
                qb, h2, kt = job
                head = 2 * g + h2
                _, ps, psB = self.bank("st", [4, 5, 6])
                p.op("pe", _mm(ps[:, :512], Kv[:, kt * 128:(kt + 1) * 128], Q3[:, head, qb * 512:(qb + 1) * 512], True, True),
                     reads=KB + self.ACTBB[2 * head:2 * head + 2], writes=[psB])
                return ps, psB

            cur = emit_S(jobs[0], 0)
            for j, job in enumerate(jobs):
                qb, h2, kt = job
                head = 2 * g + h2
                nxt = emit_S(jobs[j + 1], j + 1) if j + 1 < len(jobs) else None
                ps, psB = cur
                r = j % 2
                PT = self.scr_bf(r, 512)
                p.op("act", _act(PT, ps[:, :512], AF.Exp, scale=SM_SCALE, bias=self.EPSC[:, 1:2]),
                     reads=[psB, self.CONSTB], writes=[self.SCRB[r]])
                for qt in range(4):
                    p.op("pe", _mm(self.PS[:, qt * 512:qt * 512 + 129], PT[:, qt * 128:(qt + 1) * 128], V3[:, kt, :],
                                   kt == 0, kt == 65), reads=[self.SCRB[r]] + VB, writes=[self.PSB[qt]])
                if kt == 2 and pending:
                    pending.pop(0)()
                if kt == 65:
                    ev = make_evac(head, qb)
                    pending.append(ev)
                    if True:
                        pending.pop(0)()
                cur = nxt
        wo = self.inp("attn_w_o", [1, 1024, 1024])
        wov = wo[0, :, :].rearrange("(hc q) d -> q hc d", q=128)
        for half in range(2):
            wp, wpb = self.ring_load(wov[:, :, half * 512:(half + 1) * 512], 8, 512)
            for b in blks:
                for dl in range(4):
                    dc = half * 4 + dl
                    _, yp, ypB = self.bank("yp", [3, 4, 5, 6])
                    for hc in range(8):
                        p.op("pe", _mm(yp[:, :b.n], wp[:, hc, dl * 128:(dl + 1) * 128], OT3[:, hc, b.tb * 512:(b.tb + 1) * 512],
                                       hc == 0, hc == 7), reads=wpb + [self.AB[hc][b.tb]], writes=[ypB])
                    p.op("dve", _stt(b.h(dc), yp[:, :b.n], self.modcol(1, 2, dc, 0), b.h(dc), ALU.mult, ALU.add),
                         reads=[ypB, self.MODB[1], b.hB[dc]], writes=[b.hB[dc]])

    def gmlp(self, blks, i):
        p = self.p
        win = self.inp("gm_w_in", [1, 1024, 4096])
        wout = self.inp("gm_w_out", [1, 2048, 1024])
        winv = win[0, :, :].rearrange("(kc q) f -> q kc f", q=128)
        base = 6 * SLOT
        auxB = self.SLOTB[6:8]
        wsT = self.BIG[:, base:base + 1024].rearrange("p (g q) -> p g q", g=8)
        p.dma("pool", self.BIG[:, base:base + 1024], self.inp("gm_wsT", [128, 1024]), writes=auxB)
        BSB = self.BIG[:, base + 1024:base + 3072].bitcast(F32)
        p.dma("sp", BSB, self.inp("gm_bsb", [128, 1024]), writes=auxB)
        LG = self.BIG[:, base + 3072:base + 3136].bitcast(F32)
        p.dma("sp", LG, self.inp("gm_lgb", [128, 32]), writes=auxB)
        R3 = self.ACTB[:, 6144:8192].rearrange("p (c q) -> p c q", c=16)
        RB = self.ACTBB[12:16]
        VT = self.ACTB[:, 4096:6144]
        VTB = self.ACTBB[8:12]
        U3 = self.ACTB[:, 0:4096].bitcast(BF16).rearrange("p (c t) -> p c t", c=16)
        A8 = self.ACTA[:, 0:4096].rearrange("p (c t) -> p c t", c=8)
        G3 = self.ACTA[:, 4096:12288].rearrange("p (c t) -> p c t", c=16)
        VN = self.ACTA[:, 12288:14336]
        VNB = [self.acta_buf(24 + k) for k in range(4)]
        for g in range(8):
            _, ps, psB = self.bank("hp", [0, 1, 2])
            p.op("pe", _mm(ps[:, :128], self.ONES[:], wsT[:, g, :], True, True), reads=auxB + [self.CONSTB], writes=[psB])
            for cc in (2 * g, 2 * g + 1):
                p.op("dve", _stt(R3[:, cc, :], ps[:, :128], LG[:, 16 + cc:17 + cc], BSB[:, g * 128:(g + 1) * 128],
                                 ALU.mult, ALU.add), reads=[psB] + auxB, writes=RB)
        for b in blks:
            rs, rsB = self.rstd_block(b.h, b.hB, b.n)
            for kc in range(8):
                _, tmp, tmpB = self.bank("tmp", [4, 5])
                p.op("dve", _tt(tmp[:, :b.n], b.h(kc), rs, ALU.mult), reads=[b.hB[kc]] + rsB, writes=[tmpB])
                p.op("act", _act(A8[:, kc, :], tmp[:, :b.n], AF.Identity, scale=self.dercol(i, 0, kc, 0),
                                 bias=self.modcol(i, 0, kc, 0)), reads=[tmpB, self.DERB[i], self.MODB[i]],
                     writes=[self.acta_buf(kc)])
            a8B = [self.acta_buf(kc) for kc in range(8)]
            for cg in range(4):
                wp, wpb = self.ring_load(winv[:, :, cg * 512:(cg + 1) * 512], 8, 512)
                for cl in range(4):
                    c = cg * 4 + cl
                    _, ps, psB = self.bank("hp", [0, 1, 2])
                    for kc in range(8):
                        p.op("pe", _mm(ps[:, :512], wp[:, kc, cl * 128:(cl + 1) * 128], A8[:, kc, :], kc == 0, kc == 7),
                             reads=wpb + [a8B[kc]], writes=[psB])
                    p.op("act", _act(U3[:, c, :], ps[:, :512], AF.Gelu_apprx_tanh), reads=[psB], writes=[self.ACTBB[c // 2]])
            pv = [self.ring_load(winv[:, :, 2048 + cb * 512:2048 + (cb + 1) * 512], 8, 512) for cb in range(4)]
            for tt in range(4):
                for cb in range(4):
                    bk = 3 + cb
                    ps, psB = self.PS[:, bk * 512:(bk + 1) * 512], self.PSB[bk]
                    for kc in range(8):
                        p.op("pe", _mm(ps[:, :512], A8[:, kc, tt * 128:(tt + 1) * 128], pv[cb][0][:, kc, :], kc == 0, kc == 7),
                             reads=pv[cb][1] + [a8B[kc]], writes=[psB])
                    p.op("act", _act(VT[:, cb * 512:(cb + 1) * 512], ps[:, :512], AF.Gelu_apprx_tanh,
                                     accum_out=self.SMALL[:, 32 + cb:33 + cb]), reads=[psB], writes=[VTB[cb], self.SMALLB])
                    p.op("act", _act(self.scr_bf(3, 512), VT[:, cb * 512:(cb + 1) * 512], AF.Square,
                                     accum_out=self.SMALL[:, 36 + cb:37 + cb]), reads=[VTB[cb]],
                         writes=[self.SCRB[3], self.SMALLB])
                S = self.SMALL
                sB = [self.SMALLB]
                p.op("dve", lambda e: e.tensor_reduce(out=S[:, 40:41], in_=S[:, 32:36], axis=AX.X, op=ALU.add), reads=sB, writes=sB)
                p.op("dve", lambda e: e.tensor_reduce(out=S[:, 41:42], in_=S[:, 36:40], axis=AX.X, op=ALU.add), reads=sB, writes=sB)
                p.op("dve", _ts(S[:, 42:43], S[:, 40:41], 1.0 / 2048, None, ALU.mult), reads=sB, writes=sB)
                p.op("dve", _ts(S[:, 43:44], S[:, 41:42], 1.0 / 2048, None, ALU.mult), reads=sB, writes=sB)
                p.op("dve", _tt(S[:, 44:45], S[:, 42:43], S[:, 42:43], ALU.mult), reads=sB, writes=sB)
                p.op("dve", _tt(S[:, 45:46], S[:, 43:44], S[:, 44:45], ALU.subtract), reads=sB, writes=sB)
                p.op("act", _act(S[:, 46:47], S[:, 45:46], AF.Sqrt, bias=self.EPSC[:, 0:1]), reads=sB + [self.CONSTB], writes=sB)
                p.op("dve", lambda e: e.reciprocal(out=S[:, 47:48], in_=S[:, 46:47]), reads=sB, writes=sB)
                p.op("dve", _ts(VN, VT, S[:, 42:43], S[:, 47:48], ALU.subtract, ALU.mult), reads=VTB + sB, writes=VNB)
                for c in range(16):
                    mb = 7 if (c // 4) % 2 == 0 else 0
                    M = self.PS[:, mb * 512 + (c % 4) * 128:mb * 512 + (c % 4) * 128 + 128]
                    p.op("pe", _mm(M, VN[:, c * 128:(c + 1) * 128], wsT[:, c // 2, :], True, True),
                         reads=VNB + auxB, writes=[self.PSB[mb]])
                    r2 = c % 2
                    SV = self.scr_f32(r2, 128)
                    svB = self.SCRB[2 * r2:2 * r2 + 1]
                    p.op("dve", _stt(SV, M, LG[:, c:c + 1], R3[:, c, :], ALU.mult, ALU.add),
                         reads=[self.PSB[mb]] + auxB + RB, writes=svB)
                    p.op("dve", _tt(G3[:, c, tt * 128:(tt + 1) * 128], SV, U3[:, c, tt * 128:(tt + 1) * 128], ALU.mult),
                         reads=svB + [self.ACTBB[c // 2]], writes=[self.acta_buf(8 + c)])
            for half in range(2):
                po = [self.ring_load(wout[0, (half * 2 + k) * 512:(half * 2 + k + 1) * 512, :].rearrange("(fc q) d -> q fc d", q=128), 4, 1024)
                      for k in range(2)]
                for dc in range(8):
                    _, yp, ypB = self.bank("yp", [3, 4, 5, 6])
                    for k8 in range(8):
                        c = half * 8 + k8
                        w, wb = po[k8 // 4]
                        p.op("pe", _mm(yp[:, :512], w[:, k8 % 4, dc * 128:(dc + 1) * 128], G3[:, c, :], k8 == 0, k8 == 7),
                             reads=wb + [self.acta_buf(8 + c)], writes=[ypB])
                    p.op("dve", _stt(b.h(dc), yp[:, :512], self.modcol(i, 2, dc, 0), b.h(dc), ALU.mult, ALU.add),
                         reads=[ypB, self.MODB[i], b.hB[dc]], writes=[b.hB[dc]])

    def final_norm(self, blks):
        p = self.p
        outT = self.outp("outT", [1024, T])
        ov = outT.rearrange("(c q) t -> q c t", q=128)
        k = 0
        for b in blks:
            rs, rsB = self.rstd_block(b.h, b.hB, b.n)
            for ch in range(8):
                st = self.ACTB[:, (k % 16) * 512:(k % 16) * 512 + 512]
                stB = [self.ACTBB[k % 16]]
                k += 1
                p.op("dve", _stt(st, b.h(ch), self.FG[:, ch:ch + 1], rs, ALU.mult, ALU.mult),
                     reads=[b.hB[ch], self.CONSTB] + rsB, writes=stB)
                self.out_dma(ov[:, ch, b.tb * 512:(b.tb + 1) * 512], st, stB)

    def build_B(self):
        self.setup()
        self.load_h()
        blks = self.lat_blocks()
        self.ada_all(1)
        self.attn_core(blks)
        self.mlp(blks, 1, ada_next=2)
        self.gmlp(blks, 2)
        self.mlp(blks, 2)
        self.store_h()
        self.p.emit(final_wait_bufs=self.outbufs)
        self.es.close()
        return self.nc

    def build_C(self):
        self.setup()
        self.load_h()
        blks = self.lat_blocks()
        self.ada_all(3)
        self.pool_mix(3, 1, [self.lat_seg(blks)])
        self.mlp(blks, 3)
        self.final_norm(blks)
        self.p.emit(final_wait_bufs=self.outbufs)
        self.es.close()
        return self.nc


SEQ = 8192
_PROGS = {}


def _fm(v):
    v = np.asarray(v, np.float32)
    lead = v.shape[:-1]
    return np.moveaxis(v.reshape(*lead, 8, 128), -1, 0)


def _consts_for_core(c, inputs):
    b, r = c // 4, c % 4
    q0 = r * T
    f32 = np.float32
    d = {}
    cv = np.stack([_fm(inputs["c"][b]), _fm(inputs["c_ctx"])], axis=-1)
    d["cvec"] = np.ascontiguousarray(cv.reshape(128, 16), f32)
    ab = _fm(np.asarray(inputs["ada_b"]).reshape(4, 6, 1024))
    d["adab"] = np.ascontiguousarray(np.repeat(ab[..., None], 2, -1).reshape(128, 4 * 96), f32)
    ng = _fm(np.asarray(inputs["norm_g"]))
    d["ng2"] = np.ascontiguousarray(np.repeat(ng[..., None], 2, -1).reshape(128, 128), f32)
    ps = _fm(np.asarray(inputs["pool_scale"]))
    d["ps2"] = np.ascontiguousarray(np.repeat(ps[..., None], 2, -1).reshape(128, 32), f32)
    d["fg"] = np.ascontiguousarray(_fm(np.asarray(inputs["final_g"])).reshape(128, 8), f32)
    d["qkg"] = np.ascontiguousarray(np.stack([np.asarray(inputs["attn_q_g"])[0], np.asarray(inputs["attn_k_g"])[0]], 1), f32)
    d["ident"] = np.eye(128, dtype=f32)
    rot = np.zeros((128, 128), f32)
    for base in (0, 64):
        for m in range(32):
            rot[base + m + 32, base + m] = -1.0
            rot[base + m, base + m + 32] = 1.0
    d["rot"] = rot
    hm = np.zeros((128, 32), f32)
    hm[:, 0:8] = 1.0 if q0 > 0 else 0.0
    hm[:, 8:16] = 1.0 if q0 + T < SEQ else 0.0
    d["hm"] = hm
    edge = np.zeros((128, 128), f32)
    for g, w in enumerate((2, 4, 8, 16)):
        for e in range(16):
            pos = q0 + (e if e < 8 else T - 16 + e)
            lo, hi = max(pos - w // 2, 0), min(pos + w - w // 2, SEQ)
            edge[:, g * 16 + e] = 1.0 / (hi - lo)
            posc = e if e < 8 else NCTX - 16 + e
            lo, hi = max(posc - w // 2, 0), min(posc + w - w // 2, NCTX)
            edge[:, 64 + g * 16 + e] = 1.0 / (hi - lo)
    d["edge"] = edge
    return d


def _rope_tables(q0):
    t = np.arange(q0, q0 + T)
    row = (t // 64).astype(np.float32)
    col = (t % 64).astype(np.float32)
    inv = (np.float32(10000.0) ** (-np.arange(0, 64, 2, dtype=np.float32) / np.float32(64))).astype(np.float32)
    ang_r = (row[:, None] * inv[None, :]).astype(np.float32)
    ang_c = (col[:, None] * inv[None, :]).astype(np.float32)
    C = np.concatenate([np.cos(ang_r), np.cos(ang_r), np.cos(ang_c), np.cos(ang_c)], 1).T
    S = np.concatenate([np.sin(ang_r), np.sin(ang_r), np.sin(ang_c), np.sin(ang_c)], 1).T
    return np.ascontiguousarray(C, np.float32), np.ascontiguousarray(S, np.float32)


def _hin_from_rows(rows_b, r):
    q0 = r * T
    out = np.zeros((TC, 1024), np.float32)
    lo, hi = max(q0 - HAL, 0), min(q0 + T + HAL, SEQ)
    out[lo - (q0 - HAL):hi - (q0 - HAL)] = rows_b[lo:hi]
    return np.ascontiguousarray(out.T)


def _get_prog(seg):
    if seg not in _PROGS:
        mk = MK(seg)
        nc = getattr(mk, "build_" + seg)()
        _PROGS[seg] = (nc, list(mk.din.keys()))
    return _PROGS[seg]


def _launch(seg, per_core):
    nc, names = _get_prog(seg)
    in_maps = [{k: m[k] for k in names} for m in per_core]
    res = run_bass_kernel_spmd(nc, in_maps, core_ids=list(range(8)))
    return res.results


def run_seg_A(inputs):
    x = np.asarray(inputs["x"], np.float32)
    ctx = np.asarray(inputs["ctx"], np.float32)
    w = {k: np.asarray(inputs[k], np.float32) for k in ("ada_w", "mlp_w1", "mlp_w2", "pool_w", "attn_w_qkv")}
    per_core = []
    for c in range(8):
        b, r = c // 4, c % 4
        m = _consts_for_core(c, inputs)
        m["hin"] = _hin_from_rows(x[b], r)
        m["ctxin"] = np.ascontiguousarray(ctx[b].T)
        m["ropec"], m["ropes"] = _rope_tables(r * T)
        m.update(w)
        per_core.append(m)
    return _launch("A", per_core)


def _weights(inputs, names):
    return {k: np.asarray(inputs[k], np.float32) for k in names}


def run_seg_B(inputs, resA):
    import ml_dtypes
    bf = ml_dtypes.bfloat16
    w = _weights(inputs, ("ada_w", "mlp_w1", "mlp_w2", "attn_w_o", "gm_w_in", "gm_w_out"))
    ws = np.asarray(inputs["gm_ws"], np.float32)[0]
    wsT = np.ascontiguousarray(ws.transpose(2, 0, 1).reshape(128, 1024))
    bs = np.asarray(inputs["gm_bs"], np.float32)[0]
    bsb = np.ascontiguousarray(np.broadcast_to(bs.reshape(1, 1024), (128, 1024)), np.float32)
    lgb = np.concatenate([_fm(np.asarray(inputs["gm_ln_g"])[0].reshape(2, 1024)).reshape(128, 16),
                          _fm(np.asarray(inputs["gm_ln_b"])[0].reshape(2, 1024)).reshape(128, 16)], 1)
    lgb = np.ascontiguousarray(lgb, np.float32)
    per_core = []
    kv_cache = {}
    for b in range(2):
        kT = np.zeros((4, 128, 8448), bf)
        va = np.zeros((4, 128, 66, 129), bf)
        for r in range(4):
            ra = resA[b * 4 + r]
            k = np.asarray(ra["kT_out"])
            v = np.asarray(ra["v_out"]).reshape(18, 128, 4, 129)
            kT[:, :, r * T:(r + 1) * T] = k[:, :, :T]
            va[:, :, r * 16:(r + 1) * 16, :] = v[:16].transpose(2, 1, 0, 3)
            if r == 0:
                kT[:, :, 4 * T:] = k[:, :, T:]
                va[:, :, 64:66, :] = v[16:].transpose(2, 1, 0, 3)
        kv_cache[b] = (kT, np.ascontiguousarray(va.reshape(4, 128, 66 * 129)))
    for c in range(8):
        b, r = c // 4, c % 4
        m = _consts_for_core(c, inputs)
        hin = np.zeros((1024, TC), np.float32)
        hin[:, HAL:HAL + T] = np.asarray(resA[c]["hout"])
        m["hin"] = hin
        m["qT_in"] = np.asarray(resA[c]["qT_out"])
        m["kT_all"], m["v_all"] = kv_cache[b]
        m["gm_wsT"], m["gm_bsb"], m["gm_lgb"] = wsT, bsb, lgb
        m.update(w)
        per_core.append(m)
    return _launch("B", per_core)


def run_seg_C(inputs, resB):
    w = _weights(inputs, ("ada_w", "mlp_w1", "mlp_w2", "pool_w"))
    per_core = []
    for b in range(2):
        rows = np.concatenate([np.asarray(resB[b * 4 + r]["hout"]).T for r in range(4)], 0)
        for r in range(4):
            m = _consts_for_core(b * 4 + r, inputs)
            m["hin"] = _hin_from_rows(rows, r)
            m.update(w)
            per_core.append(m)
    return _launch("C", per_core)


def kernel(**inputs):
    resA = run_seg_A(inputs)
    resB = run_seg_B(inputs, resA)
    resC = run_seg_C(inputs, resB)
    out = np.zeros((2, SEQ, 1024), np.float32)
    for c in range(8):
        b, r = c // 4, c % 4
        out[b, r * T:(r + 1) * T, :] = np.asarray(resC[c]["outT"]).T
    return out
```

```python
import numpy as np
import concourse.bass as bass
import concourse.mybir as mybir
from concourse.bass_utils import run_bass_kernel_spmd

F32 = mybir.dt.float32
BF16 = mybir.dt.bfloat16
AF = mybir.ActivationFunctionType
ALU = mybir.AluOpType
AX = mybir.AxisListType


class Buf:
    __slots__ = ("name", "last_write", "readers")

    def __init__(self, name=""):
        self.name = name
        self.last_write = None
        self.readers = []


class _Ins:
    __slots__ = ("eng", "fn", "deps", "idx", "signal", "dma_tok", "dma_prev")

    def __init__(self, eng, fn, deps, idx):
        self.eng = eng
        self.fn = fn
        self.deps = deps
        self.idx = idx
        self.signal = False
        self.dma_tok = None
        self.dma_prev = None


ENGS = ("pe", "act", "dve", "pool", "sp")
SEM_LIMIT = 12000
N_DMA_SEMS = 12


class Prog:
    def __init__(self, nc):
        self.nc = nc
        self.streams = {e: [] for e in ENGS}
        self.dma_count = {e: 0 for e in ENGS}

    def _collect(self, reads, writes):
        deps = []
        for b in reads:
            if b.last_write is not None:
                deps.append(b.last_write)
        for b in writes:
            if b.last_write is not None:
                deps.append(b.last_write)
            deps.extend(b.readers)
        return deps

    def _commit(self, tok, reads, writes):
        for b in reads:
            b.readers.append(tok)
        for b in writes:
            b.last_write = tok
            b.readers = []

    def op(self, eng, fn, reads=(), writes=()):
        deps = self._collect(reads, writes)
        st = self.streams[eng]
        ins = _Ins(eng, fn, deps, len(st))
        st.append(ins)
        self._commit(("c", eng, ins.idx), reads, writes)
        return ins

    def coll(self, kind, groups, src, dst, reads=(), writes=()):
        return self.dma("pool", None, None, reads, writes,
                        fn=lambda e: e.collective_compute(kind, ALU.bypass, replica_groups=groups,
                                                          ins=[src], outs=[dst]))

    def dma(self, eng, out, in_, reads=(), writes=(), fn=None):
        deps = self._collect(reads, writes)
        st = self.streams[eng]
        k = self.dma_count[eng]
        self.dma_count[eng] = k + 1
        slot, rnd = k % N_DMA_SEMS, k // N_DMA_SEMS
        if fn is None:
            fn = lambda e: e.dma_start(out=out, in_=in_)
        ins = _Ins(eng, fn, deps, len(st))
        ins.dma_tok = ("d", eng, slot, 16 * (rnd + 1))
        if rnd > 0:
            ins.dma_prev = ("d", eng, slot, 16 * rnd)
        st.append(ins)
        self._commit(ins.dma_tok, reads, writes)
        return ins

    def emit(self, final_wait_bufs=()):
        nc = self.nc
        final_deps = []
        for b in final_wait_bufs:
            if b.last_write is not None:
                final_deps.append(b.last_write)
        for e in ENGS:
            for ins in self.streams[e]:
                best = {}
                kept = []
                for d in ins.deps:
                    if d[0] == "c":
                        if e == "pe" and d[1] == "pe":
                            continue
                        if d[1] not in best or best[d[1]][2] < d[2]:
                            best[d[1]] = d
                    else:
                        kept.append(d)
                ins.deps = kept + list(best.values())
                for d in ins.deps:
                    if d[0] == "c":
                        self.streams[d[1]][d[2]].signal = True
        for d in final_deps:
            if d[0] == "c":
                self.streams[d[1]][d[2]].signal = True
        count_of = {}
        n_epochs = {}
        for e in ENGS:
            c = 0
            for ins in self.streams[e]:
                if ins.signal:
                    c += 1
                    count_of[(e, ins.idx)] = c
            n_epochs[e] = max(1, (c + SEM_LIMIT - 1) // SEM_LIMIT)
        import contextlib
        with contextlib.ExitStack() as es:
            csem = {e: [es.enter_context(nc.semaphore(f"c_{e}_{i}")) for i in range(n_epochs[e])]
                    for e in ENGS}
            dsem = {e: [es.enter_context(nc.semaphore(f"d_{e}_{i}")) for i in range(N_DMA_SEMS)]
                    for e in ENGS if self.dma_count[e] > 0}
            block = es.enter_context(nc.Block())

            def resolve(tok):
                if tok[0] == "c":
                    c = count_of[(tok[1], tok[2])]
                    ep = (c - 1) // SEM_LIMIT
                    return (csem[tok[1]][ep], ("c", tok[1], ep), c - ep * SEM_LIMIT)
                return (dsem[tok[1]][tok[2]], ("d", tok[1], tok[2]), tok[3])

            def run(e, eng):
                known = {}
                for ins in self.streams[e]:
                    toks = list(ins.deps)
                    if ins.dma_prev is not None:
                        toks.append(ins.dma_prev)
                    need = {}
                    for t in toks:
                        sem, key, val = resolve(t)
                        if known.get(key, 0) >= val:
                            continue
                        if key not in need or need[key][1] < val:
                            need[key] = (sem, val)
                    for key, (sem, val) in need.items():
                        eng.wait_ge(sem, val)
                        known[key] = val
                    bi = ins.fn(eng)
                    if ins.dma_tok is not None:
                        bi.then_inc(dsem[e][ins.dma_tok[2]], 16)
                    elif ins.signal:
                        c = count_of[(e, ins.idx)]
                        ep = (c - 1) // SEM_LIMIT
                        bi.then_inc(csem[e][ep], 1)
                if e == "sp":
                    for t in final_deps:
                        sem, key, val = resolve(t)
                        if known.get(key, 0) >= val:
                            continue
                        eng.wait_ge(sem, val)
                        known[key] = max(known.get(key, 0), val)

            @block.tensor
            def _(eng):
                run("pe", eng)

            @block.scalar
            def _(eng):
                run("act", eng)

            @block.vector
            def _(eng):
                run("dve", eng)

            @block.gpsimd
            def _(eng):
                run("pool", eng)

            @block.sync
            def _(eng):
                run("sp", eng)


T = 2048
HAL = 8
TC = T + 2 * HAL
NCTX = 256
SLOT = 4288
NSLOT = 8
RING_SLOTS = 6
EPS = 1e-6
SM_SCALE = 128 ** -0.5
SM_SHIFT = -(128 ** 0.5)


def _mm(out, lhsT, rhs, start, stop):
    return lambda e: e.matmul(out, lhsT=lhsT, rhs=rhs, start=start, stop=stop)


def _act(out, in_, func, **kw):
    return lambda e: e.activation(out=out, in_=in_, func=func, **kw)


def _tt(out, in0, in1, op):
    return lambda e: e.tensor_tensor(out=out, in0=in0, in1=in1, op=op)


def _stt(out, in0, scalar, in1, op0, op1):
    return lambda e: e.scalar_tensor_tensor(out=out, in0=in0, scalar=scalar, in1=in1, op0=op0, op1=op1)


def _ts(out, in0, s1, s2, op0, op1=None):
    if op1 is None:
        return lambda e: e.tensor_scalar(out=out, in0=in0, scalar1=s1, scalar2=None, op0=op0)
    return lambda e: e.tensor_scalar(out=out, in0=in0, scalar1=s1, scalar2=s2, op0=op0, op1=op1)


class Blk:
    pass


class MK:
    ADA_LAYERS = {"A": (0, 1), "B": (1, 2), "C": (3,)}
    MLP_LAYERS = {"A": (0,), "B": (1, 2), "C": (3,)}

    def __init__(self, seg):
        import contextlib
        self.seg = seg
        self.ada_l = self.ADA_LAYERS[seg]
        self.mlp_l = self.MLP_LAYERS[seg]
        self.nc = nc = bass.Bass("TRN2", target_bir_lowering=False)
        self.es = es = contextlib.ExitStack()
        self.p = Prog(nc)
        self.din = {}
        self.dout = {}

        def sb(name, shape, dt):
            return es.enter_context(nc.sbuf_tensor(name, shape, dt))

        self.HT = sb("HT", [128, 8 * TC], F32)
        self.HT3 = self.HT[:, :].rearrange("p (c t) -> p c t", c=8)
        self.ACTA = sb("ACTA", [128, 8 * T], BF16)
        self.A3 = self.ACTA[:, :].rearrange("p (c t) -> p c t", c=8)
        self.ACTB = sb("ACTB", [128, 8192], F32)
        self.BIG = sb("BIG", [128, NSLOT * SLOT], BF16)
        self.MOD = sb("MOD", [128, 4 * 96], F32)
        self.ADAB = sb("ADAB", [128, 96], F32)
        self.NG2 = sb("NG2", [128, 128], F32)
        self.PS2 = sb("PS2", [128, 32], F32)
        self.FG = sb("FG", [128, 8], F32)
        self.CV = sb("CV", [128, 16], F32)
        self.SBF = sb("SBF", [128, 16], BF16)
        self.ONES = sb("ONES", [128, 128], BF16)
        self.IDENT = sb("IDENT", [128, 128], BF16)
        self.ROT = sb("ROT", [128, 128], F32)
        self.EPSC = sb("EPSC", [128, 2], F32)
        self.DER = sb("DER", [128, 4 * 48], F32)
        self.HM = sb("HM", [128, 32], F32)
        self.EDGE = sb("EDGE", [128, 128], F32)
        self.QKG = sb("QKG", [128, 2], F32)
        self.SCR = sb("SCR", [128, 1024], F32)
        self.SMALL = sb("SMALL", [128, 64], F32)
        self.PS = es.enter_context(nc.psum_tensor("PS", [128, 8 * 512], F32))

        self.HTB = [[Buf(f"h{c}_{t}") for t in range(4)] for c in range(8)]
        self.HALB = [Buf(f"hal{c}") for c in range(8)]
        self.AB = [[Buf(f"a{c}_{t}") for t in range(4)] for c in range(8)]
        self.ACTBB = [Buf(f"actb{k}") for k in range(16)]
        self.SLOTB = [Buf(f"slot{k}") for k in range(NSLOT)]
        self.PSB = [Buf(f"ps{k}") for k in range(8)]
        self.SCRB = [Buf(f"scr{k}") for k in range(4)]
        self.MODB = [Buf(f"mod{i}") for i in range(4)]
        self.DERB = [Buf(f"der{i}") for i in range(4)]
        self.CONSTB = Buf("const")
        self.SMALLB = Buf("small")
        self.ADABB = Buf("adab")
        self.ring_pos = 0
        self.rot = {}
        self.outbufs = []

    def inp(self, name, shape, dt=F32):
        t = self.nc.dram_tensor(name, list(shape), dt, kind="ExternalInput").ap()
        self.din[name] = t
        return t

    def outp(self, name, shape, dt=F32):
        t = self.nc.dram_tensor(name, list(shape), dt, kind="ExternalOutput").ap()
        self.dout[name] = t
        return t

    def bank(self, role, banks):
        k = self.rot.get(role, 0)
        self.rot[role] = k + 1
        b = banks[k % len(banks)]
        return b, self.PS[:, b * 512:(b + 1) * 512], self.PSB[b]

    def scr_bf(self, r, n=512):
        return self.SCR[:, r * 256:(r + 1) * 256].bitcast(BF16)[:, 0:n]

    def scr_f32(self, r2, n=512):
        return self.SCR[:, r2 * 512:r2 * 512 + n]

    def actb_f32(self, off, n):
        return self.ACTB[:, off:off + n]

    def actb_bufs(self, off, n):
        return self.ACTBB[off // 512:(off + n + 511) // 512]

    def ring_load(self, dram_ap, a, b, eng="pool"):
        elems = a * b
        n = (elems + SLOT - 1) // SLOT
        if self.ring_pos + n > RING_SLOTS:
            self.ring_pos = 0
        pos = self.ring_pos
        self.ring_pos = (pos + n) % RING_SLOTS
        view = self.BIG[:, pos * SLOT:pos * SLOT + elems].rearrange("p (a b) -> p a b", a=a)
        bufs = self.SLOTB[pos:pos + n]
        self.p.dma(eng, view, dram_ap, writes=bufs)
        return view, bufs

    def out_dma(self, dst, src, reads):
        b = Buf("out")
        self.outbufs.append(b)
        self.p.dma("sp", dst, src, reads=reads, writes=[b])

    def col(self, t, idx):
        return t[:, idx:idx + 1]

    def modcol(self, i, part, ch, s):
        return self.col(self.MOD, i * 96 + (part * 8 + ch) * 2 + s)

    def dercol(self, i, kind, ch, s):
        return self.col(self.DER, i * 48 + (kind * 8 + ch) * 2 + s)

    def setup(self):
        p = self.p
        c = self.CONSTB
        p.op("dve", lambda e: e.memset(self.ONES[:], 1.0), writes=[c])
        p.op("dve", lambda e: e.memset(self.EPSC[:, 0:1], EPS), writes=[c])
        p.op("dve", lambda e: e.memset(self.EPSC[:, 1:2], SM_SHIFT), writes=[c])
        small = [("ng2", self.NG2, 128), ("ps2", self.PS2, 32), ("fg", self.FG, 8), ("cvec", self.CV, 16),
                 ("rot", self.ROT, 128), ("hm", self.HM, 32), ("edge", self.EDGE, 128), ("qkg", self.QKG, 2)]
        for name, t, n in small:
            d = self.inp(name, [128, n])
            p.dma("sp", t[:], d, writes=[c])
        d = self.inp("ident", [128, 128])
        p.dma("pool", self.IDENT[:], d, writes=[c])
        p.op("act", _act(self.SBF[:], self.CV[:], AF.Silu), reads=[c], writes=[c])
        self.ada_w = self.inp("ada_w", [len(self.ada_l), 1024, 6144])
        self.adab_d = self.inp("adab", [128, 4 * 96])
        self.w1 = self.inp("mlp_w1", [len(self.mlp_l), 1024, 4096])
        self.w2 = self.inp("mlp_w2", [len(self.mlp_l), 4096, 1024])

    def load_h(self):
        hin = self.inp("hin", [1024, TC])
        v = hin.rearrange("(c p) t -> p c t", p=128)
        for ch in range(8):
            self.p.dma("sp", self.HT3[:, ch, :], v[:, ch, :], writes=self.HTB[ch] + [self.HALB[ch]])

    def store_h(self):
        hout = self.outp("hout", [1024, T])
        v = hout.rearrange("(c p) t -> p c t", p=128)
        for ch in range(8):
            self.out_dma(v[:, ch, :], self.HT3[:, ch, HAL:HAL + T], self.HTB[ch])

    def lat_blocks(self):
        blks = []
        for tb in range(4):
            b = Blk()
            b.n, b.s, b.tb = 512, 0, tb
            b.h = (lambda ch, tb=tb: self.HT3[:, ch, HAL + tb * 512:HAL + (tb + 1) * 512])
            b.hB = [self.HTB[ch][tb] for ch in range(8)]
            b.a = (lambda ch, tb=tb: self.A3[:, ch, tb * 512:(tb + 1) * 512])
            b.aB = [self.AB[ch][tb] for ch in range(8)]
            hid3 = self.ACTB[:, 0:4096].bitcast(BF16).rearrange("p (c t) -> p c t", c=4)
            b.hid = (lambda fc, tb=tb, hid3=hid3: hid3[:, fc, tb * 512:(tb + 1) * 512])
            b.hidB = [self.ACTBB[fc * 2 + tb // 2] for fc in range(4)]
            blks.append(b)
        return blks

    def ctx_block(self):
        b = Blk()
        b.n, b.s, b.tb = NCTX, 1, 0
        base = 6 * SLOT
        ch3 = self.BIG[:, base:base + 4096].bitcast(F32).rearrange("p (c t) -> p c t", c=8)
        ca3 = self.BIG[:, base + 4096:base + 6144].rearrange("p (c t) -> p c t", c=8)
        chid = self.BIG[:, base + 6144:base + 7168].rearrange("p (c t) -> p c t", c=4)
        self.CH3 = ch3
        b.h = lambda ch: ch3[:, ch, :]
        self.CTXHB = [Buf(f"ctxh{c}") for c in range(8)]
        b.hB = self.CTXHB
        b.a = lambda ch: ca3[:, ch, :]
        b.aB = [Buf(f"ctxa{c}") for c in range(8)]
        b.hid = lambda fc: chid[:, fc, :]
        b.hidB = [Buf(f"ctxhid{c}") for c in range(4)]
        return b

    def ada_piece(self, i, k):
        p = self.p
        v = self.ada_w[self.ada_l.index(i), :, :].rearrange("(kc q) f -> q kc f", q=128)[:, :, k * 512:(k + 1) * 512]
        w, wb = self.ring_load(v, 8, 512)
        s3 = self.SBF[:, :].rearrange("p (kc s) -> p kc s", s=2)
        for j in range(4):
            idx = k * 4 + j
            for kc in range(8):
                p.op("pe", _mm(self.PS[:, 7 * 512 + 2 * idx:7 * 512 + 2 * idx + 2], w[:, kc, j * 128:(j + 1) * 128],
                               s3[:, kc, :], kc == 0, kc == 7), reads=wb + [self.CONSTB], writes=[self.PSB[7]])

    def ada_finish(self, i):
        p = self.p
        p.dma("sp", self.ADAB[:], self.adab_d[:, i * 96:(i + 1) * 96], writes=[self.ADABB])
        p.op("dve", _tt(self.MOD[:, i * 96:(i + 1) * 96], self.PS[:, 7 * 512:7 * 512 + 96], self.ADAB[:], ALU.add),
             reads=[self.PSB[7], self.ADABB], writes=[self.MODB[i]])
        for kind, part, which in ((0, 1, 0), (1, 4, 1)):
            src = self.MOD[:, i * 96 + part * 16:i * 96 + part * 16 + 16]
            dst = self.DER[:, i * 48 + kind * 16:i * 48 + kind * 16 + 16]
            ng = self.NG2[:, (i * 2 + which) * 16:(i * 2 + which) * 16 + 16]
            p.op("dve", _stt(dst, src, 1.0, ng, ALU.add, ALU.mult), reads=[self.MODB[i], self.CONSTB],
                 writes=[self.DERB[i]])
        if i % 3 == 0:
            j = i // 3
            src = self.MOD[:, i * 96 + 2 * 16:i * 96 + 3 * 16]
            dst = self.DER[:, i * 48 + 32:i * 48 + 48]
            p.op("dve", _tt(dst, src, self.PS2[:, j * 16:(j + 1) * 16], ALU.mult), reads=[self.MODB[i], self.CONSTB],
                 writes=[self.DERB[i]])

    def ada_all(self, i):
        for k in range(12):
            self.ada_piece(i, k)
        self.ada_finish(i)

    def rstd_block(self, hfn, hbufs, n, dst=None, dstB=None):
        p = self.p
        _, ss, ssB = self.bank("ss", [0, 1])
        for ch in range(8):
            r = ch % 2
            sq = self.scr_bf(r, n)
            hb = hbufs[ch] if isinstance(hbufs[ch], list) else [hbufs[ch]]
            p.op("act", _act(sq, hfn(ch), AF.Square), reads=hb, writes=[self.SCRB[r]])
            p.op("pe", _mm(ss[:, :n], self.ONES[:], sq, ch == 0, ch == 7), reads=[self.SCRB[r], self.CONSTB],
                 writes=[ssB])
        if dst is None:
            _, rs, rsB = self.bank("rs", [2, 3])
            dst, dstB = rs[:, :n], [rsB]
        p.op("act", _act(dst, ss[:, :n], AF.Sqrt, scale=1.0 / 1024, bias=self.EPSC[:, 0:1]),
             reads=[ssB, self.CONSTB], writes=dstB)
        p.op("dve", lambda e, dst=dst: e.reciprocal(out=dst, in_=dst), reads=dstB, writes=dstB)
        return dst, dstB

    def norm_mod(self, blks, i, which):
        p = self.p
        part_sh = 0 if which == 0 else 3
        for b in blks:
            rs, rsB = self.rstd_block(b.h, b.hB, b.n)
            for ch in range(8):
                _, tmp, tmpB = self.bank("tmp", [4, 5])
                p.op("dve", _tt(tmp[:, :b.n], b.h(ch), rs, ALU.mult), reads=[b.hB[ch]] + rsB, writes=[tmpB])
                p.op("act", _act(b.a(ch), tmp[:, :b.n], AF.Identity, scale=self.dercol(i, which, ch, b.s),
                                 bias=self.modcol(i, part_sh, ch, b.s)),
                     reads=[tmpB, self.DERB[i], self.MODB[i]], writes=[b.aB[ch]])

    def mlp(self, blks, i, ada_next=None):
        p = self.p
        self.norm_mod(blks, i, 1)
        ada_k = 0
        for fb in range(8):
            li = self.mlp_l.index(i)
            v1 = self.w1[li, :, :].rearrange("(kc q) f -> q kc f", q=128)[:, :, fb * 512:(fb + 1) * 512]
            w1p, w1b = self.ring_load(v1, 8, 512)
            v2 = self.w2[li, fb * 512:(fb + 1) * 512, :].rearrange("(fc q) d -> q fc d", q=128)
            w2p, w2b = self.ring_load(v2, 4, 1024)
            for b in blks:
                for fc in range(4):
                    _, hp, hpB = self.bank("hp", [0, 1, 2])
                    for kc in range(8):
                        p.op("pe", _mm(hp[:, :b.n], w1p[:, kc, fc * 128:(fc + 1) * 128], b.a(kc), kc == 0, kc == 7),
                             reads=w1b + [b.aB[kc]], writes=[hpB])
                    r2 = self.rot.get("relu", 0) % 2
                    self.rot["relu"] = r2 + 1
                    rr = self.scr_f32(r2, b.n)
                    rrB = self.SCRB[2 * r2:2 * r2 + 2]
                    p.op("act", _act(rr, hp[:, :b.n], AF.Relu), reads=[hpB], writes=rrB)
                    p.op("dve", _tt(b.hid(fc), hp[:, :b.n], rr, ALU.mult), reads=[hpB] + rrB, writes=[b.hidB[fc]])
            for b in blks:
                for dc in range(8):
                    _, yp, ypB = self.bank("yp", [3, 4, 5, 6])
                    for fc in range(4):
                        p.op("pe", _mm(yp[:, :b.n], w2p[:, fc, dc * 128:(dc + 1) * 128], b.hid(fc), fc == 0, fc == 3),
                             reads=w2b + [b.hidB[fc]], writes=[ypB])
                    p.op("dve", _stt(b.h(dc), yp[:, :b.n], self.modcol(i, 5, dc, b.s), b.h(dc), ALU.mult, ALU.add),
                         reads=[ypB, self.MODB[i], b.hB[dc]], writes=[b.hB[dc]])
            if ada_next is not None:
                for _ in range(2):
                    if ada_k < 12:
                        self.ada_piece(ada_next, ada_k)
                        ada_k += 1
        if ada_next is not None:
            self.ada_finish(ada_next)

    def pool_mix(self, i, j, segs):
        p = self.p
        pw = self.inp_once("pool_w", [2, 4, 256, 256])
        for sg in segs:
            n, s = sg["n"], sg["s"]
            ncol = n + 2 * HAL
            A = self.actb_f32(0, ncol)
            AB_ = self.actb_bufs(0, ncol)
            SA = self.actb_f32(2560, ncol)
            SAB = self.actb_bufs(2560, ncol)
            RS = self.actb_f32(5120, ncol)
            RSB = self.actb_bufs(5120, ncol)
            c0 = 0
            for (o, m) in sg["stat_cols"]:
                self.rstd_block(lambda ch, o=o, m=m: sg["hcols"](ch, o, m), sg["hbufs_all"], m,
                                dst=RS[:, sg["rs_off"] + o:sg["rs_off"] + o + m], dstB=RSB)
            for ch in range(8):
                g = ch // 2
                w = 2 << g
                hb = sg["hbufs_ch"](ch)
                if sg["halo"]:
                    p.op("dve", _tt(A[:, 0:ncol], sg["hcols"](ch, 0, ncol), RS[:, 0:ncol], ALU.mult),
                         reads=hb + RSB, writes=AB_)
                    p.op("act", _act(A[:, 0:ncol], A[:, 0:ncol], AF.Identity, scale=self.dercol(i, 0, ch, s),
                                     bias=self.modcol(i, 0, ch, s)), reads=AB_ + [self.DERB[i], self.MODB[i]],
                         writes=AB_)
                    p.op("dve", _tt(A[:, 0:HAL], A[:, 0:HAL], self.HM[:, 0:HAL], ALU.mult), reads=AB_ + [self.CONSTB],
                         writes=AB_)
                    p.op("dve", _tt(A[:, HAL + n:ncol], A[:, HAL + n:ncol], self.HM[:, HAL:2 * HAL], ALU.mult),
                         reads=AB_ + [self.CONSTB], writes=AB_)
                else:
                    p.op("dve", lambda e, A=A, ncol=ncol: e.memset(A[:, 0:ncol], 0.0), writes=AB_)
                    p.op("dve", _tt(A[:, HAL:HAL + n], sg["hcols"](ch, 0, n), RS[:, HAL:HAL + n], ALU.mult),
                         reads=hb + RSB, writes=AB_)
                    p.op("act", _act(A[:, HAL:HAL + n], A[:, HAL:HAL + n], AF.Identity,
                                     scale=self.dercol(i, 0, ch, s), bias=self.modcol(i, 0, ch, s)),
                         reads=AB_ + [self.DERB[i], self.MODB[i]], writes=AB_)
                m = ncol - 1
                p.op("dve", _tt(SA[:, 0:m], A[:, 0:m], A[:, 1:m + 1], ALU.add), reads=AB_, writes=SAB)
                sh = 2
                while sh < w:
                    m2 = m - sh
                    p.op("dve", _tt(SA[:, 0:m2], SA[:, 0:m2], SA[:, sh:sh + m2], ALU.add), reads=SAB, writes=SAB)
                    m = m2
                    sh *= 2
                o = HAL - w // 2
                for b in sg["blks"]:
                    t0 = b.tb * 512
                    p.op("dve", _stt(b.a(ch), SA[:, o + t0:o + t0 + b.n], 1.0 / w, A[:, HAL + t0:HAL + t0 + b.n],
                                     ALU.mult, ALU.subtract), reads=SAB + AB_, writes=[b.aB[ch]])
                eo = sg["edge_off"] + g * 16
                tmpe = self.SMALL[:, 0:16]
                for (dst_t, ecol) in ((0, 0), (n - HAL, HAL)):
                    bfix = sg["blks"][0] if dst_t == 0 else sg["blks"][-1]
                    lt = dst_t - bfix.tb * 512
                    p.op("dve", _tt(tmpe[:, ecol:ecol + HAL], SA[:, o + dst_t:o + dst_t + HAL],
                                    self.EDGE[:, eo + ecol:eo + ecol + HAL], ALU.mult), reads=SAB + [self.CONSTB],
                         writes=[self.SMALLB])
                    p.op("dve", _tt(bfix.a(ch)[:, lt:lt + HAL], tmpe[:, ecol:ecol + HAL],
                                    A[:, HAL + dst_t:HAL + dst_t + HAL], ALU.subtract),
                         reads=[self.SMALLB] + AB_, writes=[bfix.aB[ch]])
        v = pw[j, :, :, :].rearrange("g (kc q) d -> q (g kc) d", q=128)
        wp, wpb = self.ring_load(v, 8, 256)
        for sg in segs:
            for b in sg["blks"]:
                for dc in range(8):
                    g, dh = dc // 2, dc % 2
                    _, yp, ypB = self.bank("yp", [3, 4, 5, 6])
                    for kc in range(2):
                        p.op("pe", _mm(yp[:, :b.n], wp[:, g * 2 + kc, dh * 128:(dh + 1) * 128], b.a(g * 2 + kc),
                                       kc == 0, kc == 1), reads=wpb + [b.aB[g * 2 + kc]], writes=[ypB])
                    p.op("dve", _stt(b.h(dc), yp[:, :b.n], self.dercol(i, 2, dc, b.s), b.h(dc), ALU.mult, ALU.add),
                         reads=[ypB, self.DERB[i], b.hB[dc]], writes=[b.hB[dc]])

    def inp_once(self, name, shape, dt=F32):
        if name in self.din:
            return self.din[name]
        return self.inp(name, shape, dt)

    def lat_seg(self, blks):
        return dict(n=T, s=0, halo=True, blks=blks, edge_off=0, rs_off=0,
                    stat_cols=[(0, 512), (512, 512), (1024, 512), (1536, 512), (2048, 16)],
                    hcols=lambda ch, o, m: self.HT3[:, ch, o:o + m],
                    hbufs_all=[self.HTB[ch] + [self.HALB[ch]] for ch in range(8)],
                    hbufs_ch=lambda ch: self.HTB[ch] + [self.HALB[ch]])

    def ctx_seg(self, cb):
        return dict(n=NCTX, s=1, halo=False, blks=[cb], edge_off=64, rs_off=HAL,
                    stat_cols=[(0, 256)],
                    hcols=lambda ch, o, m: self.CH3[:, ch, o:o + m],
                    hbufs_all=[[self.CTXHB[ch]] for ch in range(8)],
                    hbufs_ch=lambda ch: [self.CTXHB[ch]])

    def ring_load_f32(self, dram_ap, ncols, eng="sp"):
        elems = 2 * ncols
        n = (elems + SLOT - 1) // SLOT
        if self.ring_pos + n > RING_SLOTS:
            self.ring_pos = 0
        pos = self.ring_pos
        self.ring_pos = (pos + n) % RING_SLOTS
        view = self.BIG[:, pos * SLOT:pos * SLOT + elems].bitcast(F32)
        bufs = self.SLOTB[pos:pos + n]
        self.p.dma(eng, view, dram_ap, writes=bufs)
        return view, bufs

    def qk_prep(self, ps, psB, n, gcol, rope, t0, out_ap, outB):
        p = self.p
        k = self.rot.get("qkset", 0)
        self.rot["qkset"] = k + 1
        base = (k % 2) * 2560
        r = k % 2
        sq = self.scr_bf(r, n)
        p.op("act", _act(sq, ps, AF.Square), reads=[psB], writes=[self.SCRB[r]])
        _, ss, ssB = self.bank("ss2", [2, 3])
        p.op("pe", _mm(ss[:, :n], self.ONES[:], sq, True, True), reads=[self.SCRB[r], self.CONSTB], writes=[ssB])
        rs = self.actb_f32(base, n)
        rsB = self.actb_bufs(base, n)
        p.op("act", _act(rs, ss[:, :n], AF.Sqrt, scale=1.0 / 128, bias=self.EPSC[:, 0:1]),
             reads=[ssB, self.CONSTB], writes=rsB)
        p.op("dve", lambda e: e.reciprocal(out=rs, in_=rs), reads=rsB, writes=rsB)
        xg = self.actb_f32(base + 512, n)
        xgB = self.actb_bufs(base + 512, n)
        p.op("dve", _stt(xg, ps, gcol, rs, ALU.mult, ALU.mult), reads=[psB, self.CONSTB] + rsB, writes=xgB)
        if rope:
            _, rp, rpB = self.bank("rot", [4, 5])
            p.op("pe", _mm(rp[:, :n], self.ROT[:], xg, True, True), reads=xgB + [self.CONSTB], writes=[rpB])
            t1 = self.actb_f32(base + 1024, n)
            t1B = self.actb_bufs(base + 1024, n)
            t2 = self.actb_f32(base + 1536, n)
            t2B = self.actb_bufs(base + 1536, n)
            p.op("dve", _tt(t1, xg, self.ropeC[:, t0:t0 + n], ALU.mult), reads=xgB + self.ropeCB, writes=t1B)
            p.op("dve", _tt(t2, rp[:, :n], self.ropeS[:, t0:t0 + n], ALU.mult), reads=[rpB] + self.ropeSB, writes=t2B)
            p.op("dve", _tt(out_ap, t1, t2, ALU.add), reads=t1B + t2B, writes=outB)
        else:
            p.op("act", _act(out_ap, xg, AF.Copy), reads=xgB, writes=outB)

    def stage(self, n):
        k = self.rot.get("stage", 0)
        self.rot["stage"] = k + 1
        base = (k % 2) * 2560 + 2048
        v = self.ACTB[:, base:base + 256].bitcast(BF16)[:, 0:n]
        return v, self.actb_bufs(base, 256)

    def attn_pre(self, blks, cb):
        p = self.p
        allb = blks + [cb]
        self.norm_mod(allb, 1, 0)
        wqkv = self.inp("attn_w_qkv", [1, 1024, 2048])
        wv_ = wqkv[0, :, :].rearrange("(kc q) f -> q kc f", q=128)
        self.ropeC, self.ropeCB = self.ring_load_f32(self.inp("ropec", [128, T]), T)
        self.ropeS, self.ropeSB = self.ring_load_f32(self.inp("ropes", [128, T]), T)
        kT_out = self.outp("kT_out", [4, 128, T + NCTX], BF16)
        v_out = self.outp("v_out", [18, 128, 516], BF16)
        qT_out = self.outp("qT_out", [8, 128, T], BF16)
        wk, wkb = self.ring_load(wv_[:, :, 1024:1536], 8, 512)
        for g in range(4):
            for b in allb:
                _, ps, psB = self.bank("qk", [0, 1])
                for kc in range(8):
                    p.op("pe", _mm(ps[:, :b.n], wk[:, kc, g * 128:(g + 1) * 128], b.a(kc), kc == 0, kc == 7),
                         reads=wkb + [b.aB[kc]], writes=[psB])
                st, stB = self.stage(b.n)
                self.qk_prep(ps[:, :b.n], psB, b.n, self.QKG[:, 1:2], b.s == 0, b.tb * 512, st, stB)
                c0 = b.tb * 512 if b.s == 0 else T
                self.out_dma(kT_out[g, :, c0:c0 + b.n], st, stB)
        wvp, wvb = self.ring_load(wv_[:, :, 1536:2048], 8, 512)
        vst = []
        for k in range(2):
            off = 5120 + k * 512
            v3 = self.ACTB[:, off:off + 258].bitcast(BF16).rearrange("p (g d) -> p g d", g=4)
            vb = self.actb_bufs(off, 258)
            p.op("dve", lambda e, v3=v3: e.memset(v3[:, :, 128:129], 1.0), writes=vb)
            vst.append((v3, vb, self.ACTB[:, off:off + 258].bitcast(BF16)))
        for tile in range(18):
            if tile < 16:
                b = blks[tile // 4]
                lo = (tile % 4) * 128
            else:
                b = cb
                lo = (tile - 16) * 128
            _, ps, psB = self.bank("v", [6])
            for kc in range(8):
                p.op("pe", _mm(ps[:, :512], b.a(kc)[:, lo:lo + 128], wvp[:, kc, :], kc == 0, kc == 7),
                     reads=wvb + [b.aB[kc]], writes=[psB])
            v3, vb, vflat = vst[tile % 2]
            p.op("act", _act(v3[:, :, 0:128], ps[:, :512].rearrange("p (g d) -> p g d", g=4), AF.Copy),
                 reads=[psB], writes=vb)
            self.out_dma(v_out[tile, :, :], vflat, vb)
        for half in range(2):
            wq, wqb = self.ring_load(wv_[:, :, half * 512:(half + 1) * 512], 8, 512)
            for hl in range(4):
                h = half * 4 + hl
                for b in blks:
                    _, ps, psB = self.bank("qk", [0, 1])
                    for kc in range(8):
                        p.op("pe", _mm(ps[:, :b.n], wq[:, kc, hl * 128:(hl + 1) * 128], b.a(kc), kc == 0, kc == 7),
                             reads=wqb + [b.aB[kc]], writes=[psB])
                    st, stB = self.stage(b.n)
                    self.qk_prep(ps[:, :b.n], psB, b.n, self.QKG[:, 0:1], True, b.tb * 512, st, stB)
                    self.out_dma(qT_out[h, :, b.tb * 512:b.tb * 512 + b.n], st, stB)

    def build_A(self):
        self.setup()
        self.load_h()
        blks = self.lat_blocks()
        cb = self.ctx_block()
        cin = self.inp("ctxin", [1024, NCTX])
        cv = cin.rearrange("(c p) t -> p c t", p=128)
        for ch in range(8):
            self.p.dma("sp", self.CH3[:, ch, :], cv[:, ch, :], writes=[self.CTXHB[ch]])
        self.ada_all(0)
        self.pool_mix(0, 0, [self.lat_seg(blks), self.ctx_seg(cb)])
        self.mlp(blks + [cb], 0, ada_next=1)
        self.attn_pre(blks, cb)
        self.store_h()
        self.p.emit(final_wait_bufs=self.outbufs)
        self.es.close()
        return self.nc

    def acta_buf(self, region):
        return self.AB[region // 4][region % 4]

    def attn_core(self, blks):
        p = self.p
        qT_in = self.inp("qT_in", [8, 128, T], BF16)
        kT_all = self.inp("kT_all", [4, 128, 8448], BF16)
        v_all = self.inp("v_all", [4, 128, 66 * 129], BF16)
        Q3 = self.ACTB[:, :].bitcast(BF16).rearrange("p (h t) -> p h t", h=8)
        for h in range(8):
            p.dma("sp", Q3[:, h, :], qT_in[h, :, :], writes=self.ACTBB[2 * h:2 * h + 2])
        OT3 = self.A3
        TPb = self.PS[:, 7 * 512:7 * 512 + 256].bitcast(BF16)
        ON = self.scr_bf(2, 512)
        pending = []

        def make_evac(head, qb):
            def ev():
                for qt in range(4):
                    rc = self.SMALL[:, 16 + qt:17 + qt]
                    p.op("dve", lambda e, rc=rc, qt=qt: e.reciprocal(out=rc, in_=self.PS[:, qt * 512 + 128:qt * 512 + 129]),
                         reads=[self.PSB[qt]], writes=[self.SMALLB])
                    p.op("dve", _ts(ON[:, qt * 128:(qt + 1) * 128], self.PS[:, qt * 512:qt * 512 + 128], rc, None, ALU.mult),
                         reads=[self.PSB[qt], self.SMALLB], writes=[self.SCRB[2]])
                for qt in range(4):
                    p.op("pe", lambda e, qt=qt: e.transpose(out=TPb[:, qt * 128:(qt + 1) * 128],
                                                            in_=ON[:, qt * 128:(qt + 1) * 128], identity=self.IDENT[:]),
                         reads=[self.SCRB[2], self.CONSTB], writes=[self.PSB[7]])
                p.op("dve", lambda e: e.tensor_copy(out=OT3[:, head, qb * 512:(qb + 1) * 512], in_=TPb[:, 0:512]),
                     reads=[self.PSB[7]], writes=[self.AB[head][qb]])
            return ev

        for g in range(4):
            sb0 = (g % 2) * 4
            Kv = self.BIG[:, sb0 * SLOT:sb0 * SLOT + 8448]
            KB = self.SLOTB[sb0:sb0 + 2]
            V3 = self.BIG[:, (sb0 + 2) * SLOT:(sb0 + 2) * SLOT + 66 * 129].rearrange("p (k d) -> p k d", k=66)
            VB = self.SLOTB[sb0 + 2:sb0 + 4]
            p.dma("sp", Kv, kT_all[g, :, :], writes=KB)
            p.dma("sp", self.BIG[:, (sb0 + 2) * SLOT:(sb0 + 2) * SLOT + 66 * 129], v_all[g, :, :], writes=VB)
            jobs = [(qb, h2, kt) for qb in range(4) for h2 in range(2) for kt in range(66)]

            def emit_S(job, jidx):
                qb, h2, kt = job
                head = 2 * g + h2
                _, ps, psB = self.bank("st", [4, 5, 6])
                p.op("pe", _mm(ps[:, :512], Kv[:, kt * 128:(kt + 1) * 128], Q3[:, head, qb * 512:(qb + 1) * 512], True, True),
                     reads=KB + self.ACTBB[2 * head:2 * head + 2], writes=[psB])
                return ps, psB

            cur = emit_S(jobs[0], 0)
            for j, job in enumerate(jobs):
                qb, h2, kt = job
                head = 2 * g + h2
                nxt = emit_S(jobs[j + 1], j + 1) if j + 1 < len(jobs) else None
                ps, psB = cur
                r = j % 2
                PT = self.scr_bf(r, 512)
                p.op("act", _act(PT, ps[:, :512], AF.Exp, scale=SM_SCALE, bias=self.EPSC[:, 1:2]),
                     reads=[psB, self.CONSTB], writes=[self.SCRB[r]])
                for qt in range(4):
                    p.op("pe", _mm(self.PS[:, qt * 512:qt * 512 + 129], PT[:, qt * 128:(qt + 1) * 128], V3[:, kt, :],
                                   kt == 0, kt == 65), reads=[self.SCRB[r]] + VB, writes=[self.PSB[qt]])
                if kt == 2 and pending:
                    pending.pop(0)()
                if kt == 65:
                    ev = make_evac(head, qb)
                    pending.append(ev)
                    if True:
                        pending.pop(0)()
                cur = nxt
        wo = self.inp("attn_w_o", [1, 1024, 1024])
        wov = wo[0, :, :].rearrange("(hc q) d -> q hc d", q=128)
        for half in range(2):
            wp, wpb = self.ring_load(wov[:, :, half * 512:(half + 1) * 512], 8, 512)
            for b in blks:
                for dl in range(4):
                    dc = half * 4 + dl
                    _, yp, ypB = self.bank("yp", [3, 4, 5, 6])
                    for hc in range(8):
                        p.op("pe", _mm(yp[:, :b.n], wp[:, hc, dl * 128:(dl + 1) * 128], OT3[:, hc, b.tb * 512:(b.tb + 1) * 512],
                                       hc == 0, hc == 7), reads=wpb + [self.AB[hc][b.tb]], writes=[ypB])
                    p.op("dve", _stt(b.h(dc), yp[:, :b.n], self.modcol(1, 2, dc, 0), b.h(dc), ALU.mult, ALU.add),
                         reads=[ypB, self.MODB[1], b.hB[dc]], writes=[b.hB[dc]])

    def gmlp(self, blks, i):
        p = self.p
        win = self.inp("gm_w_in", [1, 1024, 4096])
        wout = self.inp("gm_w_out", [1, 2048, 1024])
        winv = win[0, :, :].rearrange("(kc q) f -> q kc f", q=128)
        base = 6 * SLOT
        auxB = self.SLOTB[6:8]
        wsT = self.BIG[:, base:base + 1024].rearrange("p (g q) -> p g q", g=8)
        p.dma("pool", self.BIG[:, base:base + 1024], self.inp("gm_wsT", [128, 1024]), writes=auxB)
        BSB = self.BIG[:, base + 1024:base + 3072].bitcast(F32)
        p.dma("sp", BSB, self.inp("gm_bsb", [128, 1024]), writes=auxB)
        LG = self.BIG[:, base + 3072:base + 3136].bitcast(F32)
        p.dma("sp", LG, self.inp("gm_lgb", [128, 32]), writes=auxB)
        R3 = self.ACTB[:, 6144:8192].rearrange("p (c q) -> p c q", c=16)
        RB = self.ACTBB[12:16]
        VT = self.ACTB[:, 4096:6144]
        VTB = self.ACTBB[8:12]
        U3 = self.ACTB[:, 0:4096].bitcast(BF16).rearrange("p (c t) -> p c t", c=16)
        A8 = self.ACTA[:, 0:4096].rearrange("p (c t) -> p c t", c=8)
        G3 = self.ACTA[:, 4096:12288].rearrange("p (c t) -> p c t", c=16)
        VN = self.ACTA[:, 12288:14336]
        VNB = [self.acta_buf(24 + k) for k in range(4)]
        for g in range(8):
            _, ps, psB = self.bank("hp", [0, 1, 2])
            p.op("pe", _mm(ps[:, :128], self.ONES[:], wsT[:, g, :], True, True), reads=auxB + [self.CONSTB], writes=[psB])
            for cc in (2 * g, 2 * g + 1):
                p.op("dve", _stt(R3[:, cc, :], ps[:, :128], LG[:, 16 + cc:17 + cc], BSB[:, g * 128:(g + 1) * 128],
                                 ALU.mult, ALU.add), reads=[psB] + auxB, writes=RB)
        for b in blks:
            rs, rsB = self.rstd_block(b.h, b.hB, b.n)
            for kc in range(8):
                _, tmp, tmpB = self.bank("tmp", [4, 5])
                p.op("dve", _tt(tmp[:, :b.n], b.h(kc), rs, ALU.mult), reads=[b.hB[kc]] + rsB, writes=[tmpB])
                p.op("act", _act(A8[:, kc, :], tmp[:, :b.n], AF.Identity, scale=self.dercol(i, 0, kc, 0),
                                 bias=self.modcol(i, 0, kc, 0)), reads=[tmpB, self.DERB[i], self.MODB[i]],
                     writes=[self.acta_buf(kc)])
            a8B = [self.acta_buf(kc) for kc in range(8)]
            for cg in range(4):
                wp, wpb = self.ring_load(winv[:, :, cg * 512:(cg + 1) * 512], 8, 512)
                for cl in range(4):
                    c = cg * 4 + cl
                    _, ps, psB = self.bank("hp", [0, 1, 2])
                    for kc in range(8):
                        p.op("pe", _mm(ps[:, :512], wp[:, kc, cl * 128:(cl + 1) * 128], A8[:, kc, :], kc == 0, kc == 7),
                             reads=wpb + [a8B[kc]], writes=[psB])
                    p.op("act", _act(U3[:, c, :], ps[:, :512], AF.Gelu_apprx_tanh), reads=[psB], writes=[self.ACTBB[c // 2]])
            pv = [self.ring_load(winv[:, :, 2048 + cb * 512:2048 + (cb + 1) * 512], 8, 512) for cb in range(4)]
            for tt in range(4):
                for cb in range(4):
                    bk = 3 + cb
                    ps, psB = self.PS[:, bk * 512:(bk + 1) * 512], self.PSB[bk]
                    for kc in range(8):
                        p.op("pe", _mm(ps[:, :512], A8[:, kc, tt * 128:(tt + 1) * 128], pv[cb][0][:, kc, :], kc == 0, kc == 7),
                             reads=pv[cb][1] + [a8B[kc]], writes=[psB])
                    p.op("act", _act(VT[:, cb * 512:(cb + 1) * 512], ps[:, :512], AF.Gelu_apprx_tanh,
                                     accum_out=self.SMALL[:, 32 + cb:33 + cb]), reads=[psB], writes=[VTB[cb], self.SMALLB])
                    p.op("act", _act(self.scr_bf(3, 512), VT[:, cb * 512:(cb + 1) * 512], AF.Square,
                                     accum_out=self.SMALL[:, 36 + cb:37 + cb]), reads=[VTB[cb]],
                         writes=[self.SCRB[3], self.SMALLB])
                S = self.SMALL
                sB = [self.SMALLB]
                p.op("dve", lambda e: e.tensor_reduce(out=S[:, 40:41], in_=S[:, 32:36], axis=AX.X, op=ALU.add), reads=sB, writes=sB)
                p.op("dve", lambda e: e.tensor_reduce(out=S[:, 41:42], in_=S[:, 36:40], axis=AX.X, op=ALU.add), reads=sB, writes=sB)
                p.op("dve", _ts(S[:, 42:43], S[:, 40:41], 1.0 / 2048, None, ALU.mult), reads=sB, writes=sB)
                p.op("dve", _ts(S[:, 43:44], S[:, 41:42], 1.0 / 2048, None, ALU.mult), reads=sB, writes=sB)
                p.op("dve", _tt(S[:, 44:45], S[:, 42:43], S[:, 42:43], ALU.mult), reads=sB, writes=sB)
                p.op("dve", _tt(S[:, 45:46], S[:, 43:44], S[:, 44:45], ALU.subtract), reads=sB, writes=sB)
                p.op("act", _act(S[:, 46:47], S[:, 45:46], AF.Sqrt, bias=self.EPSC[:, 0:1]), reads=sB + [self.CONSTB], writes=sB)
                p.op("dve", lambda e: e.reciprocal(out=S[:, 47:48], in_=S[:, 46:47]), reads=sB, writes=sB)
                p.op("dve", _ts(VN, VT, S[:, 42:43], S[:, 47:48], ALU.subtract, ALU.mult), reads=VTB + sB, writes=VNB)
                for c in range(16):
                    mb = 7 if (c // 4) % 2 == 0 else 0
                    M = self.PS[:, mb * 512 + (c % 4) * 128:mb * 512 + (c % 4) * 128 + 128]
                    p.op("pe", _mm(M, VN[:, c * 128:(c + 1) * 128], wsT[:, c // 2, :], True, True),
                         reads=VNB + auxB, writes=[self.PSB[mb]])
                    r2 = c % 2
                    SV = self.scr_f32(r2, 128)
                    svB = self.SCRB[2 * r2:2 * r2 + 1]
                    p.op("dve", _stt(SV, M, LG[:, c:c + 1], R3[:, c, :], ALU.mult, ALU.add),
                         reads=[self.PSB[mb]] + auxB + RB, writes=svB)
                    p.op("dve", _tt(G3[:, c, tt * 128:(tt + 1) * 128], SV, U3[:, c, tt * 128:(tt + 1) * 128], ALU.mult),
                         reads=svB + [self.ACTBB[c // 2]], writes=[self.acta_buf(8 + c)])
            for half in range(2):
                po = [self.ring_load(wout[0, (half * 2 + k) * 512:(half * 2 + k + 1) * 512, :].rearrange("(fc q) d -> q fc d", q=128), 4, 1024)
                      for k in range(2)]
                for dc in range(8):
                    _, yp, ypB = self.bank("yp", [3, 4, 5, 6])
                    for k8 in range(8):
                        c = half * 8 + k8
                        w, wb = po[k8 // 4]
                        p.op("pe", _mm(yp[:, :512], w[:, k8 % 4, dc * 128:(dc + 1) * 128], G3[:, c, :], k8 == 0, k8 == 7),
                             reads=wb + [self.acta_buf(8 + c)], writes=[ypB])
                    p.op("dve", _stt(b.h(dc), yp[:, :512], self.modcol(i, 2, dc, 0), b.h(dc), ALU.mult, ALU.add),
                         reads=[ypB, self.MODB[i], b.hB[dc]], writes=[b.hB[dc]])

    def final_norm(self, blks):
        p = self.p
        outT = self.outp("outT", [1024, T])
        ov = outT.rearrange("(c q) t -> q c t", q=128)
        k = 0
        for b in blks:
            rs, rsB = self.rstd_block(b.h, b.hB, b.n)
            for ch in range(8):
                st = self.ACTB[:, (k % 16) * 512:(k % 16) * 512 + 512]
                stB = [self.ACTBB[k % 16]]
                k += 1
                p.op("dve", _stt(st, b.h(ch), self.FG[:, ch:ch + 1], rs, ALU.mult, ALU.mult),
                     reads=[b.hB[ch], self.CONSTB] + rsB, writes=stB)
                self.out_dma(ov[:, ch, b.tb * 512:(b.tb + 1) * 512], st, stB)

    def build_B(self):
        self.setup()
        self.load_h()
        blks = self.lat_blocks()
        self.ada_all(1)
        self.attn_core(blks)
        self.mlp(blks, 1, ada_next=2)
        self.gmlp(blks, 2)
        self.mlp(blks, 2)
        self.store_h()
        self.p.emit(final_wait_bufs=self.outbufs)
        self.es.close()
        return self.nc

    def build_C(self):
        self.setup()
        self.load_h()
        blks = self.lat_blocks()
        self.ada_all(3)
        self.pool_mix(3, 1, [self.lat_seg(blks)])
        self.mlp(blks, 3)
        self.final_norm(blks)
        self.p.emit(final_wait_bufs=self.outbufs)
        self.es.close()
        return self.nc


SEQ = 8192
_PROGS = {}


def _fm(v):
    v = np.asarray(v, np.float32)
    lead = v.shape[:-1]
    return np.moveaxis(v.reshape(*lead, 8, 128), -1, 0)


def _consts_for_core(c, inputs):
    b, r = c // 4, c % 4
    q0 = r * T
    f32 = np.float32
    d = {}
    cv = np.stack([_fm(inputs["c"][b]), _fm(inputs["c_ctx"])], axis=-1)
    d["cvec"] = np.ascontiguousarray(cv.reshape(128, 16), f32)
    ab = _fm(np.asarray(inputs["ada_b"]).reshape(4, 6, 1024))
    d["adab"] = np.ascontiguousarray(np.repeat(ab[..., None], 2, -1).reshape(128, 4 * 96), f32)
    ng = _fm(np.asarray(inputs["norm_g"]))
    d["ng2"] = np.ascontiguousarray(np.repeat(ng[..., None], 2, -1).reshape(128, 128), f32)
    ps = _fm(np.asarray(inputs["pool_scale"]))
    d["ps2"] = np.ascontiguousarray(np.repeat(ps[..., None], 2, -1).reshape(128, 32), f32)
    d["fg"] = np.ascontiguousarray(_fm(np.asarray(inputs["final_g"])).reshape(128, 8), f32)
    d["qkg"] = np.ascontiguousarray(np.stack([np.asarray(inputs["attn_q_g"])[0], np.asarray(inputs["attn_k_g"])[0]], 1), f32)
    d["ident"] = np.eye(128, dtype=f32)
    rot = np.zeros((128, 128), f32)
    for base in (0, 64):
        for m in range(32):
            rot[base + m + 32, base + m] = -1.0
            rot[base + m, base + m + 32] = 1.0
    d["rot"] = rot
    hm = np.zeros((128, 32), f32)
    hm[:, 0:8] = 1.0 if q0 > 0 else 0.0
    hm[:, 8:16] = 1.0 if q0 + T < SEQ else 0.0
    d["hm"] = hm
    edge = np.zeros((128, 128), f32)
    for g, w in enumerate((2, 4, 8, 16)):
        for e in range(16):
            pos = q0 + (e if e < 8 else T - 16 + e)
            lo, hi = max(pos - w // 2, 0), min(pos + w - w // 2, SEQ)
            edge[:, g * 16 + e] = 1.0 / (hi - lo)
            posc = e if e < 8 else NCTX - 16 + e
            lo, hi = max(posc - w // 2, 0), min(posc + w - w // 2, NCTX)
            edge[:, 64 + g * 16 + e] = 1.0 / (hi - lo)
    d["edge"] = edge
    return d


def _rope_tables(q0):
    t = np.arange(q0, q0 + T)
    row = (t // 64).astype(np.float32)
    col = (t % 64).astype(np.float32)
    inv = (np.float32(10000.0) ** (-np.arange(0, 64, 2, dtype=np.float32) / np.float32(64))).astype(np.float32)
    ang_r = (row[:, None] * inv[None, :]).astype(np.float32)
    ang_c = (col[:, None] * inv[None, :]).astype(np.float32)
    C = np.concatenate([np.cos(ang_r), np.cos(ang_r), np.cos(ang_c), np.cos(ang_c)], 1).T
    S = np.concatenate([np.sin(ang_r), np.sin(ang_r), np.sin(ang_c), np.sin(ang_c)], 1).T
    return np.ascontiguousarray(C, np.float32), np.ascontiguousarray(S, np.float32)


def _hin_from_rows(rows_b, r):
    q0 = r * T
    out = np.zeros((TC, 1024), np.float32)
    lo, hi = max(q0 - HAL, 0), min(q0 + T + HAL, SEQ)
    out[lo - (q0 - HAL):hi - (q0 - HAL)] = rows_b[lo:hi]
    return np.ascontiguousarray(out.T)


def _layer_weights(inputs, seg):
    al, ml = list(MK.ADA_LAYERS[seg]), list(MK.MLP_LAYERS[seg])
    return {"ada_w": np.ascontiguousarray(np.asarray(inputs["ada_w"], np.float32)[al]),
            "mlp_w1": np.ascontiguousarray(np.asarray(inputs["mlp_w1"], np.float32)[ml]),
            "mlp_w2": np.ascontiguousarray(np.asarray(inputs["mlp_w2"], np.float32)[ml])}


def _weights(inputs, names):
    return {k: np.asarray(inputs[k], np.float32) for k in names}


def _get_prog(seg):
    if seg not in _PROGS:
        mk = MK(seg)
        nc = getattr(mk, "build_" + seg)()
        _PROGS[seg] = (nc, list(mk.din.keys()))
    return _PROGS[seg]


def _launch(seg, per_core):
    nc, names = _get_prog(seg)
    in_maps = [{k: m[k] for k in names} for m in per_core]
    res = run_bass_kernel_spmd(nc, in_maps, core_ids=list(range(8)))
    return res.results


def run_seg_A(inputs):
    x = np.asarray(inputs["x"], np.float32)
    ctx = np.asarray(inputs["ctx"], np.float32)
    w = _weights(inputs, ("pool_w", "attn_w_qkv"))
    w.update(_layer_weights(inputs, "A"))
    per_core = []
    for c in range(8):
        b, r = c // 4, c % 4
        m = _consts_for_core(c, inputs)
        m["hin"] = _hin_from_rows(x[b], r)
        m["ctxin"] = np.ascontiguousarray(ctx[b].T)
        m["ropec"], m["ropes"] = _rope_tables(r * T)
        m.update(w)
        per_core.append(m)
    return _launch("A", per_core)


def run_seg_B(inputs, resA):
    import ml_dtypes
    bf = ml_dtypes.bfloat16
    w = _weights(inputs, ("attn_w_o", "gm_w_in", "gm_w_out"))
    w.update(_layer_weights(inputs, "B"))
    ws = np.asarray(inputs["gm_ws"], np.float32)[0]
    wsT = np.ascontiguousarray(ws.transpose(2, 0, 1).reshape(128, 1024))
    bs = np.asarray(inputs["gm_bs"], np.float32)[0]
    bsb = np.ascontiguousarray(np.broadcast_to(bs.reshape(1, 1024), (128, 1024)), np.float32)
    lgb = np.concatenate([_fm(np.asarray(inputs["gm_ln_g"])[0].reshape(2, 1024)).reshape(128, 16),
                          _fm(np.asarray(inputs["gm_ln_b"])[0].reshape(2, 1024)).reshape(128, 16)], 1)
    lgb = np.ascontiguousarray(lgb, np.float32)
    per_core = []
    kv_cache = {}
    for b in range(2):
        kT = np.zeros((4, 128, 8448), bf)
        va = np.zeros((4, 128, 66, 129), bf)
        for r in range(4):
            ra = resA[b * 4 + r]
            k = np.asarray(ra["kT_out"])
            v = np.asarray(ra["v_out"]).reshape(18, 128, 4, 129)
            kT[:, :, r * T:(r + 1) * T] = k[:, :, :T]
            va[:, :, r * 16:(r + 1) * 16, :] = v[:16].transpose(2, 1, 0, 3)
            if r == 0:
                kT[:, :, 4 * T:] = k[:, :, T:]
                va[:, :, 64:66, :] = v[16:].transpose(2, 1, 0, 3)
        kv_cache[b] = (kT, np.ascontiguousarray(va.reshape(4, 128, 66 * 129)))
    for c in range(8):
        b, r = c // 4, c % 4
        m = _consts_for_core(c, inputs)
        hin = np.zeros((1024, TC), np.float32)
        hin[:, HAL:HAL + T] = np.asarray(resA[c]["hout"])
        m["hin"] = hin
        m["qT_in"] = np.asarray(resA[c]["qT_out"])
        m["kT_all"], m["v_all"] = kv_cache[b]
        m["gm_wsT"], m["gm_bsb"], m["gm_lgb"] = wsT, bsb, lgb
        m.update(w)
        per_core.append(m)
    return _launch("B", per_core)


def run_seg_C(inputs, resB):
    w = _weights(inputs, ("pool_w",))
    w.update(_layer_weights(inputs, "C"))
    per_core = []
    for b in range(2):
        rows = np.concatenate([np.asarray(resB[b * 4 + r]["hout"]).T for r in range(4)], 0)
        for r in range(4):
            m = _consts_for_core(b * 4 + r, inputs)
            m["hin"] = _hin_from_rows(rows, r)
            m.update(w)
            per_core.append(m)
    return _launch("C", per_core)


def kernel(**inputs):
    resA = run_seg_A(inputs)
    resB = run_seg_B(inputs, resA)
    resC = run_seg_C(inputs, resB)
    out = np.zeros((2, SEQ, 1024), np.float32)
    for c in range(8):
        b, r = c // 4, c % 4
        out[b, r * T:(r + 1) * T, :] = np.asarray(resC[c]["outT"]).T
    return out
```

```python
import numpy as np
import concourse.bass as bass
import concourse.mybir as mybir
from concourse.bass_utils import run_bass_kernel_spmd

F32 = mybir.dt.float32
BF16 = mybir.dt.bfloat16
AF = mybir.ActivationFunctionType
ALU = mybir.AluOpType
AX = mybir.AxisListType


class Buf:
    __slots__ = ("name", "last_write", "readers")

    def __init__(self, name=""):
        self.name = name
        self.last_write = None
        self.readers = []


class _Ins:
    __slots__ = ("eng", "fn", "deps", "idx", "signal", "dma_tok", "dma_prev")

    def __init__(self, eng, fn, deps, idx):
        self.eng = eng
        self.fn = fn
        self.deps = deps
        self.idx = idx
        self.signal = False
        self.dma_tok = None
        self.dma_prev = None


ENGS = ("pe", "act", "dve", "pool", "sp")
SEM_LIMIT = 12000
N_DMA_SEMS = 12


class Prog:
    def __init__(self, nc):
        self.nc = nc
        self.streams = {e: [] for e in ENGS}
        self.dma_count = {e: 0 for e in ENGS}

    def _collect(self, reads, writes):
        deps = []
        for b in reads:
            if b.last_write is not None:
                deps.append(b.last_write)
        for b in writes:
            if b.last_write is not None:
                deps.append(b.last_write)
            deps.extend(b.readers)
        return deps

    def _commit(self, tok, reads, writes):
        for b in reads:
            b.readers.append(tok)
        for b in writes:
            b.last_write = tok
            b.readers = []

    def op(self, eng, fn, reads=(), writes=()):
        deps = self._collect(reads, writes)
        st = self.streams[eng]
        ins = _Ins(eng, fn, deps, len(st))
        st.append(ins)
        self._commit(("c", eng, ins.idx), reads, writes)
        return ins

    def coll(self, kind, groups, src, dst, reads=(), writes=()):
        return self.dma("pool", None, None, reads, writes,
                        fn=lambda e: e.collective_compute(kind, ALU.bypass, replica_groups=groups,
                                                          ins=[src], outs=[dst]))

    def dma(self, eng, out, in_, reads=(), writes=(), fn=None):
        deps = self._collect(reads, writes)
        st = self.streams[eng]
        k = self.dma_count[eng]
        self.dma_count[eng] = k + 1
        slot, rnd = k % N_DMA_SEMS, k // N_DMA_SEMS
        if fn is None:
            fn = lambda e: e.dma_start(out=out, in_=in_)
        ins = _Ins(eng, fn, deps, len(st))
        ins.dma_tok = ("d", eng, slot, 16 * (rnd + 1))
        if rnd > 0:
            ins.dma_prev = ("d", eng, slot, 16 * rnd)
        st.append(ins)
        self._commit(ins.dma_tok, reads, writes)
        return ins

    def emit(self, final_wait_bufs=()):
        nc = self.nc
        final_deps = []
        for b in final_wait_bufs:
            if b.last_write is not None:
                final_deps.append(b.last_write)
        for e in ENGS:
            for ins in self.streams[e]:
                best = {}
                kept = []
                for d in ins.deps:
                    if d[0] == "c":
                        if e == "pe" and d[1] == "pe":
                            continue
                        if d[1] not in best or best[d[1]][2] < d[2]:
                            best[d[1]] = d
                    else:
                        kept.append(d)
                ins.deps = kept + list(best.values())
                for d in ins.deps:
                    if d[0] == "c":
                        self.streams[d[1]][d[2]].signal = True
        for d in final_deps:
            if d[0] == "c":
                self.streams[d[1]][d[2]].signal = True
        count_of = {}
        n_epochs = {}
        for e in ENGS:
            c = 0
            for ins in self.streams[e]:
                if ins.signal:
                    c += 1
                    count_of[(e, ins.idx)] = c
            n_epochs[e] = max(1, (c + SEM_LIMIT - 1) // SEM_LIMIT)
        import contextlib
        with contextlib.ExitStack() as es:
            csem = {e: [es.enter_context(nc.semaphore(f"c_{e}_{i}")) for i in range(n_epochs[e])]
                    for e in ENGS}
            dsem = {e: [es.enter_context(nc.semaphore(f"d_{e}_{i}")) for i in range(N_DMA_SEMS)]
                    for e in ENGS if self.dma_count[e] > 0}
            block = es.enter_context(nc.Block())

            def resolve(tok):
                if tok[0] == "c":
                    c = count_of[(tok[1], tok[2])]
                    ep = (c - 1) // SEM_LIMIT
                    return (csem[tok[1]][ep], ("c", tok[1], ep), c - ep * SEM_LIMIT)
                return (dsem[tok[1]][tok[2]], ("d", tok[1], tok[2]), tok[3])

            def run(e, eng):
                known = {}
                for ins in self.streams[e]:
                    toks = list(ins.deps)
                    if ins.dma_prev is not None:
                        toks.append(ins.dma_prev)
                    need = {}
                    for t in toks:
                        sem, key, val = resolve(t)
                        if known.get(key, 0) >= val:
                            continue
                        if key not in need or need[key][1] < val:
                            need[key] = (sem, val)
                    for key, (sem, val) in need.items():
                        eng.wait_ge(sem, val)
                        known[key] = val
                    bi = ins.fn(eng)
                    if ins.dma_tok is not None:
                        bi.then_inc(dsem[e][ins.dma_tok[2]], 16)
                    elif ins.signal:
                        c = count_of[(e, ins.idx)]
                        ep = (c - 1) // SEM_LIMIT
                        bi.then_inc(csem[e][ep], 1)
                if e == "sp":
                    for t in final_deps:
                        sem, key, val = resolve(t)
                        if known.get(key, 0) >= val:
                            continue
                        eng.wait_ge(sem, val)
                        known[key] = max(known.get(key, 0), val)

            @block.tensor
            def _(eng):
                run("pe", eng)

            @block.scalar
            def _(eng):
                run("act", eng)

            @block.vector
            def _(eng):
                run("dve", eng)

            @block.gpsimd
            def _(eng):
                run("pool", eng)

            @block.sync
            def _(eng):
                run("sp", eng)


T = 2048
HAL = 8
TC = T + 2 * HAL
NCTX = 256
SLOT = 4288
NSLOT = 8
RING_SLOTS = 6
EPS = 1e-6
SM_SCALE = 128 ** -0.5
SM_SHIFT = -(128 ** 0.5)


def _mm(out, lhsT, rhs, start, stop):
    return lambda e: e.matmul(out, lhsT=lhsT, rhs=rhs, start=start, stop=stop)


def _act(out, in_, func, **kw):
    return lambda e: e.activation(out=out, in_=in_, func=func, **kw)


def _tt(out, in0, in1, op):
    return lambda e: e.tensor_tensor(out=out, in0=in0, in1=in1, op=op)


def _stt(out, in0, scalar, in1, op0, op1):
    return lambda e: e.scalar_tensor_tensor(out=out, in0=in0, scalar=scalar, in1=in1, op0=op0, op1=op1)


def _ts(out, in0, s1, s2, op0, op1=None):
    if op1 is None:
        return lambda e: e.tensor_scalar(out=out, in0=in0, scalar1=s1, scalar2=None, op0=op0)
    return lambda e: e.tensor_scalar(out=out, in0=in0, scalar1=s1, scalar2=s2, op0=op0, op1=op1)


class Blk:
    pass


class MK:
    ADA_LAYERS = {"A": (0, 1), "B": (1, 2), "C": (3,)}
    MLP_LAYERS = {"A": (0,), "B": (1, 2), "C": (3,)}

    def __init__(self, seg):
        import contextlib
        self.seg = seg
        self.ada_l = self.ADA_LAYERS[seg]
        self.mlp_l = self.MLP_LAYERS[seg]
        self.nc = nc = bass.Bass("TRN2", target_bir_lowering=False)
        self.es = es = contextlib.ExitStack()
        self.p = Prog(nc)
        self.din = {}
        self.dout = {}

        def sb(name, shape, dt):
            return es.enter_context(nc.sbuf_tensor(name, shape, dt))

        self.HT = sb("HT", [128, 8 * TC], F32)
        self.HT3 = self.HT[:, :].rearrange("p (c t) -> p c t", c=8)
        self.ACTA = sb("ACTA", [128, 8 * T], BF16)
        self.A3 = self.ACTA[:, :].rearrange("p (c t) -> p c t", c=8)
        self.ACTB = sb("ACTB", [128, 8192], F32)
        self.BIG = sb("BIG", [128, NSLOT * SLOT], BF16)
        self.MOD = sb("MOD", [128, 4 * 96], F32)
        self.ADAB = sb("ADAB", [128, 96], F32)
        self.NG2 = sb("NG2", [128, 128], F32)
        self.PS2 = sb("PS2", [128, 32], F32)
        self.FG = sb("FG", [128, 8], F32)
        self.CV = sb("CV", [128, 16], F32)
        self.SBF = sb("SBF", [128, 16], BF16)
        self.ONES = sb("ONES", [128, 128], BF16)
        self.IDENT = sb("IDENT", [128, 128], BF16)
        self.ROT = sb("ROT", [128, 128], F32)
        self.EPSC = sb("EPSC", [128, 2], F32)
        self.DER = sb("DER", [128, 4 * 48], F32)
        self.HM = sb("HM", [128, 32], F32)
        self.EDGE = sb("EDGE", [128, 128], F32)
        self.QKG = sb("QKG", [128, 2], F32)
        self.SCR = sb("SCR", [128, 1024], F32)
        self.SMALL = sb("SMALL", [128, 64], F32)
        self.PS = es.enter_context(nc.psum_tensor("PS", [128, 8 * 512], F32))

        self.HTB = [[Buf(f"h{c}_{t}") for t in range(4)] for c in range(8)]
        self.HALB = [Buf(f"hal{c}") for c in range(8)]
        self.AB = [[Buf(f"a{c}_{t}") for t in range(4)] for c in range(8)]
        self.ACTBB = [Buf(f"actb{k}") for k in range(16)]
        self.SLOTB = [Buf(f"slot{k}") for k in range(NSLOT)]
        self.PSB = [Buf(f"ps{k}") for k in range(8)]
        self.SCRB = [Buf(f"scr{k}") for k in range(4)]
        self.MODB = [Buf(f"mod{i}") for i in range(4)]
        self.DERB = [Buf(f"der{i}") for i in range(4)]
        self.CONSTB = Buf("const")
        self.SMALLB = Buf("small")
        self.ADABB = Buf("adab")
        self.ring_pos = 0
        self.rot = {}
        self.outbufs = []

    def inp(self, name, shape, dt=F32):
        t = self.nc.dram_tensor(name, list(shape), dt, kind="ExternalInput").ap()
        self.din[name] = t
        return t

    def outp(self, name, shape, dt=F32):
        t = self.nc.dram_tensor(name, list(shape), dt, kind="ExternalOutput").ap()
        self.dout[name] = t
        return t

    def bank(self, role, banks):
        k = self.rot.get(role, 0)
        self.rot[role] = k + 1
        b = banks[k % len(banks)]
        return b, self.PS[:, b * 512:(b + 1) * 512], self.PSB[b]

    def scr_bf(self, r, n=512):
        return self.SCR[:, r * 256:(r + 1) * 256].bitcast(BF16)[:, 0:n]

    def scr_f32(self, r2, n=512):
        return self.SCR[:, r2 * 512:r2 * 512 + n]

    def actb_f32(self, off, n):
        return self.ACTB[:, off:off + n]

    def actb_bufs(self, off, n):
        return self.ACTBB[off // 512:(off + n + 511) // 512]

    def ring_load(self, dram_ap, a, b, eng="pool"):
        elems = a * b
        n = (elems + SLOT - 1) // SLOT
        if self.ring_pos + n > RING_SLOTS:
            self.ring_pos = 0
        pos = self.ring_pos
        self.ring_pos = (pos + n) % RING_SLOTS
        view = self.BIG[:, pos * SLOT:pos * SLOT + elems].rearrange("p (a b) -> p a b", a=a)
        bufs = self.SLOTB[pos:pos + n]
        self.p.dma(eng, view, dram_ap, writes=bufs)
        return view, bufs

    def out_dma(self, dst, src, reads):
        b = Buf("out")
        self.outbufs.append(b)
        self.p.dma("sp", dst, src, reads=reads, writes=[b])

    def col(self, t, idx):
        return t[:, idx:idx + 1]

    def modcol(self, i, part, ch, s):
        return self.col(self.MOD, i * 96 + (part * 8 + ch) * 2 + s)

    def dercol(self, i, kind, ch, s):
        return self.col(self.DER, i * 48 + (kind * 8 + ch) * 2 + s)

    def setup(self):
        p = self.p
        c = self.CONSTB
        p.op("dve", lambda e: e.memset(self.ONES[:], 1.0), writes=[c])
        p.op("dve", lambda e: e.memset(self.EPSC[:, 0:1], EPS), writes=[c])
        p.op("dve", lambda e: e.memset(self.EPSC[:, 1:2], SM_SHIFT), writes=[c])
        small = [("ng2", self.NG2, 128), ("ps2", self.PS2, 32), ("fg", self.FG, 8), ("cvec", self.CV, 16),
                 ("rot", self.ROT, 128), ("hm", self.HM, 32), ("edge", self.EDGE, 128), ("qkg", self.QKG, 2)]
        for name, t, n in small:
            d = self.inp(name, [128, n])
            p.dma("sp", t[:], d, writes=[c])
        d = self.inp("ident", [128, 128])
        p.dma("pool", self.IDENT[:], d, writes=[c])
        p.op("act", _act(self.SBF[:], self.CV[:], AF.Silu), reads=[c], writes=[c])
        self.ada_w = self.inp("ada_w", [len(self.ada_l), 1024, 6144])
        self.adab_d = self.inp("adab", [128, 4 * 96])
        self.w1 = self.inp("mlp_w1", [len(self.mlp_l), 1024, 4096])
        self.w2 = self.inp("mlp_w2", [len(self.mlp_l), 4096, 1024])

    def load_h(self):
        hin = self.inp("hin", [1024, TC])
        v = hin.rearrange("(c p) t -> p c t", p=128)
        for ch in range(8):
            self.p.dma("sp", self.HT3[:, ch, :], v[:, ch, :], writes=self.HTB[ch] + [self.HALB[ch]])

    def store_h(self):
        hout = self.outp("hout", [1024, T])
        v = hout.rearrange("(c p) t -> p c t", p=128)
        for ch in range(8):
            self.out_dma(v[:, ch, :], self.HT3[:, ch, HAL:HAL + T], self.HTB[ch])

    def lat_blocks(self):
        blks = []
        for tb in range(4):
            b = Blk()
            b.n, b.s, b.tb = 512, 0, tb
            b.h = (lambda ch, tb=tb: self.HT3[:, ch, HAL + tb * 512:HAL + (tb + 1) * 512])
            b.hB = [self.HTB[ch][tb] for ch in range(8)]
            b.a = (lambda ch, tb=tb: self.A3[:, ch, tb * 512:(tb + 1) * 512])
            b.aB = [self.AB[ch][tb] for ch in range(8)]
            hid3 = self.ACTB[:, 0:4096].bitcast(BF16).rearrange("p (c t) -> p c t", c=4)
            b.hid = (lambda fc, tb=tb, hid3=hid3: hid3[:, fc, tb * 512:(tb + 1) * 512])
            b.hidB = [self.ACTBB[fc * 2 + tb // 2] for fc in range(4)]
            blks.append(b)
        return blks

    def ctx_block(self):
        b = Blk()
        b.n, b.s, b.tb = NCTX, 1, 0
        base = 6 * SLOT
        ch3 = self.BIG[:, base:base + 4096].bitcast(F32).rearrange("p (c t) -> p c t", c=8)
        ca3 = self.BIG[:, base + 4096:base + 6144].rearrange("p (c t) -> p c t", c=8)
        chid = self.BIG[:, base + 6144:base + 7168].rearrange("p (c t) -> p c t", c=4)
        self.CH3 = ch3
        b.h = lambda ch: ch3[:, ch, :]
        self.CTXHB = [Buf(f"ctxh{c}") for c in range(8)]
        b.hB = self.CTXHB
        b.a = lambda ch: ca3[:, ch, :]
        b.aB = [Buf(f"ctxa{c}") for c in range(8)]
        b.hid = lambda fc: chid[:, fc, :]
        b.hidB = [Buf(f"ctxhid{c}") for c in range(4)]
        return b

    def ada_piece(self, i, k):
        p = self.p
        v = self.ada_w[self.ada_l.index(i), :, :].rearrange("(kc q) f -> q kc f", q=128)[:, :, k * 512:(k + 1) * 512]
        w, wb = self.ring_load(v, 8, 512)
        s3 = self.SBF[:, :].rearrange("p (kc s) -> p kc s", s=2)
        for j in range(4):
            idx = k * 4 + j
            for kc in range(8):
                p.op("pe", _mm(self.PS[:, 7 * 512 + 2 * idx:7 * 512 + 2 * idx + 2], w[:, kc, j * 128:(j + 1) * 128],
                               s3[:, kc, :], kc == 0, kc == 7), reads=wb + [self.CONSTB], writes=[self.PSB[7]])

    def ada_finish(self, i):
        p = self.p
        p.dma("sp", self.ADAB[:], self.adab_d[:, i * 96:(i + 1) * 96], writes=[self.ADABB])
        p.op("dve", _tt(self.MOD[:, i * 96:(i + 1) * 96], self.PS[:, 7 * 512:7 * 512 + 96], self.ADAB[:], ALU.add),
             reads=[self.PSB[7], self.ADABB], writes=[self.MODB[i]])
        for kind, part, which in ((0, 1, 0), (1, 4, 1)):
            src = self.MOD[:, i * 96 + part * 16:i * 96 + part * 16 + 16]
            dst = self.DER[:, i * 48 + kind * 16:i * 48 + kind * 16 + 16]
            ng = self.NG2[:, (i * 2 + which) * 16:(i * 2 + which) * 16 + 16]
            p.op("dve", _stt(dst, src, 1.0, ng, ALU.add, ALU.mult), reads=[self.MODB[i], self.CONSTB],
                 writes=[self.DERB[i]])
        if i % 3 == 0:
            j = i // 3
            src = self.MOD[:, i * 96 + 2 * 16:i * 96 + 3 * 16]
            dst = self.DER[:, i * 48 + 32:i * 48 + 48]
            p.op("dve", _tt(dst, src, self.PS2[:, j * 16:(j + 1) * 16], ALU.mult), reads=[self.MODB[i], self.CONSTB],
                 writes=[self.DERB[i]])

    def ada_all(self, i):
        for k in range(12):
            self.ada_piece(i, k)
        self.ada_finish(i)

    def rstd_block(self, hfn, hbufs, n, dst=None, dstB=None):
        p = self.p
        _, ss, ssB = self.bank("ss", [0, 1])
        for ch in range(8):
            r = ch % 2
            sq = self.scr_bf(r, n)
            hb = hbufs[ch] if isinstance(hbufs[ch], list) else [hbufs[ch]]
            p.op("act", _act(sq, hfn(ch), AF.Square), reads=hb, writes=[self.SCRB[r]])
            p.op("pe", _mm(ss[:, :n], self.ONES[:], sq, ch == 0, ch == 7), reads=[self.SCRB[r], self.CONSTB],
                 writes=[ssB])
        if dst is None:
            _, rs, rsB = self.bank("rs", [2, 3])
            dst, dstB = rs[:, :n], [rsB]
        p.op("act", _act(dst, ss[:, :n], AF.Sqrt, scale=1.0 / 1024, bias=self.EPSC[:, 0:1]),
             reads=[ssB, self.CONSTB], writes=dstB)
        p.op("dve", lambda e, dst=dst: e.reciprocal(out=dst, in_=dst), reads=dstB, writes=dstB)
        return dst, dstB

    def norm_mod(self, blks, i, which):
        p = self.p
        part_sh = 0 if which == 0 else 3
        for b in blks:
            rs, rsB = self.rstd_block(b.h, b.hB, b.n)
            for ch in range(8):
                _, tmp, tmpB = self.bank("tmp", [4, 5])
                p.op("dve", _tt(tmp[:, :b.n], b.h(ch), rs, ALU.mult), reads=[b.hB[ch]] + rsB, writes=[tmpB])
                p.op("act", _act(b.a(ch), tmp[:, :b.n], AF.Identity, scale=self.dercol(i, which, ch, b.s),
                                 bias=self.modcol(i, part_sh, ch, b.s)),
                     reads=[tmpB, self.DERB[i], self.MODB[i]], writes=[b.aB[ch]])

    def mlp(self, blks, i, ada_next=None):
        p = self.p
        self.norm_mod(blks, i, 1)
        ada_k = 0
        for fb in range(8):
            li = self.mlp_l.index(i)
            v1 = self.w1[li, :, :].rearrange("(kc q) f -> q kc f", q=128)[:, :, fb * 512:(fb + 1) * 512]
            w1p, w1b = self.ring_load(v1, 8, 512)
            v2 = self.w2[li, fb * 512:(fb + 1) * 512, :].rearrange("(fc q) d -> q fc d", q=128)
            w2p, w2b = self.ring_load(v2, 4, 1024)
            for b in blks:
                for fc in range(4):
                    _, hp, hpB = self.bank("hp", [0, 1, 2])
                    for kc in range(8):
                        p.op("pe", _mm(hp[:, :b.n], w1p[:, kc, fc * 128:(fc + 1) * 128], b.a(kc), kc == 0, kc == 7),
                             reads=w1b + [b.aB[kc]], writes=[hpB])
                    r2 = self.rot.get("relu", 0) % 2
                    self.rot["relu"] = r2 + 1
                    rr = self.scr_f32(r2, b.n)
                    rrB = self.SCRB[2 * r2:2 * r2 + 2]
                    p.op("act", _act(rr, hp[:, :b.n], AF.Relu), reads=[hpB], writes=rrB)
                    p.op("dve", _tt(b.hid(fc), hp[:, :b.n], rr, ALU.mult), reads=[hpB] + rrB, writes=[b.hidB[fc]])
            for b in blks:
                for dc in range(8):
                    _, yp, ypB = self.bank("yp", [3, 4, 5, 6])
                    for fc in range(4):
                        p.op("pe", _mm(yp[:, :b.n], w2p[:, fc, dc * 128:(dc + 1) * 128], b.hid(fc), fc == 0, fc == 3),
                             reads=w2b + [b.hidB[fc]], writes=[ypB])
                    p.op("dve", _stt(b.h(dc), yp[:, :b.n], self.modcol(i, 5, dc, b.s), b.h(dc), ALU.mult, ALU.add),
                         reads=[ypB, self.MODB[i], b.hB[dc]], writes=[b.hB[dc]])
            if ada_next is not None:
                for _ in range(2):
                    if ada_k < 12:
                        self.ada_piece(ada_next, ada_k)
                        ada_k += 1
        if ada_next is not None:
            self.ada_finish(ada_next)

    def pool_mix(self, i, j, segs):
        p = self.p
        pw = self.inp_once("pool_w", [2, 4, 256, 256])
        for sg in segs:
            n, s = sg["n"], sg["s"]
            ncol = n + 2 * HAL
            A = self.actb_f32(0, ncol)
            AB_ = self.actb_bufs(0, ncol)
            SA = self.actb_f32(2560, ncol)
            SAB = self.actb_bufs(2560, ncol)
            RS = self.actb_f32(5120, ncol)
            RSB = self.actb_bufs(5120, ncol)
            c0 = 0
            for (o, m) in sg["stat_cols"]:
                self.rstd_block(lambda ch, o=o, m=m: sg["hcols"](ch, o, m), sg["hbufs_all"], m,
                                dst=RS[:, sg["rs_off"] + o:sg["rs_off"] + o + m], dstB=RSB)
            for ch in range(8):
                g = ch // 2
                w = 2 << g
                hb = sg["hbufs_ch"](ch)
                if sg["halo"]:
                    p.op("dve", _tt(A[:, 0:ncol], sg["hcols"](ch, 0, ncol), RS[:, 0:ncol], ALU.mult),
                         reads=hb + RSB, writes=AB_)
                    p.op("act", _act(A[:, 0:ncol], A[:, 0:ncol], AF.Identity, scale=self.dercol(i, 0, ch, s),
                                     bias=self.modcol(i, 0, ch, s)), reads=AB_ + [self.DERB[i], self.MODB[i]],
                         writes=AB_)
                    p.op("dve", _tt(A[:, 0:HAL], A[:, 0:HAL], self.HM[:, 0:HAL], ALU.mult), reads=AB_ + [self.CONSTB],
                         writes=AB_)
                    p.op("dve", _tt(A[:, HAL + n:ncol], A[:, HAL + n:ncol], self.HM[:, HAL:2 * HAL], ALU.mult),
                         reads=AB_ + [self.CONSTB], writes=AB_)
                else:
                    p.op("dve", lambda e, A=A, ncol=ncol: e.memset(A[:, 0:ncol], 0.0), writes=AB_)
                    p.op("dve", _tt(A[:, HAL:HAL + n], sg["hcols"](ch, 0, n), RS[:, HAL:HAL + n], ALU.mult),
                         reads=hb + RSB, writes=AB_)
                    p.op("act", _act(A[:, HAL:HAL + n], A[:, HAL:HAL + n], AF.Identity,
                                     scale=self.dercol(i, 0, ch, s), bias=self.modcol(i, 0, ch, s)),
                         reads=AB_ + [self.DERB[i], self.MODB[i]], writes=AB_)
                m = ncol - 1
                p.op("dve", _tt(SA[:, 0:m], A[:, 0:m], A[:, 1:m + 1], ALU.add), reads=AB_, writes=SAB)
                sh = 2
                while sh < w:
                    m2 = m - sh
                    p.op("dve", _tt(SA[:, 0:m2], SA[:, 0:m2], SA[:, sh:sh + m2], ALU.add), reads=SAB, writes=SAB)
                    m = m2
                    sh *= 2
                o = HAL - w // 2
                for b in sg["blks"]:
                    t0 = b.tb * 512
                    p.op("dve", _stt(b.a(ch), SA[:, o + t0:o + t0 + b.n], 1.0 / w, A[:, HAL + t0:HAL + t0 + b.n],
                                     ALU.mult, ALU.subtract), reads=SAB + AB_, writes=[b.aB[ch]])
                eo = sg["edge_off"] + g * 16
                tmpe = self.SMALL[:, 0:16]
                for (dst_t, ecol) in ((0, 0), (n - HAL, HAL)):
                    bfix = sg["blks"][0] if dst_t == 0 else sg["blks"][-1]
                    lt = dst_t - bfix.tb * 512
                    p.op("dve", _tt(tmpe[:, ecol:ecol + HAL], SA[:, o + dst_t:o + dst_t + HAL],
                                    self.EDGE[:, eo + ecol:eo + ecol + HAL], ALU.mult), reads=SAB + [self.CONSTB],
                         writes=[self.SMALLB])
                    p.op("dve", _tt(bfix.a(ch)[:, lt:lt + HAL], tmpe[:, ecol:ecol + HAL],
                                    A[:, HAL + dst_t:HAL + dst_t + HAL], ALU.subtract),
                         reads=[self.SMALLB] + AB_, writes=[bfix.aB[ch]])
        v = pw[j, :, :, :].rearrange("g (kc q) d -> q (g kc) d", q=128)
        wp, wpb = self.ring_load(v, 8, 256)
        for sg in segs:
            for b in sg["blks"]:
                for dc in range(8):
                    g, dh = dc // 2, dc % 2
                    _, yp, ypB = self.bank("yp", [3, 4, 5, 6])
                    for kc in range(2):
                        p.op("pe", _mm(yp[:, :b.n], wp[:, g * 2 + kc, dh * 128:(dh + 1) * 128], b.a(g * 2 + kc),
                                       kc == 0, kc == 1), reads=wpb + [b.aB[g * 2 + kc]], writes=[ypB])
                    p.op("dve", _stt(b.h(dc), yp[:, :b.n], self.dercol(i, 2, dc, b.s), b.h(dc), ALU.mult, ALU.add),
                         reads=[ypB, self.DERB[i], b.hB[dc]], writes=[b.hB[dc]])

    def inp_once(self, name, shape, dt=F32):
        if name in self.din:
            return self.din[name]
        return self.inp(name, shape, dt)

    def lat_seg(self, blks):
        return dict(n=T, s=0, halo=True, blks=blks, edge_off=0, rs_off=0,
                    stat_cols=[(0, 512), (512, 512), (1024, 512), (1536, 512), (2048, 16)],
                    hcols=lambda ch, o, m: self.HT3[:, ch, o:o + m],
                    hbufs_all=[self.HTB[ch] + [self.HALB[ch]] for ch in range(8)],
                    hbufs_ch=lambda ch: self.HTB[ch] + [self.HALB[ch]])

    def ctx_seg(self, cb):
        return dict(n=NCTX, s=1, halo=False, blks=[cb], edge_off=64, rs_off=HAL,
                    stat_cols=[(0, 256)],
                    hcols=lambda ch, o, m: self.CH3[:, ch, o:o + m],
                    hbufs_all=[[self.CTXHB[ch]] for ch in range(8)],
                    hbufs_ch=lambda ch: [self.CTXHB[ch]])

    def ring_load_f32(self, dram_ap, ncols, eng="sp"):
        elems = 2 * ncols
        n = (elems + SLOT - 1) // SLOT
        if self.ring_pos + n > RING_SLOTS:
            self.ring_pos = 0
        pos = self.ring_pos
        self.ring_pos = (pos + n) % RING_SLOTS
        view = self.BIG[:, pos * SLOT:pos * SLOT + elems].bitcast(F32)
        bufs = self.SLOTB[pos:pos + n]
        self.p.dma(eng, view, dram_ap, writes=bufs)
        return view, bufs

    def qk_prep(self, ps, psB, n, gcol, rope, t0, out_ap, outB):
        p = self.p
        k = self.rot.get("qkset", 0)
        self.rot["qkset"] = k + 1
        base = (k % 2) * 2560
        r = k % 2
        sq = self.scr_bf(r, n)
        p.op("act", _act(sq, ps, AF.Square), reads=[psB], writes=[self.SCRB[r]])
        _, ss, ssB = self.bank("ss2", [2, 3])
        p.op("pe", _mm(ss[:, :n], self.ONES[:], sq, True, True), reads=[self.SCRB[r], self.CONSTB], writes=[ssB])
        rs = self.actb_f32(base, n)
        rsB = self.actb_bufs(base, n)
        p.op("act", _act(rs, ss[:, :n], AF.Sqrt, scale=1.0 / 128, bias=self.EPSC[:, 0:1]),
             reads=[ssB, self.CONSTB], writes=rsB)
        p.op("dve", lambda e: e.reciprocal(out=rs, in_=rs), reads=rsB, writes=rsB)
        xg = self.actb_f32(base + 512, n)
        xgB = self.actb_bufs(base + 512, n)
        p.op("dve", _stt(xg, ps, gcol, rs, ALU.mult, ALU.mult), reads=[psB, self.CONSTB] + rsB, writes=xgB)
        if rope:
            _, rp, rpB = self.bank("rot", [4, 5])
            p.op("pe", _mm(rp[:, :n], self.ROT[:], xg, True, True), reads=xgB + [self.CONSTB], writes=[rpB])
            t1 = self.actb_f32(base + 1024, n)
            t1B = self.actb_bufs(base + 1024, n)
            t2 = self.actb_f32(base + 1536, n)
            t2B = self.actb_bufs(base + 1536, n)
            p.op("dve", _tt(t1, xg, self.ropeC[:, t0:t0 + n], ALU.mult), reads=xgB + self.ropeCB, writes=t1B)
            p.op("dve", _tt(t2, rp[:, :n], self.ropeS[:, t0:t0 + n], ALU.mult), reads=[rpB] + self.ropeSB, writes=t2B)
            p.op("dve", _tt(out_ap, t1, t2, ALU.add), reads=t1B + t2B, writes=outB)
        else:
            p.op("act", _act(out_ap, xg, AF.Copy), reads=xgB, writes=outB)

    def stage(self, n):
        k = self.rot.get("stage", 0)
        self.rot["stage"] = k + 1
        base = (k % 2) * 2560 + 2048
        v = self.ACTB[:, base:base + 256].bitcast(BF16)[:, 0:n]
        return v, self.actb_bufs(base, 256)

    def attn_pre(self, blks, cb):
        p = self.p
        allb = blks + [cb]
        self.norm_mod(allb, 1, 0)
        wqkv = self.inp("attn_w_qkv", [1, 1024, 2048])
        wv_ = wqkv[0, :, :].rearrange("(kc q) f -> q kc f", q=128)
        self.ropeC, self.ropeCB = self.ring_load_f32(self.inp("ropec", [128, T]), T)
        self.ropeS, self.ropeSB = self.ring_load_f32(self.inp("ropes", [128, T]), T)
        kT_out = self.outp("kT_out", [4, 128, T + NCTX], BF16)
        v_out = self.outp("v_out", [18, 128, 516], BF16)
        qT_out = self.outp("qT_out", [8, 128, T], BF16)
        wk, wkb = self.ring_load(wv_[:, :, 1024:1536], 8, 512)
        for g in range(4):
            for b in allb:
                _, ps, psB = self.bank("qk", [0, 1])
                for kc in range(8):
                    p.op("pe", _mm(ps[:, :b.n], wk[:, kc, g * 128:(g + 1) * 128], b.a(kc), kc == 0, kc == 7),
                         reads=wkb + [b.aB[kc]], writes=[psB])
                st, stB = self.stage(b.n)
                self.qk_prep(ps[:, :b.n], psB, b.n, self.QKG[:, 1:2], b.s == 0, b.tb * 512, st, stB)
                c0 = b.tb * 512 if b.s == 0 else T
                self.out_dma(kT_out[g, :, c0:c0 + b.n], st, stB)
        wvp, wvb = self.ring_load(wv_[:, :, 1536:2048], 8, 512)
        vst = []
        for k in range(2):
            off = 5120 + k * 512
            v3 = self.ACTB[:, off:off + 258].bitcast(BF16).rearrange("p (g d) -> p g d", g=4)
            vb = self.actb_bufs(off, 258)
            p.op("dve", lambda e, v3=v3: e.memset(v3[:, :, 128:129], 1.0), writes=vb)
            vst.append((v3, vb, self.ACTB[:, off:off + 258].bitcast(BF16)))
        for tile in range(18):
            if tile < 16:
                b = blks[tile // 4]
                lo = (tile % 4) * 128
            else:
                b = cb
                lo = (tile - 16) * 128
            _, ps, psB = self.bank("v", [6])
            for kc in range(8):
                p.op("pe", _mm(ps[:, :512], b.a(kc)[:, lo:lo + 128], wvp[:, kc, :], kc == 0, kc == 7),
                     reads=wvb + [b.aB[kc]], writes=[psB])
            v3, vb, vflat = vst[tile % 2]
            p.op("act", _act(v3[:, :, 0:128], ps[:, :512].rearrange("p (g d) -> p g d", g=4), AF.Copy),
                 reads=[psB], writes=vb)
            self.out_dma(v_out[tile, :, :], vflat, vb)
        for half in range(2):
            wq, wqb = self.ring_load(wv_[:, :, half * 512:(half + 1) * 512], 8, 512)
            for hl in range(4):
                h = half * 4 + hl
                for b in blks:
                    _, ps, psB = self.bank("qk", [0, 1])
                    for kc in range(8):
                        p.op("pe", _mm(ps[:, :b.n], wq[:, kc, hl * 128:(hl + 1) * 128], b.a(kc), kc == 0, kc == 7),
                             reads=wqb + [b.aB[kc]], writes=[psB])
                    st, stB = self.stage(b.n)
                    self.qk_prep(ps[:, :b.n], psB, b.n, self.QKG[:, 0:1], True, b.tb * 512, st, stB)
                    self.out_dma(qT_out[h, :, b.tb * 512:b.tb * 512 + b.n], st, stB)

    def build_A(self):
        self.setup()
        self.load_h()
        blks = self.lat_blocks()
        cb = self.ctx_block()
        cin = self.inp("ctxin", [1024, NCTX])
        cv = cin.rearrange("(c p) t -> p c t", p=128)
        for ch in range(8):
            self.p.dma("sp", self.CH3[:, ch, :], cv[:, ch, :], writes=[self.CTXHB[ch]])
        self.ada_all(0)
        self.pool_mix(0, 0, [self.lat_seg(blks), self.ctx_seg(cb)])
        self.mlp(blks + [cb], 0, ada_next=1)
        self.attn_pre(blks, cb)
        self.store_h()
        self.p.emit(final_wait_bufs=self.outbufs)
        self.es.close()
        return self.nc

    def acta_buf(self, region):
        return self.AB[region // 4][region % 4]

    def attn_core(self, blks):
        p = self.p
        qT_in = self.inp("qT_in", [8, 128, T], BF16)
        kT_all = self.inp("kT_all", [4, 128, 8448], BF16)
        v_all = self.inp("v_all", [4, 128, 66 * 129], BF16)
        Q3 = self.ACTB[:, :].bitcast(BF16).rearrange("p (h t) -> p h t", h=8)
        for h in range(8):
            p.dma("sp", Q3[:, h, :], qT_in[h, :, :], writes=self.ACTBB[2 * h:2 * h + 2])
        OT3 = self.A3
        TPb = self.PS[:, 7 * 512:7 * 512 + 256].bitcast(BF16)
        ON = self.scr_bf(2, 512)
        pending = []

        def make_evac(head, qb):
            def nrm():
                for qt in range(4):
                    rc = self.SMALL[:, 16 + qt:17 + qt]
                    p.op("dve", lambda e, rc=rc, qt=qt: e.reciprocal(out=rc, in_=self.PS[:, qt * 512 + 128:qt * 512 + 129]),
                         reads=[self.PSB[qt]], writes=[self.SMALLB])
                    p.op("dve", _ts(ON[:, qt * 128:(qt + 1) * 128], self.PS[:, qt * 512:qt * 512 + 128], rc, None, ALU.mult),
                         reads=[self.PSB[qt], self.SMALLB], writes=[self.SCRB[2]])

            def tr():
                for qt in range(4):
                    p.op("pe", lambda e, qt=qt: e.transpose(out=TPb[:, qt * 128:(qt + 1) * 128],
                                                            in_=ON[:, qt * 128:(qt + 1) * 128], identity=self.IDENT[:]),
                         reads=[self.SCRB[2], self.CONSTB], writes=[self.PSB[7]])
                p.op("dve", lambda e: e.tensor_copy(out=OT3[:, head, qb * 512:(qb + 1) * 512], in_=TPb[:, 0:512]),
                     reads=[self.PSB[7]], writes=[self.AB[head][qb]])
            return nrm, tr

        for g in range(4):
            sb0 = (g % 2) * 4
            Kv = self.BIG[:, sb0 * SLOT:sb0 * SLOT + 8448]
            KB = self.SLOTB[sb0:sb0 + 2]
            V3 = self.BIG[:, (sb0 + 2) * SLOT:(sb0 + 2) * SLOT + 66 * 129].rearrange("p (k d) -> p k d", k=66)
            VB = self.SLOTB[sb0 + 2:sb0 + 4]
            p.dma("sp", Kv, kT_all[g, :, :], writes=KB)
            p.dma("sp", self.BIG[:, (sb0 + 2) * SLOT:(sb0 + 2) * SLOT + 66 * 129], v_all[g, :, :], writes=VB)
            jobs = [(qb, h2, kt) for qb in range(4) for h2 in range(2) for kt in range(66)]

            def emit_S(job, jidx):
                qb, h2, kt = job
                head = 2 * g + h2
                _, ps, psB = self.bank("st", [4, 5, 6])
                p.op("pe", _mm(ps[:, :512], Kv[:, kt * 128:(kt + 1) * 128], Q3[:, head, qb * 512:(qb + 1) * 512], True, True),
                     reads=KB + self.ACTBB[2 * head:2 * head + 2], writes=[psB])
                return ps, psB

            PTR = (0, 1, 3)
            sq = [emit_S(jobs[0], 0), emit_S(jobs[1], 1)]
            for j, job in enumerate(jobs):
                qb, h2, kt = job
                head = 2 * g + h2
                if j + 2 < len(jobs):
                    sq.append(emit_S(jobs[j + 2], j + 2))
                ps, psB = sq.pop(0)
                r = PTR[j % 3]
                PT = self.scr_bf(r, 512)
                p.op("act", _act(PT, ps[:, :512], AF.Exp, scale=SM_SCALE, bias=self.EPSC[:, 1:2]),
                     reads=[psB, self.CONSTB], writes=[self.SCRB[r]])
                for qt in range(4):
                    p.op("pe", _mm(self.PS[:, qt * 512:qt * 512 + 129], PT[:, qt * 128:(qt + 1) * 128], V3[:, kt, :],
                                   kt == 0, kt == 65), reads=[self.SCRB[r]] + VB, writes=[self.PSB[qt]])
                if kt == 3 and pending:
                    pending.pop(0)()
                if kt == 65:
                    nrm, tr = make_evac(head, qb)
                    nrm()
                    pending.append(tr)
        while pending:
            pending.pop(0)()
        wo = self.inp("attn_w_o", [1, 1024, 1024])
        wov = wo[0, :, :].rearrange("(hc q) d -> q hc d", q=128)
        for half in range(2):
            wp, wpb = self.ring_load(wov[:, :, half * 512:(half + 1) * 512], 8, 512)
            for b in blks:
                for dl in range(4):
                    dc = half * 4 + dl
                    _, yp, ypB = self.bank("yp", [3, 4, 5, 6])
                    for hc in range(8):
                        p.op("pe", _mm(yp[:, :b.n], wp[:, hc, dl * 128:(dl + 1) * 128], OT3[:, hc, b.tb * 512:(b.tb + 1) * 512],
                                       hc == 0, hc == 7), reads=wpb + [self.AB[hc][b.tb]], writes=[ypB])
                    p.op("dve", _stt(b.h(dc), yp[:, :b.n], self.modcol(1, 2, dc, 0), b.h(dc), ALU.mult, ALU.add),
                         reads=[ypB, self.MODB[1], b.hB[dc]], writes=[b.hB[dc]])

    def gmlp(self, blks, i):
        p = self.p
        win = self.inp("gm_w_in", [1, 1024, 4096])
        wout = self.inp("gm_w_out", [1, 2048, 1024])
        winv = win[0, :, :].rearrange("(kc q) f -> q kc f", q=128)
        base = 6 * SLOT
        auxB = self.SLOTB[6:8]
        wsT = self.BIG[:, base:base + 1024].rearrange("p (g q) -> p g q", g=8)
        p.dma("pool", self.BIG[:, base:base + 1024], self.inp("gm_wsT", [128, 1024]), writes=auxB)
        BSB = self.BIG[:, base + 1024:base + 3072].bitcast(F32)
        p.dma("sp", BSB, self.inp("gm_bsb", [128, 1024]), writes=auxB)
        LG = self.BIG[:, base + 3072:base + 3136].bitcast(F32)
        p.dma("sp", LG, self.inp("gm_lgb", [128, 32]), writes=auxB)
        R3 = self.ACTB[:, 6144:8192].rearrange("p (c q) -> p c q", c=16)
        RB = self.ACTBB[12:16]
        VT = self.ACTB[:, 4096:6144]
        VTB = self.ACTBB[8:12]
        U3 = self.ACTB[:, 0:4096].bitcast(BF16).rearrange("p (c t) -> p c t", c=16)
        A8 = self.ACTA[:, 0:4096].rearrange("p (c t) -> p c t", c=8)
        G3 = self.ACTA[:, 4096:12288].rearrange("p (c t) -> p c t", c=16)
        VNs = [self.ACTA[:, 12288 + k * 2048:14336 + k * 2048] for k in range(2)]
        VNBs = [[self.acta_buf(24 + 4 * k + q) for q in range(4)] for k in range(2)]
        for g in range(8):
            _, ps, psB = self.bank("hp", [0, 1, 2])
            p.op("pe", _mm(ps[:, :128], self.ONES[:], wsT[:, g, :], True, True), reads=auxB + [self.CONSTB], writes=[psB])
            for cc in (2 * g, 2 * g + 1):
                p.op("dve", _stt(R3[:, cc, :], ps[:, :128], LG[:, 16 + cc:17 + cc], BSB[:, g * 128:(g + 1) * 128],
                                 ALU.mult, ALU.add), reads=[psB] + auxB, writes=RB)
        for b in blks:
            rs, rsB = self.rstd_block(b.h, b.hB, b.n)
            for kc in range(8):
                _, tmp, tmpB = self.bank("tmp", [4, 5])
                p.op("dve", _tt(tmp[:, :b.n], b.h(kc), rs, ALU.mult), reads=[b.hB[kc]] + rsB, writes=[tmpB])
                p.op("act", _act(A8[:, kc, :], tmp[:, :b.n], AF.Identity, scale=self.dercol(i, 0, kc, 0),
                                 bias=self.modcol(i, 0, kc, 0)), reads=[tmpB, self.DERB[i], self.MODB[i]],
                     writes=[self.acta_buf(kc)])
            a8B = [self.acta_buf(kc) for kc in range(8)]
            for cg in range(4):
                wp, wpb = self.ring_load(winv[:, :, cg * 512:(cg + 1) * 512], 8, 512)
                for cl in range(4):
                    c = cg * 4 + cl
                    _, ps, psB = self.bank("hp", [0, 1, 2])
                    for kc in range(8):
                        p.op("pe", _mm(ps[:, :512], wp[:, kc, cl * 128:(cl + 1) * 128], A8[:, kc, :], kc == 0, kc == 7),
                             reads=wpb + [a8B[kc]], writes=[psB])
                    p.op("act", _act(U3[:, c, :], ps[:, :512], AF.Gelu_apprx_tanh), reads=[psB], writes=[self.ACTBB[c // 2]])
            pv = [self.ring_load(winv[:, :, 2048 + cb * 512:2048 + (cb + 1) * 512], 8, 512) for cb in range(4)]
            def v_mm(tt):
                for cb in range(4):
                    bk = 3 + cb
                    ps, psB = self.PS[:, bk * 512:(bk + 1) * 512], self.PSB[bk]
                    for kc in range(8):
                        p.op("pe", _mm(ps[:, :512], A8[:, kc, tt * 128:(tt + 1) * 128], pv[cb][0][:, kc, :], kc == 0, kc == 7),
                             reads=pv[cb][1] + [a8B[kc]], writes=[psB])
                    p.op("act", _act(VT[:, cb * 512:(cb + 1) * 512], ps[:, :512], AF.Gelu_apprx_tanh,
                                     accum_out=self.SMALL[:, 32 + cb:33 + cb]), reads=[psB], writes=[VTB[cb], self.SMALLB])
                    p.op("act", _act(self.scr_bf(3, 512), VT[:, cb * 512:(cb + 1) * 512], AF.Square,
                                     accum_out=self.SMALL[:, 36 + cb:37 + cb]), reads=[VTB[cb]],
                         writes=[self.SCRB[3], self.SMALLB])

            def v_ln(tt):
                VN, VNB = VNs[tt % 2], VNBs[tt % 2]
                S = self.SMALL
                sB = [self.SMALLB]
                p.op("dve", lambda e: e.tensor_reduce(out=S[:, 40:41], in_=S[:, 32:36], axis=AX.X, op=ALU.add), reads=sB, writes=sB)
                p.op("dve", lambda e: e.tensor_reduce(out=S[:, 41:42], in_=S[:, 36:40], axis=AX.X, op=ALU.add), reads=sB, writes=sB)
                p.op("dve", _ts(S[:, 42:43], S[:, 40:41], 1.0 / 2048, None, ALU.mult), reads=sB, writes=sB)
                p.op("dve", _ts(S[:, 43:44], S[:, 41:42], 1.0 / 2048, None, ALU.mult), reads=sB, writes=sB)
                p.op("dve", _tt(S[:, 44:45], S[:, 42:43], S[:, 42:43], ALU.mult), reads=sB, writes=sB)
                p.op("dve", _tt(S[:, 45:46], S[:, 43:44], S[:, 44:45], ALU.subtract), reads=sB, writes=sB)
                p.op("act", _act(S[:, 46:47], S[:, 45:46], AF.Sqrt, bias=self.EPSC[:, 0:1]), reads=sB + [self.CONSTB], writes=sB)
                p.op("dve", lambda e: e.reciprocal(out=S[:, 47:48], in_=S[:, 46:47]), reads=sB, writes=sB)
                p.op("dve", _ts(VN, VT, S[:, 42:43], S[:, 47:48], ALU.subtract, ALU.mult), reads=VTB + sB, writes=VNB)

            def spatial(tt):
                VN, VNB = VNs[tt % 2], VNBs[tt % 2]
                for c in range(16):
                    mb = 7 if (c // 4) % 2 == 0 else 0
                    M = self.PS[:, mb * 512 + (c % 4) * 128:mb * 512 + (c % 4) * 128 + 128]
                    p.op("pe", _mm(M, VN[:, c * 128:(c + 1) * 128], wsT[:, c // 2, :], True, True),
                         reads=VNB + auxB, writes=[self.PSB[mb]])
                    r2 = c % 2
                    SV = self.scr_f32(r2, 128)
                    svB = self.SCRB[2 * r2:2 * r2 + 1]
                    p.op("dve", _stt(SV, M, LG[:, c:c + 1], R3[:, c, :], ALU.mult, ALU.add),
                         reads=[self.PSB[mb]] + auxB + RB, writes=svB)
                    p.op("dve", _tt(G3[:, c, tt * 128:(tt + 1) * 128], SV, U3[:, c, tt * 128:(tt + 1) * 128], ALU.mult),
                         reads=svB + [self.ACTBB[c // 2]], writes=[self.acta_buf(8 + c)])

            for tt in range(4):
                v_mm(tt)
                if tt > 0:
                    spatial(tt - 1)
                v_ln(tt)
            spatial(3)
            for half in range(2):
                po = [self.ring_load(wout[0, (half * 2 + k) * 512:(half * 2 + k + 1) * 512, :].rearrange("(fc q) d -> q fc d", q=128), 4, 1024)
                      for k in range(2)]
                for dc in range(8):
                    _, yp, ypB = self.bank("yp", [3, 4, 5, 6])
                    for k8 in range(8):
                        c = half * 8 + k8
                        w, wb = po[k8 // 4]
                        p.op("pe", _mm(yp[:, :512], w[:, k8 % 4, dc * 128:(dc + 1) * 128], G3[:, c, :], k8 == 0, k8 == 7),
                             reads=wb + [self.acta_buf(8 + c)], writes=[ypB])
                    p.op("dve", _stt(b.h(dc), yp[:, :512], self.modcol(i, 2, dc, 0), b.h(dc), ALU.mult, ALU.add),
                         reads=[ypB, self.MODB[i], b.hB[dc]], writes=[b.hB[dc]])

    def final_norm(self, blks):
        p = self.p
        outT = self.outp("outT", [1024, T])
        ov = outT.rearrange("(c q) t -> q c t", q=128)
        k = 0
        for b in blks:
            rs, rsB = self.rstd_block(b.h, b.hB, b.n)
            for ch in range(8):
                st = self.ACTB[:, (k % 16) * 512:(k % 16) * 512 + 512]
                stB = [self.ACTBB[k % 16]]
                k += 1
                p.op("dve", _stt(st, b.h(ch), self.FG[:, ch:ch + 1], rs, ALU.mult, ALU.mult),
                     reads=[b.hB[ch], self.CONSTB] + rsB, writes=stB)
                self.out_dma(ov[:, ch, b.tb * 512:(b.tb + 1) * 512], st, stB)

    def build_B(self):
        self.setup()
        self.load_h()
        blks = self.lat_blocks()
        self.ada_all(1)
        self.attn_core(blks)
        self.mlp(blks, 1, ada_next=2)
        self.gmlp(blks, 2)
        self.mlp(blks, 2)
        self.store_h()
        self.p.emit(final_wait_bufs=self.outbufs)
        self.es.close()
        return self.nc

    def build_C(self):
        self.setup()
        self.load_h()
        blks = self.lat_blocks()
        self.ada_all(3)
        self.pool_mix(3, 1, [self.lat_seg(blks)])
        self.mlp(blks, 3)
        self.final_norm(blks)
        self.p.emit(final_wait_bufs=self.outbufs)
        self.es.close()
        return self.nc


SEQ = 8192
_PROGS = {}


def _fm(v):
    v = np.asarray(v, np.float32)
    lead = v.shape[:-1]
    return np.moveaxis(v.reshape(*lead, 8, 128), -1, 0)


def _consts_for_core(c, inputs):
    b, r = c // 4, c % 4
    q0 = r * T
    f32 = np.float32
    d = {}
    cv = np.stack([_fm(inputs["c"][b]), _fm(inputs["c_ctx"])], axis=-1)
    d["cvec"] = np.ascontiguousarray(cv.reshape(128, 16), f32)
    ab = _fm(np.asarray(inputs["ada_b"]).reshape(4, 6, 1024))
    d["adab"] = np.ascontiguousarray(np.repeat(ab[..., None], 2, -1).reshape(128, 4 * 96), f32)
    ng = _fm(np.asarray(inputs["norm_g"]))
    d["ng2"] = np.ascontiguousarray(np.repeat(ng[..., None], 2, -1).reshape(128, 128), f32)
    ps = _fm(np.asarray(inputs["pool_scale"]))
    d["ps2"] = np.ascontiguousarray(np.repeat(ps[..., None], 2, -1).reshape(128, 32), f32)
    d["fg"] = np.ascontiguousarray(_fm(np.asarray(inputs["final_g"])).reshape(128, 8), f32)
    d["qkg"] = np.ascontiguousarray(np.stack([np.asarray(inputs["attn_q_g"])[0], np.asarray(inputs["attn_k_g"])[0]], 1), f32)
    d["ident"] = np.eye(128, dtype=f32)
    rot = np.zeros((128, 128), f32)
    for base in (0, 64):
        for m in range(32):
            rot[base + m + 32, base + m] = -1.0
            rot[base + m, base + m + 32] = 1.0
    d["rot"] = rot
    hm = np.zeros((128, 32), f32)
    hm[:, 0:8] = 1.0 if q0 > 0 else 0.0
    hm[:, 8:16] = 1.0 if q0 + T < SEQ else 0.0
    d["hm"] = hm
    edge = np.zeros((128, 128), f32)
    for g, w in enumerate((2, 4, 8, 16)):
        for e in range(16):
            pos = q0 + (e if e < 8 else T - 16 + e)
            lo, hi = max(pos - w // 2, 0), min(pos + w - w // 2, SEQ)
            edge[:, g * 16 + e] = 1.0 / (hi - lo)
            posc = e if e < 8 else NCTX - 16 + e
            lo, hi = max(posc - w // 2, 0), min(posc + w - w // 2, NCTX)
            edge[:, 64 + g * 16 + e] = 1.0 / (hi - lo)
    d["edge"] = edge
    return d


def _rope_tables(q0):
    t = np.arange(q0, q0 + T)
    row = (t // 64).astype(np.float32)
    col = (t % 64).astype(np.float32)
    inv = (np.float32(10000.0) ** (-np.arange(0, 64, 2, dtype=np.float32) / np.float32(64))).astype(np.float32)
    ang_r = (row[:, None] * inv[None, :]).astype(np.float32)
    ang_c = (col[:, None] * inv[None, :]).astype(np.float32)
    C = np.concatenate([np.cos(ang_r), np.cos(ang_r), np.cos(ang_c), np.cos(ang_c)], 1).T
    S = np.concatenate([np.sin(ang_r), np.sin(ang_r), np.sin(ang_c), np.sin(ang_c)], 1).T
    return np.ascontiguousarray(C, np.float32), np.ascontiguousarray(S, np.float32)


def _hin_from_rows(rows_b, r):
    q0 = r * T
    out = np.zeros((TC, 1024), np.float32)
    lo, hi = max(q0 - HAL, 0), min(q0 + T + HAL, SEQ)
    out[lo - (q0 - HAL):hi - (q0 - HAL)] = rows_b[lo:hi]
    return np.ascontiguousarray(out.T)


def _layer_weights(inputs, seg):
    al, ml = list(MK.ADA_LAYERS[seg]), list(MK.MLP_LAYERS[seg])
    return {"ada_w": np.ascontiguousarray(np.asarray(inputs["ada_w"], np.float32)[al]),
            "mlp_w1": np.ascontiguousarray(np.asarray(inputs["mlp_w1"], np.float32)[ml]),
            "mlp_w2": np.ascontiguousarray(np.asarray(inputs["mlp_w2"], np.float32)[ml])}


def _weights(inputs, names):
    return {k: np.asarray(inputs[k], np.float32) for k in names}


def _get_prog(seg):
    if seg not in _PROGS:
        mk = MK(seg)
        nc = getattr(mk, "build_" + seg)()
        _PROGS[seg] = (nc, list(mk.din.keys()))
    return _PROGS[seg]


def _launch(seg, per_core):
    nc, names = _get_prog(seg)
    in_maps = [{k: m[k] for k in names} for m in per_core]
    res = run_bass_kernel_spmd(nc, in_maps, core_ids=list(range(8)))
    return res.results


def run_seg_A(inputs):
    x = np.asarray(inputs["x"], np.float32)
    ctx = np.asarray(inputs["ctx"], np.float32)
    w = _weights(inputs, ("pool_w", "attn_w_qkv"))
    w.update(_layer_weights(inputs, "A"))
    per_core = []
    for c in range(8):
        b, r = c // 4, c % 4
        m = _consts_for_core(c, inputs)
        m["hin"] = _hin_from_rows(x[b], r)
        m["ctxin"] = np.ascontiguousarray(ctx[b].T)
        m["ropec"], m["ropes"] = _rope_tables(r * T)
        m.update(w)
        per_core.append(m)
    return _launch("A", per_core)


def run_seg_B(inputs, resA):
    import ml_dtypes
    bf = ml_dtypes.bfloat16
    w = _weights(inputs, ("attn_w_o", "gm_w_in", "gm_w_out"))
    w.update(_layer_weights(inputs, "B"))
    ws = np.asarray(inputs["gm_ws"], np.float32)[0]
    wsT = np.ascontiguousarray(ws.transpose(2, 0, 1).reshape(128, 1024))
    bs = np.asarray(inputs["gm_bs"], np.float32)[0]
    bsb = np.ascontiguousarray(np.broadcast_to(bs.reshape(1, 1024), (128, 1024)), np.float32)
    lgb = np.concatenate([_fm(np.asarray(inputs["gm_ln_g"])[0].reshape(2, 1024)).reshape(128, 16),
                          _fm(np.asarray(inputs["gm_ln_b"])[0].reshape(2, 1024)).reshape(128, 16)], 1)
    lgb = np.ascontiguousarray(lgb, np.float32)
    per_core = []
    kv_cache = {}
    for b in range(2):
        kT = np.zeros((4, 128, 8448), bf)
        va = np.zeros((4, 128, 66, 129), bf)
        for r in range(4):
            ra = resA[b * 4 + r]
            k = np.asarray(ra["kT_out"])
            v = np.asarray(ra["v_out"]).reshape(18, 128, 4, 129)
            kT[:, :, r * T:(r + 1) * T] = k[:, :, :T]
            va[:, :, r * 16:(r + 1) * 16, :] = v[:16].transpose(2, 1, 0, 3)
            if r == 0:
                kT[:, :, 4 * T:] = k[:, :, T:]
                va[:, :, 64:66, :] = v[16:].transpose(2, 1, 0, 3)
        kv_cache[b] = (kT, np.ascontiguousarray(va.reshape(4, 128, 66 * 129)))
    for c in range(8):
        b, r = c // 4, c % 4
        m = _consts_for_core(c, inputs)
        hin = np.zeros((1024, TC), np.float32)
        hin[:, HAL:HAL + T] = np.asarray(resA[c]["hout"])
        m["hin"] = hin
        m["qT_in"] = np.asarray(resA[c]["qT_out"])
        m["kT_all"], m["v_all"] = kv_cache[b]
        m["gm_wsT"], m["gm_bsb"], m["gm_lgb"] = wsT, bsb, lgb
        m.update(w)
        per_core.append(m)
    return _launch("B", per_core)


def run_seg_C(inputs, resB):
    w = _weights(inputs, ("pool_w",))
    w.update(_layer_weights(inputs, "C"))
    per_core = []
    for b in range(2):
        rows = np.concatenate([np.asarray(resB[b * 4 + r]["hout"]).T for r in range(4)], 0)
        for r in range(4):
            m = _consts_for_core(b * 4 + r, inputs)
            m["hin"] = _hin_from_rows(rows, r)
            m.update(w)
            per_core.append(m)
    return _launch("C", per_core)


def kernel(**inputs):
    resA = run_seg_A(inputs)
    resB = run_seg_B(inputs, resA)
    resC = run_seg_C(inputs, resB)
    out = np.zeros((2, SEQ, 1024), np.float32)
    for c in range(8):
        b, r = c // 4, c % 4
        out[b, r * T:(r + 1) * T, :] = np.asarray(resC[c]["outT"]).T
    return out
```

```python
import numpy as np
import concourse.bass as bass
import concourse.mybir as mybir
from concourse.bass_utils import run_bass_kernel_spmd

F32 = mybir.dt.float32
BF16 = mybir.dt.bfloat16
AF = mybir.ActivationFunctionType
ALU = mybir.AluOpType
AX = mybir.AxisListType


class Buf:
    __slots__ = ("name", "last_write", "readers")

    def __init__(self, name=""):
        self.name = name
        self.last_write = None
        self.readers = []


class _Ins:
    __slots__ = ("eng", "fn", "deps", "idx", "signal", "dma_tok", "dma_prev")

    def __init__(self, eng, fn, deps, idx):
        self.eng = eng
        self.fn = fn
        self.deps = deps
        self.idx = idx
        self.signal = False
        self.dma_tok = None
        self.dma_prev = None


ENGS = ("pe", "act", "dve", "pool", "sp")
SEM_LIMIT = 12000
N_DMA_SEMS = 12


class Prog:
    def __init__(self, nc):
        self.nc = nc
        self.streams = {e: [] for e in ENGS}
        self.dma_count = {e: 0 for e in ENGS}

    def _collect(self, reads, writes):
        deps = []
        for b in reads:
            if b.last_write is not None:
                deps.append(b.last_write)
        for b in writes:
            if b.last_write is not None:
                deps.append(b.last_write)
            deps.extend(b.readers)
        return deps

    def _commit(self, tok, reads, writes):
        for b in reads:
            b.readers.append(tok)
        for b in writes:
            b.last_write = tok
            b.readers = []

    def op(self, eng, fn, reads=(), writes=()):
        deps = self._collect(reads, writes)
        st = self.streams[eng]
        ins = _Ins(eng, fn, deps, len(st))
        st.append(ins)
        self._commit(("c", eng, ins.idx), reads, writes)
        return ins

    def coll(self, kind, groups, src, dst, reads=(), writes=()):
        return self.dma("pool", None, None, reads, writes,
                        fn=lambda e: e.collective_compute(kind, ALU.bypass, replica_groups=groups,
                                                          ins=[src], outs=[dst]))

    def dma(self, eng, out, in_, reads=(), writes=(), fn=None):
        deps = self._collect(reads, writes)
        st = self.streams[eng]
        k = self.dma_count[eng]
        self.dma_count[eng] = k + 1
        slot, rnd = k % N_DMA_SEMS, k // N_DMA_SEMS
        if fn is None:
            fn = lambda e: e.dma_start(out=out, in_=in_)
        ins = _Ins(eng, fn, deps, len(st))
        ins.dma_tok = ("d", eng, slot, 16 * (rnd + 1))
        if rnd > 0:
            ins.dma_prev = ("d", eng, slot, 16 * rnd)
        st.append(ins)
        self._commit(ins.dma_tok, reads, writes)
        return ins

    def emit(self, final_wait_bufs=()):
        nc = self.nc
        final_deps = []
        for b in final_wait_bufs:
            if b.last_write is not None:
                final_deps.append(b.last_write)
        for e in ENGS:
            for ins in self.streams[e]:
                best = {}
                kept = []
                for d in ins.deps:
                    if d[0] == "c":
                        if e == "pe" and d[1] == "pe":
                            continue
                        if d[1] not in best or best[d[1]][2] < d[2]:
                            best[d[1]] = d
                    else:
                        kept.append(d)
                ins.deps = kept + list(best.values())
                for d in ins.deps:
                    if d[0] == "c":
                        self.streams[d[1]][d[2]].signal = True
        for d in final_deps:
            if d[0] == "c":
                self.streams[d[1]][d[2]].signal = True
        count_of = {}
        n_epochs = {}
        for e in ENGS:
            c = 0
            for ins in self.streams[e]:
                if ins.signal:
                    c += 1
                    count_of[(e, ins.idx)] = c
            n_epochs[e] = max(1, (c + SEM_LIMIT - 1) // SEM_LIMIT)
        import contextlib
        with contextlib.ExitStack() as es:
            csem = {e: [es.enter_context(nc.semaphore(f"c_{e}_{i}")) for i in range(n_epochs[e])]
                    for e in ENGS}
            dsem = {e: [es.enter_context(nc.semaphore(f"d_{e}_{i}")) for i in range(N_DMA_SEMS)]
                    for e in ENGS if self.dma_count[e] > 0}
            block = es.enter_context(nc.Block())

            def resolve(tok):
                if tok[0] == "c":
                    c = count_of[(tok[1], tok[2])]
                    ep = (c - 1) // SEM_LIMIT
                    return (csem[tok[1]][ep], ("c", tok[1], ep), c - ep * SEM_LIMIT)
                return (dsem[tok[1]][tok[2]], ("d", tok[1], tok[2]), tok[3])

            def run(e, eng):
                known = {}
                for ins in self.streams[e]:
                    toks = list(ins.deps)
                    if ins.dma_prev is not None:
                        toks.append(ins.dma_prev)
                    need = {}
                    for t in toks:
                        sem, key, val = resolve(t)
                        if known.get(key, 0) >= val:
                            continue
                        if key not in need or need[key][1] < val:
                            need[key] = (sem, val)
                    for key, (sem, val) in need.items():
                        eng.wait_ge(sem, val)
                        known[key] = val
                    bi = ins.fn(eng)
                    if ins.dma_tok is not None:
                        bi.then_inc(dsem[e][ins.dma_tok[2]], 16)
                    elif ins.signal:
                        c = count_of[(e, ins.idx)]
                        ep = (c - 1) // SEM_LIMIT
                        bi.then_inc(csem[e][ep], 1)
                if e == "sp":
                    for t in final_deps:
                        sem, key, val = resolve(t)
                        if known.get(key, 0) >= val:
                            continue
                        eng.wait_ge(sem, val)
                        known[key] = max(known.get(key, 0), val)

            @block.tensor
            def _(eng):
                run("pe", eng)

            @block.scalar
            def _(eng):
                run("act", eng)

            @block.vector
            def _(eng):
                run("dve", eng)

            @block.gpsimd
            def _(eng):
                run("pool", eng)

            @block.sync
            def _(eng):
                run("sp", eng)


T = 2048
HAL = 8
TC = T + 2 * HAL
NCTX = 256
SLOT = 4288
NSLOT = 8
RING_SLOTS = 6
EPS = 1e-6
SM_SCALE = 128 ** -0.5
SM_SHIFT = -(128 ** 0.5)


def _mm(out, lhsT, rhs, start, stop):
    return lambda e: e.matmul(out, lhsT=lhsT, rhs=rhs, start=start, stop=stop)


def _act(out, in_, func, **kw):
    return lambda e: e.activation(out=out, in_=in_, func=func, **kw)


def _tt(out, in0, in1, op):
    return lambda e: e.tensor_tensor(out=out, in0=in0, in1=in1, op=op)


def _stt(out, in0, scalar, in1, op0, op1):
    return lambda e: e.scalar_tensor_tensor(out=out, in0=in0, scalar=scalar, in1=in1, op0=op0, op1=op1)


def _ts(out, in0, s1, s2, op0, op1=None):
    if op1 is None:
        return lambda e: e.tensor_scalar(out=out, in0=in0, scalar1=s1, scalar2=None, op0=op0)
    return lambda e: e.tensor_scalar(out=out, in0=in0, scalar1=s1, scalar2=s2, op0=op0, op1=op1)


class Blk:
    pass


class MK:
    ADA_LAYERS = {"A": (0, 1), "B": (1, 2), "C": (3,)}
    MLP_LAYERS = {"A": (0,), "B": (1, 2), "C": (3,)}

    def __init__(self, seg):
        import contextlib
        self.seg = seg
        self.ada_l = self.ADA_LAYERS[seg]
        self.mlp_l = self.MLP_LAYERS[seg]
        self.nc = nc = bass.Bass("TRN2", target_bir_lowering=False)
        self.es = es = contextlib.ExitStack()
        self.p = Prog(nc)
        self.din = {}
        self.dout = {}

        def sb(name, shape, dt):
            return es.enter_context(nc.sbuf_tensor(name, shape, dt))

        self.HT = sb("HT", [128, 8 * TC], F32)
        self.HT3 = self.HT[:, :].rearrange("p (c t) -> p c t", c=8)
        self.ACTA = sb("ACTA", [128, 8 * T], BF16)
        self.A3 = self.ACTA[:, :].rearrange("p (c t) -> p c t", c=8)
        self.ACTB = sb("ACTB", [128, 8192], F32)
        self.BIG = sb("BIG", [128, NSLOT * SLOT], BF16)
        self.MOD = sb("MOD", [128, 4 * 96], F32)
        self.ADAB = sb("ADAB", [128, 96], F32)
        self.NG2 = sb("NG2", [128, 128], F32)
        self.PS2 = sb("PS2", [128, 32], F32)
        self.FG = sb("FG", [128, 8], F32)
        self.CV = sb("CV", [128, 16], F32)
        self.SBF = sb("SBF", [128, 16], BF16)
        self.ONES = sb("ONES", [128, 128], BF16)
        self.IDENT = sb("IDENT", [128, 128], BF16)
        self.ROT = sb("ROT", [128, 128], F32)
        self.EPSC = sb("EPSC", [128, 2], F32)
        self.DER = sb("DER", [128, 4 * 48], F32)
        self.HM = sb("HM", [128, 32], F32)
        self.EDGE = sb("EDGE", [128, 128], F32)
        self.QKG = sb("QKG", [128, 2], F32)
        self.SCR = sb("SCR", [128, 1024], F32)
        self.SMALL = sb("SMALL", [128, 64], F32)
        self.PS = es.enter_context(nc.psum_tensor("PS", [128, 8 * 512], F32))

        self.HTB = [[Buf(f"h{c}_{t}") for t in range(4)] for c in range(8)]
        self.HALB = [Buf(f"hal{c}") for c in range(8)]
        self.AB = [[Buf(f"a{c}_{t}") for t in range(4)] for c in range(8)]
        self.ACTBB = [Buf(f"actb{k}") for k in range(16)]
        self.SLOTB = [Buf(f"slot{k}") for k in range(NSLOT)]
        self.PSB = [Buf(f"ps{k}") for k in range(8)]
        self.SCRB = [Buf(f"scr{k}") for k in range(4)]
        self.MODB = [[Buf(f"mod{i}_{q}") for q in range(6)] for i in range(4)]
        self.DERB = [[Buf(f"der{i}_{q}") for q in range(3)] for i in range(4)]
        self._adab_layer = None
        self.CONSTB = Buf("const")
        self.SMALLB = Buf("small")
        self.ADABB = Buf("adab")
        self.ring_pos = 0
        self.rot = {}
        self.outbufs = []

    def inp(self, name, shape, dt=F32):
        t = self.nc.dram_tensor(name, list(shape), dt, kind="ExternalInput").ap()
        self.din[name] = t
        return t

    def outp(self, name, shape, dt=F32):
        t = self.nc.dram_tensor(name, list(shape), dt, kind="ExternalOutput").ap()
        self.dout[name] = t
        return t

    def bank(self, role, banks):
        k = self.rot.get(role, 0)
        self.rot[role] = k + 1
        b = banks[k % len(banks)]
        return b, self.PS[:, b * 512:(b + 1) * 512], self.PSB[b]

    def scr_bf(self, r, n=512):
        return self.SCR[:, r * 256:(r + 1) * 256].bitcast(BF16)[:, 0:n]

    def scr_f32(self, r2, n=512):
        return self.SCR[:, r2 * 512:r2 * 512 + n]

    def actb_f32(self, off, n):
        return self.ACTB[:, off:off + n]

    def actb_bufs(self, off, n):
        return self.ACTBB[off // 512:(off + n + 511) // 512]

    def ring_load(self, dram_ap, a, b, eng="pool"):
        elems = a * b
        n = (elems + SLOT - 1) // SLOT
        if self.ring_pos + n > RING_SLOTS:
            self.ring_pos = 0
        pos = self.ring_pos
        self.ring_pos = (pos + n) % RING_SLOTS
        view = self.BIG[:, pos * SLOT:pos * SLOT + elems].rearrange("p (a b) -> p a b", a=a)
        bufs = self.SLOTB[pos:pos + n]
        self.p.dma(eng, view, dram_ap, writes=bufs)
        return view, bufs

    def out_dma(self, dst, src, reads):
        b = Buf("out")
        self.outbufs.append(b)
        self.p.dma("sp", dst, src, reads=reads, writes=[b])

    def col(self, t, idx):
        return t[:, idx:idx + 1]

    def modcol(self, i, part, ch, s):
        return self.col(self.MOD, i * 96 + (part * 8 + ch) * 2 + s)

    def dercol(self, i, kind, ch, s):
        return self.col(self.DER, i * 48 + (kind * 8 + ch) * 2 + s)

    def setup(self):
        p = self.p
        c = self.CONSTB
        p.op("dve", lambda e: e.memset(self.ONES[:], 1.0), writes=[c])
        p.op("dve", lambda e: e.memset(self.EPSC[:, 0:1], EPS), writes=[c])
        p.op("dve", lambda e: e.memset(self.EPSC[:, 1:2], SM_SHIFT), writes=[c])
        small = [("ng2", self.NG2, 128), ("ps2", self.PS2, 32), ("fg", self.FG, 8), ("cvec", self.CV, 16),
                 ("rot", self.ROT, 128), ("hm", self.HM, 32), ("edge", self.EDGE, 128), ("qkg", self.QKG, 2)]
        for name, t, n in small:
            d = self.inp(name, [128, n])
            p.dma("sp", t[:], d, writes=[c])
        d = self.inp("ident", [128, 128])
        p.dma("pool", self.IDENT[:], d, writes=[c])
        p.op("act", _act(self.SBF[:], self.CV[:], AF.Silu), reads=[c], writes=[c])
        self.ada_w = self.inp("ada_w", [len(self.ada_l), 1024, 6144])
        self.adab_d = self.inp("adab", [128, 4 * 96])
        self.w1 = self.inp("mlp_w1", [len(self.mlp_l), 1024, 4096])
        self.w2 = self.inp("mlp_w2", [len(self.mlp_l), 4096, 1024])

    def load_h(self):
        hin = self.inp("hin", [1024, TC])
        v = hin.rearrange("(c p) t -> p c t", p=128)
        for ch in range(8):
            self.p.dma("sp", self.HT3[:, ch, :], v[:, ch, :], writes=self.HTB[ch] + [self.HALB[ch]])

    def store_h(self):
        hout = self.outp("hout", [1024, T])
        v = hout.rearrange("(c p) t -> p c t", p=128)
        for ch in range(8):
            self.out_dma(v[:, ch, :], self.HT3[:, ch, HAL:HAL + T], self.HTB[ch])

    def lat_blocks(self):
        blks = []
        for tb in range(4):
            b = Blk()
            b.n, b.s, b.tb = 512, 0, tb
            b.h = (lambda ch, tb=tb: self.HT3[:, ch, HAL + tb * 512:HAL + (tb + 1) * 512])
            b.hB = [self.HTB[ch][tb] for ch in range(8)]
            b.a = (lambda ch, tb=tb: self.A3[:, ch, tb * 512:(tb + 1) * 512])
            b.aB = [self.AB[ch][tb] for ch in range(8)]
            hid3 = self.ACTB[:, 0:4096].bitcast(BF16).rearrange("p (c t) -> p c t", c=4)
            b.hid = (lambda fc, tb=tb, hid3=hid3: hid3[:, fc, tb * 512:(tb + 1) * 512])
            b.hidB = [self.ACTBB[fc * 2 + tb // 2] for fc in range(4)]
            blks.append(b)
        return blks

    def ctx_block(self):
        b = Blk()
        b.n, b.s, b.tb = NCTX, 1, 0
        base = 6 * SLOT
        ch3 = self.BIG[:, base:base + 4096].bitcast(F32).rearrange("p (c t) -> p c t", c=8)
        ca3 = self.BIG[:, base + 4096:base + 6144].rearrange("p (c t) -> p c t", c=8)
        chid = self.BIG[:, base + 6144:base + 7168].rearrange("p (c t) -> p c t", c=4)
        self.CH3 = ch3
        b.h = lambda ch: ch3[:, ch, :]
        self.CTXHB = [Buf(f"ctxh{c}") for c in range(8)]
        b.hB = self.CTXHB
        b.a = lambda ch: ca3[:, ch, :]
        b.aB = [Buf(f"ctxa{c}") for c in range(8)]
        b.hid = lambda fc: chid[:, fc, :]
        b.hidB = [Buf(f"ctxhid{c}") for c in range(4)]
        return b

    def ada_piece(self, i, k):
        p = self.p
        v = self.ada_w[self.ada_l.index(i), :, :].rearrange("(kc q) f -> q kc f", q=128)[:, :, k * 512:(k + 1) * 512]
        w, wb = self.ring_load(v, 8, 512)
        s3 = self.SBF[:, :].rearrange("p (kc s) -> p kc s", s=2)
        for j in range(4):
            idx = k * 4 + j
            for kc in range(8):
                p.op("pe", _mm(self.PS[:, 7 * 512 + 2 * idx:7 * 512 + 2 * idx + 2], w[:, kc, j * 128:(j + 1) * 128],
                               s3[:, kc, :], kc == 0, kc == 7), reads=wb + [self.CONSTB], writes=[self.PSB[7]])

    def ada_finish_parts(self, i, parts):
        p = self.p
        if self._adab_layer != i:
            self._adab_layer = i
            p.dma("sp", self.ADAB[:], self.adab_d[:, i * 96:(i + 1) * 96], writes=[self.ADABB])
        for part in parts:
            p.op("dve", _tt(self.MOD[:, i * 96 + part * 16:i * 96 + part * 16 + 16],
                            self.PS[:, 7 * 512 + part * 16:7 * 512 + part * 16 + 16],
                            self.ADAB[:, part * 16:part * 16 + 16], ALU.add),
                 reads=[self.PSB[7], self.ADABB], writes=[self.MODB[i][part]])
        for kind, part, which in ((0, 1, 0), (1, 4, 1)):
            if part not in parts:
                continue
            src = self.MOD[:, i * 96 + part * 16:i * 96 + part * 16 + 16]
            dst = self.DER[:, i * 48 + kind * 16:i * 48 + kind * 16 + 16]
            ng = self.NG2[:, (i * 2 + which) * 16:(i * 2 + which) * 16 + 16]
            p.op("dve", _stt(dst, src, 1.0, ng, ALU.add, ALU.mult), reads=[self.MODB[i][part], self.CONSTB],
                 writes=[self.DERB[i][kind]])
        if i % 3 == 0 and 2 in parts:
            j = i // 3
            src = self.MOD[:, i * 96 + 2 * 16:i * 96 + 3 * 16]
            dst = self.DER[:, i * 48 + 32:i * 48 + 48]
            p.op("dve", _tt(dst, src, self.PS2[:, j * 16:(j + 1) * 16], ALU.mult), reads=[self.MODB[i][2], self.CONSTB],
                 writes=[self.DERB[i][2]])

    def ada_finish(self, i):
        self.ada_finish_parts(i, range(6))

    def ada_early(self, i):
        for k in range(4):
            self.ada_piece(i, k)
        self.ada_finish_parts(i, [0, 1])

    def ada_late(self, i):
        for k in range(4, 12):
            self.ada_piece(i, k)
        self.ada_finish_parts(i, [2, 3, 4, 5])

    def ada_all(self, i):
        for k in range(12):
            self.ada_piece(i, k)
        self.ada_finish(i)

    def rstd_block(self, hfn, hbufs, n, dst=None, dstB=None):
        p = self.p
        _, ss, ssB = self.bank("ss", [0, 1])
        for ch in range(8):
            r = ch % 2
            sq = self.scr_bf(r, n)
            hb = hbufs[ch] if isinstance(hbufs[ch], list) else [hbufs[ch]]
            p.op("act", _act(sq, hfn(ch), AF.Square), reads=hb, writes=[self.SCRB[r]])
            p.op("pe", _mm(ss[:, :n], self.ONES[:], sq, ch == 0, ch == 7), reads=[self.SCRB[r], self.CONSTB],
                 writes=[ssB])
        if dst is None:
            _, rs, rsB = self.bank("rs", [2, 3])
            dst, dstB = rs[:, :n], [rsB]
        p.op("act", _act(dst, ss[:, :n], AF.Sqrt, scale=1.0 / 1024, bias=self.EPSC[:, 0:1]),
             reads=[ssB, self.CONSTB], writes=dstB)
        p.op("dve", lambda e, dst=dst: e.reciprocal(out=dst, in_=dst), reads=dstB, writes=dstB)
        return dst, dstB

    def norm_mod(self, blks, i, which):
        p = self.p
        part_sh = 0 if which == 0 else 3
        for b in blks:
            rs, rsB = self.rstd_block(b.h, b.hB, b.n)
            for ch in range(8):
                _, tmp, tmpB = self.bank("tmp", [4, 5])
                p.op("dve", _tt(tmp[:, :b.n], b.h(ch), rs, ALU.mult), reads=[b.hB[ch]] + rsB, writes=[tmpB])
                p.op("act", _act(b.a(ch), tmp[:, :b.n], AF.Identity, scale=self.dercol(i, which, ch, b.s),
                                 bias=self.modcol(i, part_sh, ch, b.s)),
                     reads=[tmpB, self.DERB[i][which], self.MODB[i][part_sh]], writes=[b.aB[ch]])

    def mlp(self, blks, i, ada_next=None):
        p = self.p
        self.norm_mod(blks, i, 1)
        ada_k = 0
        for fb in range(8):
            li = self.mlp_l.index(i)
            v1 = self.w1[li, :, :].rearrange("(kc q) f -> q kc f", q=128)[:, :, fb * 512:(fb + 1) * 512]
            w1p, w1b = self.ring_load(v1, 8, 512)
            v2 = self.w2[li, fb * 512:(fb + 1) * 512, :].rearrange("(fc q) d -> q fc d", q=128)
            w2p, w2b = self.ring_load(v2, 4, 1024)
            for b in blks:
                for fc in range(4):
                    _, hp, hpB = self.bank("hp", [0, 1, 2])
                    for kc in range(8):
                        p.op("pe", _mm(hp[:, :b.n], w1p[:, kc, fc * 128:(fc + 1) * 128], b.a(kc), kc == 0, kc == 7),
                             reads=w1b + [b.aB[kc]], writes=[hpB])
                    r2 = self.rot.get("relu", 0) % 2
                    self.rot["relu"] = r2 + 1
                    rr = self.scr_f32(r2, b.n)
                    rrB = self.SCRB[2 * r2:2 * r2 + 2]
                    p.op("act", _act(rr, hp[:, :b.n], AF.Relu), reads=[hpB], writes=rrB)
                    p.op("dve", _tt(b.hid(fc), hp[:, :b.n], rr, ALU.mult), reads=[hpB] + rrB, writes=[b.hidB[fc]])
            for b in blks:
                for dc in range(8):
                    _, yp, ypB = self.bank("yp", [3, 4, 5, 6])
                    for fc in range(4):
                        p.op("pe", _mm(yp[:, :b.n], w2p[:, fc, dc * 128:(dc + 1) * 128], b.hid(fc), fc == 0, fc == 3),
                             reads=w2b + [b.hidB[fc]], writes=[ypB])
                    p.op("dve", _stt(b.h(dc), yp[:, :b.n], self.modcol(i, 5, dc, b.s), b.h(dc), ALU.mult, ALU.add),
                         reads=[ypB, self.MODB[i][5], b.hB[dc]], writes=[b.hB[dc]])
            if ada_next is not None:
                for _ in range(2):
                    if ada_k < 12:
                        self.ada_piece(ada_next, ada_k)
                        ada_k += 1
        if ada_next is not None:
            self.ada_finish(ada_next)

    def pool_mix(self, i, j, segs, mid_hook=None):
        p = self.p
        pw = self.inp_once("pool_w", [2, 4, 256, 256])
        for sg in segs:
            n, s = sg["n"], sg["s"]
            ncol = n + 2 * HAL
            A = self.actb_f32(0, ncol)
            AB_ = self.actb_bufs(0, ncol)
            SA = self.actb_f32(2560, ncol)
            SAB = self.actb_bufs(2560, ncol)
            RS = self.actb_f32(5120, ncol)
            RSB = self.actb_bufs(5120, ncol)
            c0 = 0
            for (o, m) in sg["stat_cols"]:
                self.rstd_block(lambda ch, o=o, m=m: sg["hcols"](ch, o, m), sg["hbufs_all"], m,
                                dst=RS[:, sg["rs_off"] + o:sg["rs_off"] + o + m], dstB=RSB)
            for ch in range(8):
                g = ch // 2
                w = 2 << g
                hb = sg["hbufs_ch"](ch)
                if sg["halo"]:
                    p.op("dve", _tt(A[:, 0:ncol], sg["hcols"](ch, 0, ncol), RS[:, 0:ncol], ALU.mult),
                         reads=hb + RSB, writes=AB_)
                    p.op("act", _act(A[:, 0:ncol], A[:, 0:ncol], AF.Identity, scale=self.dercol(i, 0, ch, s),
                                     bias=self.modcol(i, 0, ch, s)), reads=AB_ + [self.DERB[i][0], self.MODB[i][0]],
                         writes=AB_)
                    p.op("dve", _tt(A[:, 0:HAL], A[:, 0:HAL], self.HM[:, 0:HAL], ALU.mult), reads=AB_ + [self.CONSTB],
                         writes=AB_)
                    p.op("dve", _tt(A[:, HAL + n:ncol], A[:, HAL + n:ncol], self.HM[:, HAL:2 * HAL], ALU.mult),
                         reads=AB_ + [self.CONSTB], writes=AB_)
                else:
                    p.op("dve", lambda e, A=A, ncol=ncol: e.memset(A[:, 0:ncol], 0.0), writes=AB_)
                    p.op("dve", _tt(A[:, HAL:HAL + n], sg["hcols"](ch, 0, n), RS[:, HAL:HAL + n], ALU.mult),
                         reads=hb + RSB, writes=AB_)
                    p.op("act", _act(A[:, HAL:HAL + n], A[:, HAL:HAL + n], AF.Identity,
                                     scale=self.dercol(i, 0, ch, s), bias=self.modcol(i, 0, ch, s)),
                         reads=AB_ + [self.DERB[i][0], self.MODB[i][0]], writes=AB_)
                m = ncol - 1
                p.op("dve", _tt(SA[:, 0:m], A[:, 0:m], A[:, 1:m + 1], ALU.add), reads=AB_, writes=SAB)
                sh = 2
                while sh < w:
                    m2 = m - sh
                    p.op("dve", _tt(SA[:, 0:m2], SA[:, 0:m2], SA[:, sh:sh + m2], ALU.add), reads=SAB, writes=SAB)
                    m = m2
                    sh *= 2
                o = HAL - w // 2
                for b in sg["blks"]:
                    t0 = b.tb * 512
                    p.op("dve", _stt(b.a(ch), SA[:, o + t0:o + t0 + b.n], 1.0 / w, A[:, HAL + t0:HAL + t0 + b.n],
                                     ALU.mult, ALU.subtract), reads=SAB + AB_, writes=[b.aB[ch]])
                eo = sg["edge_off"] + g * 16
                tmpe = self.SMALL[:, 0:16]
                for (dst_t, ecol) in ((0, 0), (n - HAL, HAL)):
                    bfix = sg["blks"][0] if dst_t == 0 else sg["blks"][-1]
                    lt = dst_t - bfix.tb * 512
                    p.op("dve", _tt(tmpe[:, ecol:ecol + HAL], SA[:, o + dst_t:o + dst_t + HAL],
                                    self.EDGE[:, eo + ecol:eo + ecol + HAL], ALU.mult), reads=SAB + [self.CONSTB],
                         writes=[self.SMALLB])
                    p.op("dve", _tt(bfix.a(ch)[:, lt:lt + HAL], tmpe[:, ecol:ecol + HAL],
                                    A[:, HAL + dst_t:HAL + dst_t + HAL], ALU.subtract),
                         reads=[self.SMALLB] + AB_, writes=[bfix.aB[ch]])
        if mid_hook is not None:
            mid_hook()
        v = pw[j, :, :, :].rearrange("g (kc q) d -> q (g kc) d", q=128)
        wp, wpb = self.ring_load(v, 8, 256)
        for sg in segs:
            for b in sg["blks"]:
                for dc in range(8):
                    g, dh = dc // 2, dc % 2
                    _, yp, ypB = self.bank("yp", [3, 4, 5, 6])
                    for kc in range(2):
                        p.op("pe", _mm(yp[:, :b.n], wp[:, g * 2 + kc, dh * 128:(dh + 1) * 128], b.a(g * 2 + kc),
                                       kc == 0, kc == 1), reads=wpb + [b.aB[g * 2 + kc]], writes=[ypB])
                    p.op("dve", _stt(b.h(dc), yp[:, :b.n], self.dercol(i, 2, dc, b.s), b.h(dc), ALU.mult, ALU.add),
                         reads=[ypB, self.DERB[i][2], b.hB[dc]], writes=[b.hB[dc]])

    def inp_once(self, name, shape, dt=F32):
        if name in self.din:
            return self.din[name]
        return self.inp(name, shape, dt)

    def lat_seg(self, blks):
        return dict(n=T, s=0, halo=True, blks=blks, edge_off=0, rs_off=0,
                    stat_cols=[(0, 512), (512, 512), (1024, 512), (1536, 512), (2048, 16)],
                    hcols=lambda ch, o, m: self.HT3[:, ch, o:o + m],
                    hbufs_all=[self.HTB[ch] + [self.HALB[ch]] for ch in range(8)],
                    hbufs_ch=lambda ch: self.HTB[ch] + [self.HALB[ch]])

    def ctx_seg(self, cb):
        return dict(n=NCTX, s=1, halo=False, blks=[cb], edge_off=64, rs_off=HAL,
                    stat_cols=[(0, 256)],
                    hcols=lambda ch, o, m: self.CH3[:, ch, o:o + m],
                    hbufs_all=[[self.CTXHB[ch]] for ch in range(8)],
                    hbufs_ch=lambda ch: [self.CTXHB[ch]])

    def ring_load_f32(self, dram_ap, ncols, eng="sp"):
        elems = 2 * ncols
        n = (elems + SLOT - 1) // SLOT
        if self.ring_pos + n > RING_SLOTS:
            self.ring_pos = 0
        pos = self.ring_pos
        self.ring_pos = (pos + n) % RING_SLOTS
        view = self.BIG[:, pos * SLOT:pos * SLOT + elems].bitcast(F32)
        bufs = self.SLOTB[pos:pos + n]
        self.p.dma(eng, view, dram_ap, writes=bufs)
        return view, bufs

    def qk_prep(self, ps, psB, n, gcol, rope, t0, out_ap, outB):
        p = self.p
        k = self.rot.get("qkset", 0)
        self.rot["qkset"] = k + 1
        base = (k % 2) * 2560
        r = k % 2
        sq = self.scr_bf(r, n)
        p.op("act", _act(sq, ps, AF.Square), reads=[psB], writes=[self.SCRB[r]])
        _, ss, ssB = self.bank("ss2", [2, 3])
        p.op("pe", _mm(ss[:, :n], self.ONES[:], sq, True, True), reads=[self.SCRB[r], self.CONSTB], writes=[ssB])
        rs = self.actb_f32(base, n)
        rsB = self.actb_bufs(base, n)
        p.op("act", _act(rs, ss[:, :n], AF.Sqrt, scale=1.0 / 128, bias=self.EPSC[:, 0:1]),
             reads=[ssB, self.CONSTB], writes=rsB)
        p.op("dve", lambda e: e.reciprocal(out=rs, in_=rs), reads=rsB, writes=rsB)
        xg = self.actb_f32(base + 512, n)
        xgB = self.actb_bufs(base + 512, n)
        p.op("dve", _stt(xg, ps, gcol, rs, ALU.mult, ALU.mult), reads=[psB, self.CONSTB] + rsB, writes=xgB)
        if rope:
            _, rp, rpB = self.bank("rot", [4, 5])
            p.op("pe", _mm(rp[:, :n], self.ROT[:], xg, True, True), reads=xgB + [self.CONSTB], writes=[rpB])
            t1 = self.actb_f32(base + 1024, n)
            t1B = self.actb_bufs(base + 1024, n)
            t2 = self.actb_f32(base + 1536, n)
            t2B = self.actb_bufs(base + 1536, n)
            p.op("dve", _tt(t1, xg, self.ropeC[:, t0:t0 + n], ALU.mult), reads=xgB + self.ropeCB, writes=t1B)
            p.op("dve", _tt(t2, rp[:, :n], self.ropeS[:, t0:t0 + n], ALU.mult), reads=[rpB] + self.ropeSB, writes=t2B)
            p.op("dve", _tt(out_ap, t1, t2, ALU.add), reads=t1B + t2B, writes=outB)
        else:
            p.op("act", _act(out_ap, xg, AF.Copy), reads=xgB, writes=outB)

    def stage(self, n):
        k = self.rot.get("stage", 0)
        self.rot["stage"] = k + 1
        base = (k % 2) * 2560 + 2048
        v = self.ACTB[:, base:base + 256].bitcast(BF16)[:, 0:n]
        return v, self.actb_bufs(base, 256)

    def attn_pre(self, blks, cb):
        p = self.p
        allb = blks + [cb]
        self.norm_mod(allb, 1, 0)
        wqkv = self.inp("attn_w_qkv", [1, 1024, 2048])
        wv_ = wqkv[0, :, :].rearrange("(kc q) f -> q kc f", q=128)
        self.ropeC, self.ropeCB = self.ring_load_f32(self.inp("ropec", [128, T]), T)
        self.ropeS, self.ropeSB = self.ring_load_f32(self.inp("ropes", [128, T]), T)
        kT_out = self.outp("kT_out", [4, 128, T + NCTX], BF16)
        v_out = self.outp("v_out", [18, 128, 516], BF16)
        qT_out = self.outp("qT_out", [8, 128, T], BF16)
        wk, wkb = self.ring_load(wv_[:, :, 1024:1536], 8, 512)
        for g in range(4):
            for b in allb:
                _, ps, psB = self.bank("qk", [0, 1])
                for kc in range(8):
                    p.op("pe", _mm(ps[:, :b.n], wk[:, kc, g * 128:(g + 1) * 128], b.a(kc), kc == 0, kc == 7),
                         reads=wkb + [b.aB[kc]], writes=[psB])
                st, stB = self.stage(b.n)
                self.qk_prep(ps[:, :b.n], psB, b.n, self.QKG[:, 1:2], b.s == 0, b.tb * 512, st, stB)
                c0 = b.tb * 512 if b.s == 0 else T
                self.out_dma(kT_out[g, :, c0:c0 + b.n], st, stB)
        wvp, wvb = self.ring_load(wv_[:, :, 1536:2048], 8, 512)
        vst = []
        for k in range(2):
            off = 5120 + k * 512
            v3 = self.ACTB[:, off:off + 258].bitcast(BF16).rearrange("p (g d) -> p g d", g=4)
            vb = self.actb_bufs(off, 258)
            p.op("dve", lambda e, v3=v3: e.memset(v3[:, :, 128:129], 1.0), writes=vb)
            vst.append((v3, vb, self.ACTB[:, off:off + 258].bitcast(BF16)))
        for tile in range(18):
            if tile < 16:
                b = blks[tile // 4]
                lo = (tile % 4) * 128
            else:
                b = cb
                lo = (tile - 16) * 128
            _, ps, psB = self.bank("v", [6])
            for kc in range(8):
                p.op("pe", _mm(ps[:, :512], b.a(kc)[:, lo:lo + 128], wvp[:, kc, :], kc == 0, kc == 7),
                     reads=wvb + [b.aB[kc]], writes=[psB])
            v3, vb, vflat = vst[tile % 2]
            p.op("act", _act(v3[:, :, 0:128], ps[:, :512].rearrange("p (g d) -> p g d", g=4), AF.Copy),
                 reads=[psB], writes=vb)
            self.out_dma(v_out[tile, :, :], vflat, vb)
        for half in range(2):
            wq, wqb = self.ring_load(wv_[:, :, half * 512:(half + 1) * 512], 8, 512)
            for hl in range(4):
                h = half * 4 + hl
                for b in blks:
                    _, ps, psB = self.bank("qk", [0, 1])
                    for kc in range(8):
                        p.op("pe", _mm(ps[:, :b.n], wq[:, kc, hl * 128:(hl + 1) * 128], b.a(kc), kc == 0, kc == 7),
                             reads=wqb + [b.aB[kc]], writes=[psB])
                    st, stB = self.stage(b.n)
                    self.qk_prep(ps[:, :b.n], psB, b.n, self.QKG[:, 0:1], True, b.tb * 512, st, stB)
                    self.out_dma(qT_out[h, :, b.tb * 512:b.tb * 512 + b.n], st, stB)

    def build_A(self):
        self.setup()
        self.load_h()
        blks = self.lat_blocks()
        cb = self.ctx_block()
        cin = self.inp("ctxin", [1024, NCTX])
        cv = cin.rearrange("(c p) t -> p c t", p=128)
        for ch in range(8):
            self.p.dma("sp", self.CH3[:, ch, :], cv[:, ch, :], writes=[self.CTXHB[ch]])
        self.ada_early(0)
        self.pool_mix(0, 0, [self.lat_seg(blks), self.ctx_seg(cb)], mid_hook=lambda: self.ada_late(0))
        self.mlp(blks + [cb], 0, ada_next=1)
        self.attn_pre(blks, cb)
        self.store_h()
        self.p.emit(final_wait_bufs=self.outbufs)
        self.es.close()
        return self.nc

    def acta_buf(self, region):
        return self.AB[region // 4][region % 4]

    def attn_core(self, blks):
        p = self.p
        qT_in = self.inp("qT_in", [8, 128, T], BF16)
        kT_all = self.inp("kT_all", [4, 128, 8448], BF16)
        v_all = self.inp("v_all", [4, 128, 66 * 129], BF16)
        Q3 = self.ACTB[:, :].bitcast(BF16).rearrange("p (h t) -> p h t", h=8)
        for h in range(8):
            p.dma("sp", Q3[:, h, :], qT_in[h, :, :], writes=self.ACTBB[2 * h:2 * h + 2])
        OT3 = self.A3
        TPb = self.PS[:, 7 * 512:7 * 512 + 256].bitcast(BF16)
        ON = self.scr_bf(2, 512)
        pending = []

        def make_evac(head, qb):
            def nrm():
                for qt in range(4):
                    rc = self.SMALL[:, 16 + qt:17 + qt]
                    p.op("dve", lambda e, rc=rc, qt=qt: e.reciprocal(out=rc, in_=self.PS[:, qt * 512 + 128:qt * 512 + 129]),
                         reads=[self.PSB[qt]], writes=[self.SMALLB])
                    p.op("dve", _ts(ON[:, qt * 128:(qt + 1) * 128], self.PS[:, qt * 512:qt * 512 + 128], rc, None, ALU.mult),
                         reads=[self.PSB[qt], self.SMALLB], writes=[self.SCRB[2]])

            def tr():
                for qt in range(4):
                    p.op("pe", lambda e, qt=qt: e.transpose(out=TPb[:, qt * 128:(qt + 1) * 128],
                                                            in_=ON[:, qt * 128:(qt + 1) * 128], identity=self.IDENT[:]),
                         reads=[self.SCRB[2], self.CONSTB], writes=[self.PSB[7]])
                p.op("dve", lambda e: e.tensor_copy(out=OT3[:, head, qb * 512:(qb + 1) * 512], in_=TPb[:, 0:512]),
                     reads=[self.PSB[7]], writes=[self.AB[head][qb]])
            return nrm, tr

        for g in range(4):
            sb0 = (g % 2) * 4
            Kv = self.BIG[:, sb0 * SLOT:sb0 * SLOT + 8448]
            KB = self.SLOTB[sb0:sb0 + 2]
            V3 = self.BIG[:, (sb0 + 2) * SLOT:(sb0 + 2) * SLOT + 66 * 129].rearrange("p (k d) -> p k d", k=66)
            VB = self.SLOTB[sb0 + 2:sb0 + 4]
            p.dma("sp", Kv, kT_all[g, :, :], writes=KB)
            p.dma("sp", self.BIG[:, (sb0 + 2) * SLOT:(sb0 + 2) * SLOT + 66 * 129], v_all[g, :, :], writes=VB)
            jobs = [(qb, h2, kt) for qb in range(4) for h2 in range(2) for kt in range(66)]

            def emit_S(job, jidx):
                qb, h2, kt = job
                head = 2 * g + h2
                _, ps, psB = self.bank("st", [4, 5, 6])
                p.op("pe", _mm(ps[:, :512], Kv[:, kt * 128:(kt + 1) * 128], Q3[:, head, qb * 512:(qb + 1) * 512], True, True),
                     reads=KB + self.ACTBB[2 * head:2 * head + 2], writes=[psB])
                return ps, psB

            PTR = (0, 1, 3)
            sq = [emit_S(jobs[0], 0), emit_S(jobs[1], 1)]
            for j, job in enumerate(jobs):
                qb, h2, kt = job
                head = 2 * g + h2
                if j + 2 < len(jobs):
                    sq.append(emit_S(jobs[j + 2], j + 2))
                ps, psB = sq.pop(0)
                r = PTR[j % 3]
                PT = self.scr_bf(r, 512)
                p.op("act", _act(PT, ps[:, :512], AF.Exp, scale=SM_SCALE, bias=self.EPSC[:, 1:2]),
                     reads=[psB, self.CONSTB], writes=[self.SCRB[r]])
                for qt in range(4):
                    p.op("pe", _mm(self.PS[:, qt * 512:qt * 512 + 129], PT[:, qt * 128:(qt + 1) * 128], V3[:, kt, :],
                                   kt == 0, kt == 65), reads=[self.SCRB[r]] + VB, writes=[self.PSB[qt]])
                if kt == 3 and pending:
                    pending.pop(0)()
                if kt == 65:
                    nrm, tr = make_evac(head, qb)
                    nrm()
                    pending.append(tr)
        while pending:
            pending.pop(0)()
        wo = self.inp("attn_w_o", [1, 1024, 1024])
        wov = wo[0, :, :].rearrange("(hc q) d -> q hc d", q=128)
        for half in range(2):
            wp, wpb = self.ring_load(wov[:, :, half * 512:(half + 1) * 512], 8, 512)
            for b in blks:
                for dl in range(4):
                    dc = half * 4 + dl
                    _, yp, ypB = self.bank("yp", [3, 4, 5, 6])
                    for hc in range(8):
                        p.op("pe", _mm(yp[:, :b.n], wp[:, hc, dl * 128:(dl + 1) * 128], OT3[:, hc, b.tb * 512:(b.tb + 1) * 512],
                                       hc == 0, hc == 7), reads=wpb + [self.AB[hc][b.tb]], writes=[ypB])
                    p.op("dve", _stt(b.h(dc), yp[:, :b.n], self.modcol(1, 2, dc, 0), b.h(dc), ALU.mult, ALU.add),
                         reads=[ypB, self.MODB[1][2], b.hB[dc]], writes=[b.hB[dc]])

    def gmlp(self, blks, i):
        p = self.p
        win = self.inp("gm_w_in", [1, 1024, 4096])
        wout = self.inp("gm_w_out", [1, 2048, 1024])
        winv = win[0, :, :].rearrange("(kc q) f -> q kc f", q=128)
        base = 6 * SLOT
        auxB = self.SLOTB[6:8]
        wsT = self.BIG[:, base:base + 1024].rearrange("p (g q) -> p g q", g=8)
        p.dma("pool", self.BIG[:, base:base + 1024], self.inp("gm_wsT", [128, 1024]), writes=auxB)
        BSB = self.BIG[:, base + 1024:base + 3072].bitcast(F32)
        p.dma("sp", BSB, self.inp("gm_bsb", [128, 1024]), writes=auxB)
        LG = self.BIG[:, base + 3072:base + 3136].bitcast(F32)
        p.dma("sp", LG, self.inp("gm_lgb", [128, 32]), writes=auxB)
        R3 = self.ACTB[:, 6144:8192].rearrange("p (c q) -> p c q", c=16)
        RB = self.ACTBB[12:16]
        VT = self.ACTB[:, 4096:6144]
        VTB = self.ACTBB[8:12]
        U3 = self.ACTB[:, 0:4096].bitcast(BF16).rearrange("p (c t) -> p c t", c=16)
        A8 = self.ACTA[:, 0:4096].rearrange("p (c t) -> p c t", c=8)
        G3 = self.ACTA[:, 4096:12288].rearrange("p (c t) -> p c t", c=16)
        VNs = [self.ACTA[:, 12288 + k * 2048:14336 + k * 2048] for k in range(2)]
        VNBs = [[self.acta_buf(24 + 4 * k + q) for q in range(4)] for k in range(2)]
        for g in range(8):
            _, ps, psB = self.bank("hp", [0, 1, 2])
            p.op("pe", _mm(ps[:, :128], self.ONES[:], wsT[:, g, :], True, True), reads=auxB + [self.CONSTB], writes=[psB])
            for cc in (2 * g, 2 * g + 1):
                p.op("dve", _stt(R3[:, cc, :], ps[:, :128], LG[:, 16 + cc:17 + cc], BSB[:, g * 128:(g + 1) * 128],
                                 ALU.mult, ALU.add), reads=[psB] + auxB, writes=RB)
        for b in blks:
            rs, rsB = self.rstd_block(b.h, b.hB, b.n)
            for kc in range(8):
                _, tmp, tmpB = self.bank("tmp", [4, 5])
                p.op("dve", _tt(tmp[:, :b.n], b.h(kc), rs, ALU.mult), reads=[b.hB[kc]] + rsB, writes=[tmpB])
                p.op("act", _act(A8[:, kc, :], tmp[:, :b.n], AF.Identity, scale=self.dercol(i, 0, kc, 0),
                                 bias=self.modcol(i, 0, kc, 0)), reads=[tmpB, self.DERB[i][0], self.MODB[i][0]],
                     writes=[self.acta_buf(kc)])
            a8B = [self.acta_buf(kc) for kc in range(8)]
            for cg in range(4):
                wp, wpb = self.ring_load(winv[:, :, cg * 512:(cg + 1) * 512], 8, 512)
                for cl in range(4):
                    c = cg * 4 + cl
                    _, ps, psB = self.bank("hp", [0, 1, 2])
                    for kc in range(8):
                        p.op("pe", _mm(ps[:, :512], wp[:, kc, cl * 128:(cl + 1) * 128], A8[:, kc, :], kc == 0, kc == 7),
                             reads=wpb + [a8B[kc]], writes=[psB])
                    p.op("act", _act(U3[:, c, :], ps[:, :512], AF.Gelu_apprx_tanh), reads=[psB], writes=[self.ACTBB[c // 2]])
            pv = [self.ring_load(winv[:, :, 2048 + cb * 512:2048 + (cb + 1) * 512], 8, 512) for cb in range(4)]
            def v_mm(tt):
                for cb in range(4):
                    bk = 3 + cb
                    ps, psB = self.PS[:, bk * 512:(bk + 1) * 512], self.PSB[bk]
                    for kc in range(8):
                        p.op("pe", _mm(ps[:, :512], A8[:, kc, tt * 128:(tt + 1) * 128], pv[cb][0][:, kc, :], kc == 0, kc == 7),
                             reads=pv[cb][1] + [a8B[kc]], writes=[psB])
                    p.op("act", _act(VT[:, cb * 512:(cb + 1) * 512], ps[:, :512], AF.Gelu_apprx_tanh,
                                     accum_out=self.SMALL[:, 32 + cb:33 + cb]), reads=[psB], writes=[VTB[cb], self.SMALLB])
                    p.op("act", _act(self.scr_bf(3, 512), VT[:, cb * 512:(cb + 1) * 512], AF.Square,
                                     accum_out=self.SMALL[:, 36 + cb:37 + cb]), reads=[VTB[cb]],
                         writes=[self.SCRB[3], self.SMALLB])

            def v_ln(tt):
                VN, VNB = VNs[tt % 2], VNBs[tt % 2]
                S = self.SMALL
                sB = [self.SMALLB]
                p.op("dve", lambda e: e.tensor_reduce(out=S[:, 40:41], in_=S[:, 32:36], axis=AX.X, op=ALU.add), reads=sB, writes=sB)
                p.op("dve", lambda e: e.tensor_reduce(out=S[:, 41:42], in_=S[:, 36:40], axis=AX.X, op=ALU.add), reads=sB, writes=sB)
                p.op("dve", _ts(S[:, 42:43], S[:, 40:41], 1.0 / 2048, None, ALU.mult), reads=sB, writes=sB)
                p.op("dve", _ts(S[:, 43:44], S[:, 41:42], 1.0 / 2048, None, ALU.mult), reads=sB, writes=sB)
                p.op("dve", _tt(S[:, 44:45], S[:, 42:43], S[:, 42:43], ALU.mult), reads=sB, writes=sB)
                p.op("dve", _tt(S[:, 45:46], S[:, 43:44], S[:, 44:45], ALU.subtract), reads=sB, writes=sB)
                p.op("act", _act(S[:, 46:47], S[:, 45:46], AF.Sqrt, bias=self.EPSC[:, 0:1]), reads=sB + [self.CONSTB], writes=sB)
                p.op("dve", lambda e: e.reciprocal(out=S[:, 47:48], in_=S[:, 46:47]), reads=sB, writes=sB)
                p.op("dve", _ts(VN, VT, S[:, 42:43], S[:, 47:48], ALU.subtract, ALU.mult), reads=VTB + sB, writes=VNB)

            def spatial(tt):
                VN, VNB = VNs[tt % 2], VNBs[tt % 2]
                for c in range(16):
                    mb = 7 if (c // 4) % 2 == 0 else 0
                    M = self.PS[:, mb * 512 + (c % 4) * 128:mb * 512 + (c % 4) * 128 + 128]
                    p.op("pe", _mm(M, VN[:, c * 128:(c + 1) * 128], wsT[:, c // 2, :], True, True),
                         reads=VNB + auxB, writes=[self.PSB[mb]])
                    r2 = c % 2
                    SV = self.scr_f32(r2, 128)
                    svB = self.SCRB[2 * r2:2 * r2 + 1]
                    p.op("dve", _stt(SV, M, LG[:, c:c + 1], R3[:, c, :], ALU.mult, ALU.add),
                         reads=[self.PSB[mb]] + auxB + RB, writes=svB)
                    p.op("dve", _tt(G3[:, c, tt * 128:(tt + 1) * 128], SV, U3[:, c, tt * 128:(tt + 1) * 128], ALU.mult),
                         reads=svB + [self.ACTBB[c // 2]], writes=[self.acta_buf(8 + c)])

            for tt in range(4):
                v_mm(tt)
                if tt > 0:
                    spatial(tt - 1)
                v_ln(tt)
            spatial(3)
            for half in range(2):
                po = [self.ring_load(wout[0, (half * 2 + k) * 512:(half * 2 + k + 1) * 512, :].rearrange("(fc q) d -> q fc d", q=128), 4, 1024)
                      for k in range(2)]
                for dc in range(8):
                    _, yp, ypB = self.bank("yp", [3, 4, 5, 6])
                    for k8 in range(8):
                        c = half * 8 + k8
                        w, wb = po[k8 // 4]
                        p.op("pe", _mm(yp[:, :512], w[:, k8 % 4, dc * 128:(dc + 1) * 128], G3[:, c, :], k8 == 0, k8 == 7),
                             reads=wb + [self.acta_buf(8 + c)], writes=[ypB])
                    p.op("dve", _stt(b.h(dc), yp[:, :512], self.modcol(i, 2, dc, 0), b.h(dc), ALU.mult, ALU.add),
                         reads=[ypB, self.MODB[i][2], b.hB[dc]], writes=[b.hB[dc]])

    def final_norm(self, blks):
        p = self.p
        outT = self.outp("outT", [1024, T])
        ov = outT.rearrange("(c q) t -> q c t", q=128)
        k = 0
        for b in blks:
            rs, rsB = self.rstd_block(b.h, b.hB, b.n)
            for ch in range(8):
                st = self.ACTB[:, (k % 16) * 512:(k % 16) * 512 + 512]
                stB = [self.ACTBB[k % 16]]
                k += 1
                p.op("dve", _stt(st, b.h(ch), self.FG[:, ch:ch + 1], rs, ALU.mult, ALU.mult),
                     reads=[b.hB[ch], self.CONSTB] + rsB, writes=stB)
                self.out_dma(ov[:, ch, b.tb * 512:(b.tb + 1) * 512], st, stB)

    def build_B(self):
        self.setup()
        self.load_h()
        blks = self.lat_blocks()
        self.ada_all(1)
        self.attn_core(blks)
        self.mlp(blks, 1, ada_next=2)
        self.gmlp(blks, 2)
        self.mlp(blks, 2)
        self.store_h()
        self.p.emit(final_wait_bufs=self.outbufs)
        self.es.close()
        return self.nc

    def build_C(self):
        self.setup()
        self.load_h()
        blks = self.lat_blocks()
        self.ada_early(3)
        self.pool_mix(3, 1, [self.lat_seg(blks)], mid_hook=lambda: self.ada_late(3))
        self.mlp(blks, 3)
        self.final_norm(blks)
        self.p.emit(final_wait_bufs=self.outbufs)
        self.es.close()
        return self.nc


SEQ = 8192
_PROGS = {}


def _fm(v):
    v = np.asarray(v, np.float32)
    lead = v.shape[:-1]
    return np.moveaxis(v.reshape(*lead, 8, 128), -1, 0)


def _consts_for_core(c, inputs):
    b, r = c // 4, c % 4
    q0 = r * T
    f32 = np.float32
    d = {}
    cv = np.stack([_fm(inputs["c"][b]), _fm(inputs["c_ctx"])], axis=-1)
    d["cvec"] = np.ascontiguousarray(cv.reshape(128, 16), f32)
    ab = _fm(np.asarray(inputs["ada_b"]).reshape(4, 6, 1024))
    d["adab"] = np.ascontiguousarray(np.repeat(ab[..., None], 2, -1).reshape(128, 4 * 96), f32)
    ng = _fm(np.asarray(inputs["norm_g"]))
    d["ng2"] = np.ascontiguousarray(np.repeat(ng[..., None], 2, -1).reshape(128, 128), f32)
    ps = _fm(np.asarray(inputs["pool_scale"]))
    d["ps2"] = np.ascontiguousarray(np.repeat(ps[..., None], 2, -1).reshape(128, 32), f32)
    d["fg"] = np.ascontiguousarray(_fm(np.asarray(inputs["final_g"])).reshape(128, 8), f32)
    d["qkg"] = np.ascontiguousarray(np.stack([np.asarray(inputs["attn_q_g"])[0], np.asarray(inputs["attn_k_g"])[0]], 1), f32)
    d["ident"] = np.eye(128, dtype=f32)
    rot = np.zeros((128, 128), f32)
    for base in (0, 64):
        for m in range(32):
            rot[base + m + 32, base + m] = -1.0
            rot[base + m, base + m + 32] = 1.0
    d["rot"] = rot
    hm = np.zeros((128, 32), f32)
    hm[:, 0:8] = 1.0 if q0 > 0 else 0.0
    hm[:, 8:16] = 1.0 if q0 + T < SEQ else 0.0
    d["hm"] = hm
    edge = np.zeros((128, 128), f32)
    for g, w in enumerate((2, 4, 8, 16)):
        for e in range(16):
            pos = q0 + (e if e < 8 else T - 16 + e)
            lo, hi = max(pos - w // 2, 0), min(pos + w - w // 2, SEQ)
            edge[:, g * 16 + e] = 1.0 / (hi - lo)
            posc = e if e < 8 else NCTX - 16 + e
            lo, hi = max(posc - w // 2, 0), min(posc + w - w // 2, NCTX)
            edge[:, 64 + g * 16 + e] = 1.0 / (hi - lo)
    d["edge"] = edge
    return d


def _rope_tables(q0):
    t = np.arange(q0, q0 + T)
    row = (t // 64).astype(np.float32)
    col = (t % 64).astype(np.float32)
    inv = (np.float32(10000.0) ** (-np.arange(0, 64, 2, dtype=np.float32) / np.float32(64))).astype(np.float32)
    ang_r = (row[:, None] * inv[None, :]).astype(np.float32)
    ang_c = (col[:, None] * inv[None, :]).astype(np.float32)
    C = np.concatenate([np.cos(ang_r), np.cos(ang_r), np.cos(ang_c), np.cos(ang_c)], 1).T
    S = np.concatenate([np.sin(ang_r), np.sin(ang_r), np.sin(ang_c), np.sin(ang_c)], 1).T
    return np.ascontiguousarray(C, np.float32), np.ascontiguousarray(S, np.float32)


def _hin_from_rows(rows_b, r):
    q0 = r * T
    out = np.zeros((TC, 1024), np.float32)
    lo, hi = max(q0 - HAL, 0), min(q0 + T + HAL, SEQ)
    out[lo - (q0 - HAL):hi - (q0 - HAL)] = rows_b[lo:hi]
    return np.ascontiguousarray(out.T)


def _layer_weights(inputs, seg):
    al, ml = list(MK.ADA_LAYERS[seg]), list(MK.MLP_LAYERS[seg])
    return {"ada_w": np.ascontiguousarray(np.asarray(inputs["ada_w"], np.float32)[al]),
            "mlp_w1": np.ascontiguousarray(np.asarray(inputs["mlp_w1"], np.float32)[ml]),
            "mlp_w2": np.ascontiguousarray(np.asarray(inputs["mlp_w2"], np.float32)[ml])}


def _weights(inputs, names):
    return {k: np.asarray(inputs[k], np.float32) for k in names}


def _get_prog(seg):
    if seg not in _PROGS:
        mk = MK(seg)
        nc = getattr(mk, "build_" + seg)()
        _PROGS[seg] = (nc, list(mk.din.keys()))
    return _PROGS[seg]


def _launch(seg, per_core):
    nc, names = _get_prog(seg)
    in_maps = [{k: m[k] for k in names} for m in per_core]
    res = run_bass_kernel_spmd(nc, in_maps, core_ids=list(range(8)))
    return res.results


def run_seg_A(inputs):
    x = np.asarray(inputs["x"], np.float32)
    ctx = np.asarray(inputs["ctx"], np.float32)
    w = _weights(inputs, ("pool_w", "attn_w_qkv"))
    w.update(_layer_weights(inputs, "A"))
    per_core = []
    for c in range(8):
        b, r = c // 4, c % 4
        m = _consts_for_core(c, inputs)
        m["hin"] = _hin_from_rows(x[b], r)
        m["ctxin"] = np.ascontiguousarray(ctx[b].T)
        m["ropec"], m["ropes"] = _rope_tables(r * T)
        m.update(w)
        per_core.append(m)
    return _launch("A", per_core)


def run_seg_B(inputs, resA):
    import ml_dtypes
    bf = ml_dtypes.bfloat16
    w = _weights(inputs, ("attn_w_o", "gm_w_in", "gm_w_out"))
    w.update(_layer_weights(inputs, "B"))
    ws = np.asarray(inputs["gm_ws"], np.float32)[0]
    wsT = np.ascontiguousarray(ws.transpose(2, 0, 1).reshape(128, 1024))
    bs = np.asarray(inputs["gm_bs"], np.float32)[0]
    bsb = np.ascontiguousarray(np.broadcast_to(bs.reshape(1, 1024), (128, 1024)), np.float32)
    lgb = np.concatenate([_fm(np.asarray(inputs["gm_ln_g"])[0].reshape(2, 1024)).reshape(128, 16),
                          _fm(np.asarray(inputs["gm_ln_b"])[0].reshape(2, 1024)).reshape(128, 16)], 1)
    lgb = np.ascontiguousarray(lgb, np.float32)
    per_core = []
    kv_cache = {}
    for b in range(2):
        kT = np.zeros((4, 128, 8448), bf)
        va = np.zeros((4, 128, 66, 129), bf)
        for r in range(4):
            ra = resA[b * 4 + r]
            k = np.asarray(ra["kT_out"])
            v = np.asarray(ra["v_out"]).reshape(18, 128, 4, 129)
            kT[:, :, r * T:(r + 1) * T] = k[:, :, :T]
            va[:, :, r * 16:(r + 1) * 16, :] = v[:16].transpose(2, 1, 0, 3)
            if r == 0:
                kT[:, :, 4 * T:] = k[:, :, T:]
                va[:, :, 64:66, :] = v[16:].transpose(2, 1, 0, 3)
        kv_cache[b] = (kT, np.ascontiguousarray(va.reshape(4, 128, 66 * 129)))
    for c in range(8):
        b, r = c // 4, c % 4
        m = _consts_for_core(c, inputs)
        hin = np.zeros((1024, TC), np.float32)
        hin[:, HAL:HAL + T] = np.asarray(resA[c]["hout"])
        m["hin"] = hin
        m["qT_in"] = np.asarray(resA[c]["qT_out"])
        m["kT_all"], m["v_all"] = kv_cache[b]
        m["gm_wsT"], m["gm_bsb"], m["gm_lgb"] = wsT, bsb, lgb
        m.update(w)
        per_core.append(m)
    return _launch("B", per_core)


def run_seg_C(inputs, resB):
    w = _weights(inputs, ("pool_w",))
    w.update(_layer_weights(inputs, "C"))
    per_core = []
    for b in range(2):
        rows = np.concatenate([np.asarray(resB[b * 4 + r]["hout"]).T for r in range(4)], 0)
        for r in range(4):
            m = _consts_for_core(b * 4 + r, inputs)
            m["hin"] = _hin_from_rows(rows, r)
            m.update(w)
            per_core.append(m)
    return _launch("C", per_core)


def kernel(**inputs):
    resA = run_seg_A(inputs)
    resB = run_seg_B(inputs, resA)
    resC = run_seg_C(inputs, resB)
    out = np.zeros((2, SEQ, 1024), np.float32)
    for c in range(8):
        b, r = c // 4, c % 4
        out[b, r * T:(r + 1) * T, :] = np.asarray(resC[c]["outT"]).T
    return out
```
